# Optimizing a Trainium2 kernel written in Bass

```python
import math
import jax, jax.numpy as jnp
from jax import lax
import numpy as np

D_MODEL = 1024
BATCH = 4
SEQ = 4096
DEPTH = 2
DEC_BATCH = 32
DEC_SEQ = 1
PAST_LEN = 8192
PAGE_SIZE = 128

HEAD_DIM = 64
SELF_WIDTH = 3 * D_MODEL // 4
N_HEADS_FOX = SELF_WIDTH // HEAD_DIM
N_HEADS_DIFF = SELF_WIDTH // (2 * HEAD_DIM)
N_MEM = 256
N_HEADS_MEM = 4
MEM_WIDTH = D_MODEL - SELF_WIDTH
MEM_HEAD_DIM = MEM_WIDTH // N_HEADS_MEM
D_FF = -(-8 * D_MODEL // (3 * 256)) * 256
ROPE_THETA = 10000.0
Q_BLOCK = 128
NORM_EPS = 1e-6
NEG_INF = -1e30
N_FOX = (DEPTH + 1) // 2
N_DIFF = DEPTH // 2
FOX_IN = 3 * SELF_WIDTH + N_HEADS_FOX + MEM_WIDTH
DIFF_IN = 3 * SELF_WIDTH + MEM_WIDTH
POOL_NUM, POOL_DEN = 5, 4
FORGET_BIAS_INIT = 2.0

kernel_name = 'fox_diff_memory_hybrid_step'


def rms_norm(x, g):
    xf = x.astype(jnp.float32)
    y = xf * lax.rsqrt(jnp.mean(xf * xf, axis=-1, keepdims=True) + NORM_EPS)
    return (y * g.astype(jnp.float32)).astype(x.dtype)


def rope(x, pos):
    d = x.shape[-1]
    inv = ROPE_THETA ** (-jnp.arange(0, d, 2, dtype=jnp.float32) / d)
    ang = pos.astype(jnp.float32)[:, None] * inv[None, :]
    cos = jnp.cos(ang)[:, None, :]
    sin = jnp.sin(ang)[:, None, :]
    xf = x.astype(jnp.float32)
    x1, x2 = xf[..., : d // 2], xf[..., d // 2:]
    return jnp.concatenate([x1 * cos - x2 * sin, x1 * sin + x2 * cos], axis=-1).astype(x.dtype)


def sweep_query_blocks(block_fn, seq_len):
    starts = jnp.arange(seq_len // Q_BLOCK) * Q_BLOCK
    out = lax.map(block_fn, starts)
    nb, b, qb, h, dv = out.shape
    return jnp.moveaxis(out, 0, 1).reshape(b, nb * qb, h, dv)


def causal_mask(q_pos, k_pos):
    return k_pos[None, :] <= q_pos[:, None]


def masked_softmax(s, mask):
    return jax.nn.softmax(jnp.where(mask, s, NEG_INF), axis=-1)


def apply_weights(p, v):
    return jnp.einsum('bhqk,bkhd->bqhd', p.astype(v.dtype), v)


def gather_pages(cache, j, page_table):
    g = cache[j, page_table]
    return g.reshape(g.shape[0], -1, *g.shape[3:])


def fox_project(xn, w_in, b_f):
    b, t, _ = xn.shape
    w = SELF_WIDTH
    z = xn @ w_in
    q = z[..., :w].reshape(b, t, N_HEADS_FOX, HEAD_DIM)
    k = z[..., w:2 * w].reshape(b, t, N_HEADS_FOX, HEAD_DIM)
    v = z[..., 2 * w:3 * w].reshape(b, t, N_HEADS_FOX, HEAD_DIM)
    logf = jax.nn.log_sigmoid((z[..., 3 * w:3 * w + N_HEADS_FOX] + b_f).astype(jnp.float32))
    qm = z[..., 3 * w + N_HEADS_FOX:].reshape(b, t, N_HEADS_MEM, MEM_HEAD_DIM)
    return q, k, v, logf, qm


def fox_weights(qb, k, c_q, c_k, mask):
    s = jnp.einsum('bqhd,bkhd->bhqk', qb, k).astype(jnp.float32) * (HEAD_DIM ** -0.5)
    s = s + c_q[..., :, None] - c_k[..., None, :]
    return masked_softmax(s, mask)


def fox_prompt(q, k, v, logf):
    seq_len = q.shape[1]
    c = jnp.swapaxes(jnp.cumsum(logf, axis=1), 1, 2)
    k_pos = jnp.arange(seq_len)

    def block(start):
        qb = lax.dynamic_slice_in_dim(q, start, Q_BLOCK, axis=1)
        c_q = lax.dynamic_slice_in_dim(c, start, Q_BLOCK, axis=2)
        mask = causal_mask(start + jnp.arange(Q_BLOCK), k_pos)
        return apply_weights(fox_weights(qb, k, c_q, c, mask), v)

    return sweep_query_blocks(block, seq_len)


def fox_sample(q, k, v, logf, k_past, v_past, logf_past):
    past_len, t = k_past.shape[1], q.shape[1]
    k_all = jnp.concatenate([k_past, k], axis=1)
    v_all = jnp.concatenate([v_past, v], axis=1)
    lf_all = jnp.concatenate([logf_past.astype(jnp.float32), logf], axis=1)
    c = jnp.swapaxes(jnp.cumsum(lf_all, axis=1), 1, 2)
    mask = causal_mask(past_len + jnp.arange(t), jnp.arange(past_len + t))
    return apply_weights(fox_weights(q, k_all, c[:, :, past_len:], c, mask), v_all)


def diff_project(xn, w_in, pos):
    b, t, _ = xn.shape
    w = SELF_WIDTH
    z = xn @ w_in
    q = rope(z[..., :w].reshape(b, t, 2 * N_HEADS_DIFF, HEAD_DIM), pos)
    k = rope(z[..., w:2 * w].reshape(b, t, 2 * N_HEADS_DIFF, HEAD_DIM), pos)
    q = q.reshape(b, t, N_HEADS_DIFF, 2 * HEAD_DIM)
    k = k.reshape(b, t, N_HEADS_DIFF, 2 * HEAD_DIM)
    v = z[..., 2 * w:3 * w].reshape(b, t, N_HEADS_DIFF, 2 * HEAD_DIM)
    qm = z[..., 3 * w:].reshape(b, t, N_HEADS_MEM, MEM_HEAD_DIM)
    return q, k, v, qm


def diff_lambda(lq1, lk1, lq2, lk2, lam_init):
    f32 = jnp.float32
    return (jnp.exp(jnp.sum(lq1.astype(f32) * lk1.astype(f32)))
            - jnp.exp(jnp.sum(lq2.astype(f32) * lk2.astype(f32))) + lam_init)


def diff_weights(qb, k, lam, mask):
    scale = HEAD_DIM ** -0.5
    s1 = jnp.einsum('bqhd,bkhd->bhqk', qb[..., :HEAD_DIM], k[..., :HEAD_DIM]).astype(jnp.float32) * scale
    s2 = jnp.einsum('bqhd,bkhd->bhqk', qb[..., HEAD_DIM:], k[..., HEAD_DIM:]).astype(jnp.float32) * scale
    return masked_softmax(s1, mask) - lam * masked_softmax(s2, mask)


def diff_prompt(q, k, v, lam):
    seq_len = q.shape[1]
    k_pos = jnp.arange(seq_len)

    def block(start):
        qb = lax.dynamic_slice_in_dim(q, start, Q_BLOCK, axis=1)
        mask = causal_mask(start + jnp.arange(Q_BLOCK), k_pos)
        return apply_weights(diff_weights(qb, k, lam, mask), v)

    return sweep_query_blocks(block, seq_len)


def diff_sample(q, k, v, k_past, v_past, lam):
    past_len, t = k_past.shape[1], q.shape[1]
    k_all = jnp.concatenate([k_past, k], axis=1)
    v_all = jnp.concatenate([v_past, v], axis=1)
    mask = causal_mask(past_len + jnp.arange(t), jnp.arange(past_len + t))
    return apply_weights(diff_weights(q, k_all, lam, mask), v_all)


def diff_head_norm(o, g, lam_init):
    return rms_norm(o, g) * (1.0 - lam_init)


def memory_kv(mem, g, w):
    b, n, _ = mem.shape
    kv = rms_norm(mem, g) @ w
    mk = kv[..., :MEM_WIDTH].reshape(b, n, N_HEADS_MEM, MEM_HEAD_DIM)
    mv = kv[..., MEM_WIDTH:].reshape(b, n, N_HEADS_MEM, MEM_HEAD_DIM)
    return mk, mv


def memory_attend(qm, mk, mv):
    s = jnp.einsum('bthd,bmhd->bhtm', qm, mk).astype(jnp.float32) * (MEM_HEAD_DIM ** -0.5)
    p = jax.nn.softmax(s, axis=-1)
    return jnp.einsum('bhtm,bmhd->bthd', p.astype(mv.dtype), mv)


def finish_layer(h, o_self, qm, mk, mv, g_post_mix, w_out, g_pre_ffn, g_post_ffn, w_gate_up, w_down):
    b, t, _ = h.shape
    o_mem = memory_attend(qm, mk, mv)
    merged = jnp.concatenate([o_self.reshape(b, t, SELF_WIDTH), o_mem.reshape(b, t, MEM_WIDTH)], axis=-1)
    h = h + rms_norm(merged @ w_out, g_post_mix)
    u = rms_norm(h, g_pre_ffn) @ w_gate_up
    f = (jax.nn.silu(u[..., :D_FF]) * u[..., D_FF:]) @ w_down
    return h + rms_norm(f, g_post_ffn)


def setup_inputs(seed: int = 0) -> dict:
    key = jax.random.key(seed)
    ks = jax.random.split(key, 28)
    f32 = jnp.float32

    def nrm(i, shape, scale):
        return jax.random.normal(ks[i], shape, f32) * scale

    n_pages = PAST_LEN // PAGE_SIZE
    n_pool = (DEC_BATCH * n_pages * POOL_NUM) // POOL_DEN
    page_table = jax.random.permutation(ks[9], n_pool)[: DEC_BATCH * n_pages]
    page_table = page_table.reshape(DEC_BATCH, n_pages).astype(jnp.int32)
    return {
        'x_prompt': nrm(0, (BATCH, SEQ, D_MODEL), 1.0),
        'x_sample': nrm(1, (DEC_BATCH, DEC_SEQ, D_MODEL), 1.0),
        'cache_fox_k': nrm(2, (N_FOX, n_pool, PAGE_SIZE, N_HEADS_FOX, HEAD_DIM), 1.0),
        'cache_fox_v': nrm(3, (N_FOX, n_pool, PAGE_SIZE, N_HEADS_FOX, HEAD_DIM), 1.0),
        'cache_fox_logf': jax.nn.log_sigmoid(nrm(4, (N_FOX, n_pool, PAGE_SIZE, N_HEADS_FOX), 1.0) + FORGET_BIAS_INIT),
        'cache_diff_k': nrm(5, (N_DIFF, n_pool, PAGE_SIZE, N_HEADS_DIFF, 2 * HEAD_DIM), 1.0),
        'cache_diff_v': nrm(6, (N_DIFF, n_pool, PAGE_SIZE, N_HEADS_DIFF, 2 * HEAD_DIM), 1.0),
        'cache_mem_k': nrm(7, (DEPTH, DEC_BATCH, N_MEM, N_HEADS_MEM, MEM_HEAD_DIM), 1.0),
        'cache_mem_v': nrm(8, (DEPTH, DEC_BATCH, N_MEM, N_HEADS_MEM, MEM_HEAD_DIM), 1.0),
        'page_table': page_table,
        'mem_prompt': nrm(10, (BATCH, N_MEM, D_MODEL), 1.0),
        'w_in_fox': nrm(11, (N_FOX, D_MODEL, FOX_IN), D_MODEL ** -0.5),
        'b_f_fox': FORGET_BIAS_INIT + nrm(12, (N_FOX, N_HEADS_FOX), 0.1),
        'w_in_diff': nrm(13, (N_DIFF, D_MODEL, DIFF_IN), D_MODEL ** -0.5),
        'lam_q1': nrm(14, (N_DIFF, HEAD_DIM), 0.1),
        'lam_k1': nrm(15, (N_DIFF, HEAD_DIM), 0.1),
        'lam_q2': nrm(16, (N_DIFF, HEAD_DIM), 0.1),
        'lam_k2': nrm(17, (N_DIFF, HEAD_DIM), 0.1),
        'g_subln': 1.0 + nrm(18, (N_DIFF, 2 * HEAD_DIM), 0.02),
        'g_pre_mix': 1.0 + nrm(19, (DEPTH, D_MODEL), 0.02),
        'g_post_mix': 1.0 + nrm(20, (DEPTH, D_MODEL), 0.02),
        'g_pre_ffn': 1.0 + nrm(21, (DEPTH, D_MODEL), 0.02),
        'g_post_ffn': 1.0 + nrm(22, (DEPTH, D_MODEL), 0.02),
        'g_mem': 1.0 + nrm(23, (DEPTH, D_MODEL), 0.02),
        'w_mem_kv': nrm(24, (DEPTH, D_MODEL, 2 * MEM_WIDTH), D_MODEL ** -0.5),
        'w_out': nrm(25, (DEPTH, SELF_WIDTH + MEM_WIDTH, D_MODEL), (SELF_WIDTH + MEM_WIDTH) ** -0.5),
        'w_gate_up': nrm(26, (DEPTH, D_MODEL, 2 * D_FF), D_MODEL ** -0.5),
        'w_down': nrm(27, (DEPTH, D_FF, D_MODEL), D_FF ** -0.5),
    }


def reference(x_prompt, x_sample, cache_fox_k, cache_fox_v, cache_fox_logf, cache_diff_k, cache_diff_v,
              cache_mem_k, cache_mem_v, page_table, mem_prompt, w_in_fox, b_f_fox, w_in_diff,
              lam_q1, lam_k1, lam_q2, lam_k2, g_subln, g_pre_mix, g_post_mix, g_pre_ffn, g_post_ffn,
              g_mem, w_mem_kv, w_out, w_gate_up, w_down):
    t_p, t_s = x_prompt.shape[1], x_sample.shape[1]
    past_len = page_table.shape[1] * PAGE_SIZE
    pos_p = jnp.arange(t_p)
    pos_s = past_len + jnp.arange(t_s)
    h_p, h_s = x_prompt, x_sample
    fk_p, fv_p, fl_p, fk_s, fv_s, fl_s = [], [], [], [], [], []
    dk_p, dv_p, dk_s, dv_s = [], [], [], []
    mk_list, mv_list = [], []

    for i in range(DEPTH):
        j = i // 2
        xn_p = rms_norm(h_p, g_pre_mix[i])
        xn_s = rms_norm(h_s, g_pre_mix[i])
        mk_p, mv_p = memory_kv(mem_prompt, g_mem[i], w_mem_kv[i])
        mk_list.append(mk_p)
        mv_list.append(mv_p)
        if i % 2 == 0:
            q, k, v, lf, qm_p = fox_project(xn_p, w_in_fox[j], b_f_fox[j])
            o_p = fox_prompt(q, k, v, lf)
            fk_p.append(k); fv_p.append(v); fl_p.append(lf)
            q, k, v, lf, qm_s = fox_project(xn_s, w_in_fox[j], b_f_fox[j])
            o_s = fox_sample(q, k, v, lf,
                             gather_pages(cache_fox_k, j, page_table),
                             gather_pages(cache_fox_v, j, page_table),
                             gather_pages(cache_fox_logf, j, page_table))
            fk_s.append(k); fv_s.append(v); fl_s.append(lf)
        else:
            lam_init = 0.8 - 0.6 * math.exp(-0.3 * i)
            lam = diff_lambda(lam_q1[j], lam_k1[j], lam_q2[j], lam_k2[j], lam_init)
            q, k, v, qm_p = diff_project(xn_p, w_in_diff[j], pos_p)
            o_p = diff_head_norm(diff_prompt(q, k, v, lam), g_subln[j], lam_init)
            dk_p.append(k); dv_p.append(v)
            q, k, v, qm_s = diff_project(xn_s, w_in_diff[j], pos_s)
            o_s = diff_sample(q, k, v,
                              gather_pages(cache_diff_k, j, page_table),
                              gather_pages(cache_diff_v, j, page_table), lam)
            o_s = diff_head_norm(o_s, g_subln[j], lam_init)
            dk_s.append(k); dv_s.append(v)
        h_p = finish_layer(h_p, o_p, qm_p, mk_p, mv_p, g_post_mix[i], w_out[i],
                           g_pre_ffn[i], g_post_ffn[i], w_gate_up[i], w_down[i])
        h_s = finish_layer(h_s, o_s, qm_s, cache_mem_k[i], cache_mem_v[i], g_post_mix[i], w_out[i],
                           g_pre_ffn[i], g_post_ffn[i], w_gate_up[i], w_down[i])

    fox_k_prompt, fox_v_prompt, fox_logf_prompt = jnp.stack(fk_p), jnp.stack(fv_p), jnp.stack(fl_p)
    fox_k_sample, fox_v_sample, fox_logf_sample = jnp.stack(fk_s), jnp.stack(fv_s), jnp.stack(fl_s)
    diff_k_prompt, diff_v_prompt = jnp.stack(dk_p), jnp.stack(dv_p)
    diff_k_sample, diff_v_sample = jnp.stack(dk_s), jnp.stack(dv_s)
    mem_k_prompt, mem_v_prompt = jnp.stack(mk_list), jnp.stack(mv_list)
    return (h_p, h_s, fox_k_prompt, fox_v_prompt, fox_logf_prompt, fox_k_sample, fox_v_sample,
            fox_logf_sample, diff_k_prompt, diff_v_prompt, diff_k_sample, diff_v_sample,
            mem_k_prompt, mem_v_prompt)
```

```python
import math
import os
from contextlib import ExitStack
import numpy as np
import concourse.bass as bass
import concourse.mybir as mybir
from concourse.bass_utils import run_bass_kernel_spmd

F32 = mybir.dt.float32
BF16 = mybir.dt.bfloat16
I32 = mybir.dt.int32
AF = mybir.ActivationFunctionType
ALU = mybir.AluOpType
AX = mybir.AxisListType

D = 1024
NCORES = 8
SW = 768
DFF = 2816
EPS = 1e-6
NEG = -1e30
NS = 4
KSTOP = os.environ.get('KSTOP', '')


class _Stop(Exception):
    pass

SAME_ENG_SYNC = ('noself' not in os.environ.get('KVAR', ''))


class _Rec:
    def __getattr__(self, name):
        def f(*a, **k):
            self.__dict__['call'] = (name, a, k)
            return self
        return f


class Sched:
    ENG = ['tensor', 'vector', 'scalar', 'gpsimd', 'sync']
    DMAQ = ['sync', 'gpsimd', 'scalar']
    PSUM_KEYS = frozenset(['pT', 'pA', 'pB', 'pS0', 'pS1', 'pO0', 'pO1', 'pO2'])

    def __init__(self, nc, stack, nslots=12):
        self.nc = nc
        self.prog = {e: [] for e in self.ENG}
        self.sem = {e: stack.enter_context(nc.semaphore("s_" + e)) for e in self.ENG[:4]}
        self.cnt = {e: 0 for e in self.ENG}
        self.ns = nslots
        self.dsl = {q: [stack.enter_context(nc.semaphore("d_%s_%d" % (q, i))) for i in range(nslots)]
                    for q in self.DMAQ}
        self.dn = {q: 0 for q in self.DMAQ}
        self.waited = {}
        self.lastw = {}
        self.readers = {}

    def op(self, eng, fn, reads=(), writes=(), dma=False):
        deps = []
        for k in reads:
            if k in self.lastw:
                deps.append(self.lastw[k])
            if k in self.PSUM_KEYS:
                deps.extend(t for t in self.readers.get(k, ()) if t[3] != eng)
        for k in writes:
            if k in self.lastw:
                deps.append(self.lastw[k])
            deps.extend(self.readers.get(k, ()))
        waits = []

        def need(semname, sem, val):
            key = (eng, semname)
            if self.waited.get(key, 0) >= val:
                return
            self.waited[key] = val
            waits.append((sem, val))

        for (semname, sem, val, peng, pdma) in deps:
            if peng == eng and not pdma:
                if eng == 'tensor' or not SAME_ENG_SYNC:
                    continue
            need(semname, sem, val)
        if dma:
            n = self.dn[eng]
            slot, rnd = n % self.ns, n // self.ns
            sem = self.dsl[eng][slot]
            semname = "d_%s_%d" % (eng, slot)
            if rnd > 0:
                need(semname, sem, 16 * rnd)
            val = 16 * (rnd + 1)
            inc = 16
            self.dn[eng] = n + 1
        else:
            sem = self.sem[eng]
            semname = "s_" + eng
            self.cnt[eng] += 1
            val = self.cnt[eng]
            inc = 1
        rec = _Rec()
        fn(rec)
        self.prog[eng].append((waits, rec.__dict__['call'], sem, inc))
        tok = (semname, sem, val, eng, dma)
        for k in writes:
            self.lastw[k] = tok
            self.readers[k] = []
        for k in reads:
            if k not in writes:
                self.readers.setdefault(k, []).append(tok)

    def barrier(self):
        for e in self.ENG:
            waits = []
            for p in self.ENG[:4]:
                v = self.cnt[p]
                if v > self.waited.get((e, 's_' + p), 0):
                    self.waited[(e, 's_' + p)] = v
                    waits.append((self.sem[p], v))
            for q in self.DMAQ:
                n = self.dn[q]
                for slot in range(self.ns):
                    c = n // self.ns + (1 if slot < n % self.ns else 0)
                    nm = "d_%s_%d" % (q, slot)
                    if c > 0 and 16 * c > self.waited.get((e, nm), 0):
                        self.waited[(e, nm)] = 16 * c
                        waits.append((self.dsl[q][slot], 16 * c))
            self.prog[e].append((waits, None, None, 0))

    def emit(self):
        fin = {}
        for q in self.DMAQ:
            lst = []
            for slot in range(self.ns):
                n = self.dn[q]
                cntslot = n // self.ns + (1 if slot < n % self.ns else 0)
                if cntslot > 0:
                    lst.append((self.dsl[q][slot], 16 * cntslot))
            fin[q] = lst
        with self.nc.Block() as block:
            for e in self.ENG:
                def body(eng, e=e):
                    for waits, fn, sem, inc in self.prog[e]:
                        for (s, v) in waits:
                            eng.wait_ge(s, v)
                        if fn is not None:
                            name, a, k = fn
                            getattr(eng, name)(*a, **k).then_inc(sem, inc)
                    for (s, v) in fin.get(e, ()):
                        eng.wait_ge(s, v)
                getattr(block, e)(body)


def build(S, NPG, NPOOL):
    NT = S // 128
    NB = S // 512
    nc = bass.Bass("TRN2", target_bir_lowering=False)

    def din(name, shape, dt=F32):
        return nc.dram_tensor(name, list(shape), dt, kind="ExternalInput").ap()

    def dout(name, shape, dt=F32):
        return nc.dram_tensor(name, list(shape), dt, kind="ExternalOutput").ap()

    def dscr(name, shape, dt=F32):
        return nc.dram_tensor(name, list(shape), dt, kind="Internal").ap()

    xp = din("xp", [S, D]); xs = din("xs", [NS, D])
    cfk = din("cfk", [NPOOL * 128, SW]); cfv = din("cfv", [NPOOL * 128, SW]); cfl = din("cfl", [NPOOL * 128, 12])
    cdk = din("cdk", [NPOOL * 128, SW]); cdv = din("cdv", [NPOOL * 128, SW])
    cmk = din("cmk", [2, NS, 256, 256]); cmv = din("cmv", [2, NS, 256, 256])
    pt = din("pt", [NS, NPG], I32)
    memp = din("memp", [256, D])
    w_in = [din("w_in_fox", [D, 2572]), din("w_in_diff", [D, 2560])]
    b_f = din("b_f", [1, 12])
    lamv = din("lamv", [4, 64])
    gsub = din("gsub", [128, 1])
    gcols = din("gcols", [2, 3, 128, 8])
    gpost = din("gpost", [2, 2, D])
    w_mem = din("w_mem", [2, D, 512]); w_out = din("w_out", [2, D, D])
    w_gu = din("w_gu", [2, D, 2 * DFF]); w_dn = din("w_dn", [2, DFF, D])
    c_ident = din("c_ident", [128, 128]); c_tri = din("c_tri", [128, 128]); c_suf = din("c_suf", [128, 128])
    c_maskT = din("c_maskT", [128, 128]); c_iota = din("c_iota", [128, 1])
    c_cos = din("c_cos", [S, 384]); c_sin = din("c_sin", [S, 384])
    c_cos_s = din("c_cos_s", [NS, 384]); c_sin_s = din("c_sin_s", [NS, 384])
    c_selfb = din("c_selfb", [128, 12])
    c_bm_fox = din("c_bm_fox", [12, SW]); c_bm_diff = din("c_bm_diff", [12, SW])

    o_yp = dout("o_yp", [S, D]); o_ys = dout("o_ys", [NS, D])
    o_kp = [dout("o_fkp", [S, SW]), dout("o_dkp", [S, SW])]
    o_vp = [dout("o_fvp", [S, SW]), dout("o_dvp", [S, SW])]
    o_ks = [dout("o_fks", [NS, SW]), dout("o_dks", [NS, SW])]
    o_vs = [dout("o_fvs", [NS, SW]), dout("o_dvs", [NS, SW])]
    o_flp = dout("o_flp", [S, 12]); o_fls = dout("o_fls", [NS, 12])
    o_mk = dout("o_mk", [2, 256, 256]); o_mv = dout("o_mv", [2, 256, 256])

    hmid_d = dscr("hmid_d", [S + NS, D]); h1_d = dscr("h1_d", [S + NS, D])
    qt_d = dscr("qt_d", [8, 128, S], BF16)
    zs_d = dscr("zs_d", [NS, 2572])

    top = ExitStack()
    try:
      with top:
        sch = Sched(nc, top)
        uid = [0]

        def sb(stack, shape, dt=F32, name=None):
            uid[0] += 1
            return stack.enter_context(nc.sbuf_tensor("%s_%d" % (name or "t", uid[0]), list(shape), dt))

        def ps(stack, shape, dt=F32, name=None):
            uid[0] += 1
            return stack.enter_context(nc.psum_tensor("%s_%d" % (name or "p", uid[0]), list(shape), dt))

        def dma(q, out, in_, reads, writes):
            sch.op(q, lambda e: e.dma_start(out=out, in_=in_), reads, writes, dma=True)

        def V_(fn, reads, writes):
            sch.op('vector', fn, reads, writes)

        def A_(fn, reads, writes):
            sch.op('scalar', fn, reads, writes)

        def G_(fn, reads, writes):
            sch.op('gpsimd', fn, reads, writes)

        def T_(fn, reads, writes):
            sch.op('tensor', fn, reads, writes)

        def mm(out, lhsT, rhs, start, stop, reads, writes):
            T_(lambda e: e.matmul(out, lhsT, rhs, start=start, stop=stop, skip_group_check=True), reads, writes)

        pT = ps(top, [128, 1024], BF16, "pT")
        pA = ps(top, [128, 512], F32, "pA"); pB = ps(top, [128, 512], F32, "pB")
        pS = [ps(top, [128, 512], F32, "pS%d" % i) for i in range(2)]
        pO = [ps(top, [128, 512], F32, "pO%d" % i) for i in range(3)]
        pT3 = pT[:].rearrange("p (c t) -> p c t", c=8)

        ident_f = sb(top, [128, 128], F32, "identf"); ident_b = sb(top, [128, 128], BF16, "identb")
        tri_f = sb(top, [128, 128], F32, "tri"); suf_f = sb(top, [128, 128], F32, "suf")
        ones_f = sb(top, [128, 128], F32, "onesf"); ones_b = sb(top, [128, 128], BF16, "onesb")
        maskT = sb(top, [128, 128], BF16, "maskT"); iota_c = sb(top, [128, 1], F32, "iota")
        selfb = sb(top, [128, 12], F32, "selfb")
        bm = [sb(top, [12, SW], F32, "bmf"), sb(top, [12, SW], F32, "bmd")]
        eps_c = sb(top, [128, 1], F32, "epsc")
        nlam = sb(top, [128, 1], F32, "nlam")
        gsub_c = sb(top, [128, 1], F32, "gsubc")
        bf_bc = sb(top, [128, 12], F32, "bfbc")
        OTs_self = sb(top, [128, 12, NS], BF16, "OTs_self")
        OTs_mem = sb(top, [64, 4, NS], BF16, "OTs_mem")

        dma('sync', ident_f[:], c_ident, [], ['identf'])
        if 'nocast' in os.environ.get('KVAR', ''):
            V_(lambda e: e.tensor_copy(out=ident_b[:], in_=ident_f[:]), ['identf'], ['identb'])
        else:
            dma('gpsimd', ident_b[:], c_ident, [], ['identb'])
        dma('sync', tri_f[:], c_tri, [], ['tri'])
        dma('sync', suf_f[:], c_suf, [], ['suf'])
        if 'nocast' in os.environ.get('KVAR', ''):
            dma('sync', tri_f[:], c_maskT, [], ['tri'])
            V_(lambda e: e.tensor_copy(out=maskT[:], in_=tri_f[:]), ['tri'], ['maskT'])
        else:
            dma('gpsimd', maskT[:], c_maskT, [], ['maskT'])
        dma('sync', iota_c[:], c_iota, [], ['iota'])
        dma('sync', selfb[:], c_selfb, [], ['selfb'])
        dma('sync', bm[0][:], c_bm_fox, [], ['bm0'])
        dma('sync', bm[1][:], c_bm_diff, [], ['bm1'])
        dma('sync', gsub_c[:], gsub, [], ['gsubc'])
        dma('sync', bf_bc[:], b_f.to_broadcast([128, 12]), [], ['bfbc'])
        V_(lambda e: e.memset(ones_f[:], 1.0), [], ['onesf'])
        V_(lambda e: e.memset(ones_b[:], 1.0), [], ['onesb'])
        V_(lambda e: e.memset(eps_c[:], EPS), [], ['epsc'])

        lam_init = 0.8 - 0.6 * math.exp(-0.3 * 1)
        if True:
            st = top
            lv = sb(st, [128, 4, 64], F32, "lv")
            for j in range(4):
                dma('sync', lv[:, j, :], lamv[j:j + 1, :].to_broadcast([128, 64]), [], ['lv%d' % j])
            pr = sb(st, [128, 2, 64], F32, "lpr"); sm = sb(st, [128, 2], F32, "lsm"); ex = sb(st, [128, 2], F32, "lex")
            V_(lambda e: e.tensor_tensor(out=pr[:, 0, :], in0=lv[:, 0, :], in1=lv[:, 1, :], op=ALU.mult), ['lv0', 'lv1'], ['lpr0'])
            V_(lambda e: e.tensor_tensor(out=pr[:, 1, :], in0=lv[:, 2, :], in1=lv[:, 3, :], op=ALU.mult), ['lv2', 'lv3'], ['lpr1'])
            V_(lambda e: e.tensor_reduce(out=sm[:], in_=pr[:], axis=AX.X, op=ALU.add), ['lpr0', 'lpr1'], ['lsm'])
            A_(lambda e: e.activation(out=ex[:], in_=sm[:], func=AF.Exp), ['lsm'], ['lex'])
            V_(lambda e: e.tensor_tensor(out=nlam[:], in0=ex[:, 1:2], in1=ex[:, 0:1], op=ALU.subtract), ['lex'], ['nlam'])
            V_(lambda e: e.tensor_scalar(out=nlam[:], in0=nlam[:], scalar1=-lam_init, scalar2=None, op0=ALU.add), ['nlam'], ['nlam'])
            V_(lambda e: e.tensor_scalar(out=gsub_c[:], in0=gsub_c[:], scalar1=(1.0 - lam_init), scalar2=None, op0=ALU.mult), ['gsubc'], ['gsubc'])

        cnt = {'ev': 0, 'pab': 0, 'q': 0}

        def ldq():
            cnt['q'] += 1
            return 'sync'

        n_junk = sb(top, [128, D], BF16, "njunk"); n_ss = sb(top, [128, 1], F32, "nss")
        n_rstd = sb(top, [128, 1], F32, "nrstd"); n_xn = sb(top, [128, D], BF16, "nxn")
        PTK = ['pT'] * 8

        def norm_from(st_h, r, gcol, gkey, xnT, xkey, tok0, hkey):
            junk, ss, rstd, xn = n_junk, n_ss, n_rstd, n_xn
            if 'noaccum' not in os.environ.get('KVAR', ''):
                A_(lambda e: e.activation(out=junk[0:r, :], in_=st_h[0:r, :], func=AF.Square, accum_out=ss[0:r, :]), [hkey], ['nss', 'njunk'])
            else:
                A_(lambda e: e.activation(out=junk[0:r, :], in_=st_h[0:r, :], func=AF.Square), [hkey], ['njunk'])
                V_(lambda e: e.tensor_reduce(out=ss[0:r, :], in_=junk[0:r, :], axis=AX.X, op=ALU.add), ['njunk'], ['nss'])
            V_(lambda e: e.tensor_scalar(out=ss[0:r, :], in0=ss[0:r, :], scalar1=1.0 / D, scalar2=EPS, op0=ALU.mult, op1=ALU.add), ['nss'], ['nss'])
            A_(lambda e: e.activation(out=ss[0:r, :], in_=ss[0:r, :], func=AF.Sqrt), ['nss'], ['nss'])
            V_(lambda e: e.reciprocal(out=rstd[0:r, :], in_=ss[0:r, :]), ['nss'], ['nrstd'])
            V_(lambda e: e.tensor_scalar(out=xn[0:r, :], in0=st_h[0:r, :], scalar1=rstd[0:r, :], scalar2=None, op0=ALU.mult), [hkey, 'nrstd'], ['nxn'])
            for c in range(8):
                T_(lambda e, c=c: e.transpose(pT3[:, c, 0:r], xn[0:r, c * 128:(c + 1) * 128], ident_b[0:r, 0:r]),
                   ['nxn', 'identb'], [PTK[c]])
            for c in range(8):
                if False:
                    A_(lambda e, c=c: e.activation(out=xnT[:, c, tok0:tok0 + r], in_=pT3[:, c, 0:r], func=AF.Identity, scale=gcol[:, c:c + 1]),
                       [PTK[c], gkey], [xkey + '_%d' % c])
                else:
                    V_(lambda e, c=c: e.tensor_scalar(out=xnT[:, c, tok0:tok0 + r], in0=pT3[:, c, 0:r], scalar1=gcol[:, c:c + 1], scalar2=None, op0=ALU.mult),
                       [PTK[c], gkey], [xkey + '_%d' % c])

        def norm_T(st_h, src_ap, r, gcol, gkey, xnT, xkey, tok0, hkey, srckeys=()):
            dma(ldq(), st_h[0:r, :], src_ap, list(srckeys), [hkey])
            norm_from(st_h, r, gcol, gkey, xnT, xkey, tok0, hkey)

        def xkeys(xkey):
            return [xkey + '_%d' % c for c in range(8)]

        def next_pab():
            cnt['pab'] += 1
            return (pA, 'pA') if cnt['pab'] % 2 else (pB, 'pB')

        def gemm_tm(xnT, xkey, tok0, r, W, wkey, col0, n, pt_, pkey):
            for c in range(8):
                mm(pt_[0:r, 0:n], xnT[:, c, tok0:tok0 + r], W[:, c, col0:col0 + n], c == 0, c == 7,
                   xkeys(xkey) + [wkey], [pkey])

        def load_w_bf16(W, wkey, src, rows_total, ncols, kchunks, prow=128):
            for c in range(kchunks):
                dma('gpsimd', W[0:prow, c, 0:ncols], src[c * prow:(c + 1) * prow, 0:ncols], [], [wkey])

        pn_ss = sb(top, [128, 2], F32, "pss"); pn_junk = sb(top, [128, 512], BF16, "pj"); pn_rstd = sb(top, [128, 1], F32, "prs")

        def post_norm_residual(st, y_halves, ykeys, r, gp, gpkey, hres, hkey, outt, okey):
            ss, junk, rstd = pn_ss, pn_junk, pn_rstd
            k = 'pn'
            for hf in range(2):
                A_(lambda e, hf=hf: e.activation(out=junk[0:r, :], in_=y_halves[hf][0:r, :], func=AF.Square, accum_out=ss[0:r, hf:hf + 1]),
                   [ykeys[hf]], [k + 'ss%d' % hf, k + 'j'])
            V_(lambda e: e.tensor_tensor(out=rstd[0:r, :], in0=ss[0:r, 0:1], in1=ss[0:r, 1:2], op=ALU.add), [k + 'ss0', k + 'ss1'], [k + 'r'])
            V_(lambda e: e.tensor_scalar(out=rstd[0:r, :], in0=rstd[0:r, :], scalar1=1.0 / D, scalar2=EPS, op0=ALU.mult, op1=ALU.add), [k + 'r'], [k + 'r'])
            A_(lambda e: e.activation(out=rstd[0:r, :], in_=rstd[0:r, :], func=AF.Sqrt), [k + 'r'], [k + 'r'])
            V_(lambda e: e.reciprocal(out=rstd[0:r, :], in_=rstd[0:r, :]), [k + 'r'], [k + 'r'])
            for hf in range(2):
                sl = slice(hf * 512, (hf + 1) * 512)
                V_(lambda e, hf=hf, sl=sl: e.scalar_tensor_tensor(out=outt[0:r, sl], in0=y_halves[hf][0:r, :], scalar=rstd[0:r, :], in1=gp[0:r, sl], op0=ALU.mult, op1=ALU.mult),
                   [ykeys[hf], k + 'r', gpkey], [okey + '%d' % hf])
                G_(lambda e, sl=sl: e.tensor_tensor(out=outt[0:r, sl], in0=outt[0:r, sl], in1=hres[0:r, sl], op=ALU.add),
                   [okey + '%d' % hf, hkey], [okey + '%d' % hf])

        def stop_if(tag):
            if KSTOP == tag:
                raise _Stop()

        for L in (range(2) if not KSTOP.startswith('pre') else []):
          try:
              fox = (L == 0)
              NIN = 2572 if fox else 2560
              QMC = 2316 if fox else 2304
              h_src = xp if L == 0 else h1_d[0:S, :]
              hs_src = xs if L == 0 else h1_d[S:S + NS, :]
              h_dst = h1_d[0:S, :] if L == 0 else o_yp
              hs_dst = h1_d[S:S + NS, :] if L == 0 else o_ys
              ck, cv = (cfk, cfv) if fox else (cdk, cdv)
              lay = ExitStack()
              with lay:
                  KT = sb(lay, [128, 6, S], BF16, "KT")
                  if fox:
                      VV = sb(lay, [128, NT, 12, 65], BF16, "VV")
                      G_(lambda e: e.memset(VV[:, :, :, 64:65], 1.0), [], ['VVones'])
                      negC = sb(lay, [128, NT, 12], F32, "negC"); Cend = sb(lay, [128, NT, 12], F32, "Cend")
                      Rrun = sb(lay, [128, 12], F32, "Rrun")
                      V_(lambda e: e.memset(Rrun[:], 0.0), [], ['Rrun'])
                  else:
                      VV = sb(lay, [128, NT, SW], BF16, "VV")
                  MKT = sb(lay, [128, 2, 256], BF16, "MKT"); MV = sb(lay, [128, 2, 4, 65], BF16, "MV")
                  G_(lambda e: e.memset(MV[:, :, :, 64:65], 1.0), [], ['MVones'])
                  gc = sb(lay, [128, 3, 8], F32, "gc")
                  for j in range(3):
                      dma('sync', gc[:, j, :], gcols[L, j], [], ['gc%d' % j])

                  p1 = ExitStack()
                  with p1:
                      Win = sb(p1, [128, 8, NIN], BF16, "Win")
                      load_w_bf16(Win, 'Win', w_in[L], D, NIN, 8)
                      Wm = sb(p1, [128, 8, 512], BF16, "Wm")
                      load_w_bf16(Wm, 'Wm', w_mem[L], D, 512, 8)
                      xnT = sb(p1, [128, 8, 128], BF16, "xnT")
                      ht = sb(p1, [128, D], F32, "ht")
                      zq = sb(p1, [128, SW], F32, "zq"); zk = sb(p1, [128, SW], F32, "zk"); zv = sb(p1, [128, SW], F32, "zv")
                      zr = sb(p1, [128, 268], F32, "zr")
                      qb_ = sb(p1, [128, SW], BF16, "qb"); kb_ = sb(p1, [128, SW], BF16, "kb"); mb_ = sb(p1, [128, 256], BF16, "mb")
                      qT = sb(p1, [128, 8, 128], BF16, "qTt")
                      cs = sb(p1, [128, 384], F32, "cs"); sn = sb(p1, [128, 384], F32, "sn")
                      t1 = sb(p1, [128, 384], F32, "t1"); t2 = sb(p1, [128, 384], F32, "t2")
                      lf = sb(p1, [128, 12], F32, "lf"); la = sb(p1, [128, 12], F32, "la"); lb = sb(p1, [128, 12], F32, "lb")

                      stop_if('P1w')
                      for mt in range(2):
                          norm_T(ht, memp[mt * 128:(mt + 1) * 128, :], 128, gc[:, 2, :], 'gc2', xnT, 'xnT', 0, 'ht')
                          stop_if('P1n')
                          gemm_tm(xnT, 'xnT', 0, 128, Wm, 'Wm', 0, 512, pA, 'pA')
                          stop_if('P1g')
                          V_(lambda e: e.tensor_copy(out=zq[:, 0:512], in_=pA[:, :]), ['pA'], ['zq'])
                          dma('sync', o_mk[L, mt * 128:(mt + 1) * 128, :], zq[:, 0:256], ['zq'], [])
                          dma('sync', o_mv[L, mt * 128:(mt + 1) * 128, :], zq[:, 256:512], ['zq'], [])
                          stop_if('P1o')
                          if os.environ.get('KACT') == 'dve':
                              V_(lambda e: e.tensor_copy(out=mb_[:, :], in_=pA[:, 0:256]), ['pA'], ['mb'])
                              V_(lambda e, mt=mt: e.tensor_copy(out=MV[:, mt, :, 0:64], in_=pA[:, 256:512].rearrange("p (h d) -> p h d", h=4)), ['pA', 'MVones'], ['MV'])
                          elif os.environ.get('KACT') == 'zq':
                              A_(lambda e: e.activation(out=mb_[:, :], in_=zq[:, 0:256], func=AF.Identity), ['zq'], ['mb'])
                              A_(lambda e, mt=mt: e.activation(out=MV[:, mt, :, 0:64], in_=zq[:, 256:512].rearrange("p (h d) -> p h d", h=4), func=AF.Identity), ['zq', 'MVones'], ['MV'])
                          else:
                              A_(lambda e: e.activation(out=mb_[:, :], in_=pA[:, 0:256], func=AF.Identity), ['pA'], ['mb'])
                              stop_if('P1v1')
                              A_(lambda e, mt=mt: e.activation(out=MV[:, mt, :, 0:64], in_=pA[:, 256:512].rearrange("p (h d) -> p h d", h=4), func=AF.Identity), ['pA', 'MVones'], ['MV'])
                          stop_if('P1v')
                          for g in range(2):
                              T_(lambda e, g=g: e.transpose(pT3[:, g, :], mb_[:, g * 128:(g + 1) * 128], ident_b[:, :]), ['mb', 'identb'], [PTK[g]])
                          V_(lambda e, mt=mt: e.tensor_copy(out=MKT[:, :, mt * 128:(mt + 1) * 128], in_=pT3[:, 0:2, :]), PTK[0:2], ['MKT'])

                      stop_if('P1m')
                      for t in range(NT + 1):
                          smp = (t == NT)
                          r = NS if smp else 128
                          src = hs_src if smp else h_src[t * 128:(t + 1) * 128, :]
                          norm_T(ht, src, r, gc[:, 0, :], 'gc0', xnT, 'xnT', 0, 'ht', ['h1_d'])
                          if not fox:
                              dma(ldq(), cs[0:r, :], (c_cos_s if smp else c_cos[t * 128:(t + 1) * 128, :]), [], ['cs'])
                              dma(ldq(), sn[0:r, :], (c_sin_s if smp else c_sin[t * 128:(t + 1) * 128, :]), [], ['sn'])
                          for which in range(3):
                              zt = (zq, zk, zv)[which]; zkey = ('zq', 'zk', 'zv')[which]
                              for (c0, n) in ((0, 512), (512, 256)):
                                  pp, pk = next_pab()
                                  gemm_tm(xnT, 'xnT', 0, r, Win, 'Win', which * SW + c0, n, pp, pk)
                                  if fox or which == 2:
                                      A_(lambda e, pp=pp, zt=zt, c0=c0, n=n: e.activation(out=zt[0:r, c0:c0 + n], in_=pp[0:r, 0:n], func=AF.Identity), [pk], [zkey])
                                  else:
                                      nh = n // 64; h0 = c0 // 64
                                      x = pp[0:r, 0:n].rearrange("p (h two d) -> p h two d", two=2, d=32)
                                      o = zt[0:r, c0:c0 + n].rearrange("p (h two d) -> p h two d", two=2, d=32)
                                      cc = cs[0:r, h0 * 32:(h0 + nh) * 32].rearrange("p (h d) -> p h d", d=32)
                                      sc = sn[0:r, h0 * 32:(h0 + nh) * 32].rearrange("p (h d) -> p h d", d=32)
                                      a1 = t1[0:r, 0:nh * 32].rearrange("p (h d) -> p h d", d=32)
                                      a2 = t2[0:r, 0:nh * 32].rearrange("p (h d) -> p h d", d=32)
                                      V_(lambda e, x=x, cc=cc, a1=a1: e.tensor_tensor(out=a1, in0=x[:, :, 0, :], in1=cc, op=ALU.mult), [pk, 'cs'], ['t1'])
                                      V_(lambda e, x=x, sc=sc, a2=a2: e.tensor_tensor(out=a2, in0=x[:, :, 1, :], in1=sc, op=ALU.mult), [pk, 'sn'], ['t2'])
                                      V_(lambda e, o=o, a1=a1, a2=a2: e.tensor_tensor(out=o[:, :, 0, :], in0=a1, in1=a2, op=ALU.subtract), ['t1', 't2'], [zkey])
                                      V_(lambda e, x=x, sc=sc, a1=a1: e.tensor_tensor(out=a1, in0=x[:, :, 0, :], in1=sc, op=ALU.mult), [pk, 'sn'], ['t1'])
                                      V_(lambda e, x=x, cc=cc, a2=a2: e.tensor_tensor(out=a2, in0=x[:, :, 1, :], in1=cc, op=ALU.mult), [pk, 'cs'], ['t2'])
                                      V_(lambda e, o=o, a1=a1, a2=a2: e.tensor_tensor(out=o[:, :, 1, :], in0=a1, in1=a2, op=ALU.add), ['t1', 't2'], [zkey])
                          nrest = NIN - 3 * SW
                          pp, pk = next_pab()
                          gemm_tm(xnT, 'xnT', 0, r, Win, 'Win', 3 * SW, nrest, pp, pk)
                          A_(lambda e, pp=pp: e.activation(out=zr[0:r, 0:nrest], in_=pp[0:r, 0:nrest], func=AF.Identity), [pk], ['zr'])
                          if fox:
                              V_(lambda e: e.tensor_tensor(out=la[0:r, :], in0=zr[0:r, 0:12], in1=bf_bc[0:r, :], op=ALU.add), ['zr', 'bfbc'], ['la'])
                              V_(lambda e: e.tensor_scalar(out=lb[0:r, :], in0=la[0:r, :], scalar1=-1.0, scalar2=None, op0=ALU.mult), ['la'], ['lb'])
                              V_(lambda e: e.tensor_tensor(out=lb[0:r, :], in0=lb[0:r, :], in1=la[0:r, :], op=ALU.min), ['la', 'lb'], ['lb'])
                              A_(lambda e: e.activation(out=lb[0:r, :], in_=lb[0:r, :], func=AF.Exp), ['lb'], ['lb'])
                              A_(lambda e: e.activation(out=lb[0:r, :], in_=lb[0:r, :], func=AF.Ln, bias=1.0), ['lb'], ['lb'])
                              V_(lambda e: e.tensor_scalar(out=la[0:r, :], in0=la[0:r, :], scalar1=0.0, scalar2=None, op0=ALU.min), ['la'], ['la'])
                              V_(lambda e: e.tensor_tensor(out=lf[0:r, :], in0=la[0:r, :], in1=lb[0:r, :], op=ALU.subtract), ['la', 'lb'], ['lf'])
                          kcol = 0
                          if smp:
                              dma('sync', zs_d[:, 0:SW], zq[0:NS, :], ['zq'], ['zs_d'])
                              dma('sync', zs_d[:, SW:2 * SW], zk[0:NS, :], ['zk'], ['zs_d'])
                              dma('sync', zs_d[:, 2 * SW:3 * SW], zv[0:NS, :], ['zv'], ['zs_d'])
                              dma('sync', zs_d[:, 3 * SW:NIN], zr[0:NS, 0:nrest], ['zr'], ['zs_d'])
                              dma('sync', o_ks[L], zk[0:NS, :], ['zk'], [])
                              dma('sync', o_vs[L], zv[0:NS, :], ['zv'], [])
                              if fox:
                                  dma('sync', zs_d[:, 3 * SW:3 * SW + 12], lf[0:NS, :], ['lf', 'zs_d'], ['zs_d'])
                                  dma('sync', o_fls, lf[0:NS, :], ['lf'], [])
                              continue
                          ts_ = slice(t * 128, (t + 1) * 128)
                          dma('sync', o_kp[L][ts_, :], zk[:, :], ['zk'], [])
                          dma('sync', o_vp[L][ts_, :], zv[:, :], ['zv'], [])
                          V_(lambda e: e.tensor_copy(out=qb_[:, :], in_=zq[:, :]), ['zq'], ['qb'])
                          G_(lambda e: e.tensor_copy(out=kb_[:, :], in_=zk[:, :]), ['zk'], ['kb'])
                          V_(lambda e: e.tensor_copy(out=mb_[:, :], in_=zr[:, nrest - 256:nrest]), ['zr'], ['mb'])
                          if fox:
                              G_(lambda e, t=t: e.tensor_copy(out=VV[:, t, :, 0:64], in_=zv[:, :].rearrange("p (h d) -> p h d", h=12)), ['zv', 'VVones'], ['VV%d' % (t // 4)])
                          else:
                              G_(lambda e, t=t: e.tensor_copy(out=VV[:, t, :], in_=zv[:, :]), ['zv'], ['VV%d' % (t // 4)])
                          for g in range(6):
                              T_(lambda e, g=g: e.transpose(pT3[:, g, :], kb_[:, g * 128:(g + 1) * 128], ident_b[:, :]), ['kb', 'identb'], [PTK[g]])
                          V_(lambda e, t=t: e.tensor_copy(out=KT[:, :, t * 128:(t + 1) * 128], in_=pT3[:, 0:6, :]), PTK[0:6], ['KT%d' % (t // 4)])
                          for g in range(6):
                              T_(lambda e, g=g: e.transpose(pT3[:, g, :], qb_[:, g * 128:(g + 1) * 128], ident_b[:, :]), ['qb', 'identb'], [PTK[g]])
                          for g in range(2):
                              T_(lambda e, g=g: e.transpose(pT3[:, 6 + g, :], mb_[:, g * 128:(g + 1) * 128], ident_b[:, :]), ['mb', 'identb'], [PTK[6 + g]])
                          A_(lambda e: e.activation(out=qT[:, :, :], in_=pT3[:, :, :], func=AF.Identity), PTK, ['qT'])
                          dma('sync', qt_d[:, :, t * 128:(t + 1) * 128].rearrange("c p t -> p c t"), qT[:, :, :], ['qT'], ['qt_d%d' % (t // 4)])
                          if fox:
                              dma('sync', o_flp[ts_, :], lf[:, :], ['lf'], [])
                              mm(pO[0][:, 0:12], tri_f[:, :], lf[:, :], True, False, ['tri', 'lf'], ['pO0'])
                              mm(pO[0][:, 0:12], ones_f[:, :], Rrun[:, :], False, True, ['onesf', 'Rrun'], ['pO0'])
                              V_(lambda e, t=t: e.tensor_scalar(out=negC[:, t, :], in0=pO[0][:, 0:12], scalar1=-1.0, scalar2=None, op0=ALU.mult), ['pO0'], ['negC'])
                              V_(lambda e: e.tensor_tensor(out=Rrun[:, :], in0=Rrun[:, :], in1=lf[:, :], op=ALU.add), ['Rrun', 'lf'], ['Rrun'])
                              mm(pO[1][:, 0:12], ones_f[:, :], Rrun[:, :], True, True, ['onesf', 'Rrun'], ['pO1'])
                              V_(lambda e, t=t: e.tensor_copy(out=Cend[:, t, :], in_=pO[1][:, 0:12]), ['pO1'], ['Cend'])

                  stop_if('P1_%d' % L)
                  sch.barrier()
                  dec = ExitStack()
                  with dec:
                      PG = 2 if NPG % 2 == 0 else 1
                      NP1 = NPG + 1
                      Kb = [sb(dec, [128, PG, SW], F32, "Kb%d" % i) for i in range(2)]
                      Vb = [sb(dec, [128, PG, SW], F32, "Vb%d" % i) for i in range(2)]
                      Vbf = [sb(dec, [128, SW + 1], BF16, "Vbf%d" % i) for i in range(2)]
                      for i in range(2):
                          G_(lambda e, i=i: e.memset(Vbf[i][:, SW:SW + 1], 1.0), [], ['Vbf%d' % i])
                      qbc = sb(dec, [128, SW], F32, "qbc"); tmp = [sb(dec, [128, SW], F32, "dtmp%d" % i) for i in range(2)]
                      Ssc = sb(dec, [128, NP1, 12], F32, "Ssc"); Dd = sb(dec, [128, NP1, 12], F32, "Dd")
                      Pp = sb(dec, [128, NP1, 12], BF16, "Pp")
                      LF = sb(dec, [128, NPG, 12], F32, "LF"); E0 = sb(dec, [128, NPG, 12], F32, "E0"); E1 = sb(dec, [128, NPG, 12], F32, "E1")
                      ptb = sb(dec, [128, NPG], I32, "ptb"); ptf = sb(dec, [128, NPG], F32, "ptf"); idx = sb(dec, [128, NPG], I32, "idx")
                      lfn = sb(dec, [128, 12], F32, "lfn")
                      mx = sb(dec, [128, 12], F32, "mx"); mrow = sb(dec, [12, 1], F32, "mrow"); dg = sb(dec, [12, 12], F32, "dg")
                      nmb = sb(dec, [128, 12], F32, "nmb")
                      ob = sb(dec, [12, SW + 1], F32, "ob"); ob2 = sb(dec, [12, SW], F32, "ob2")
                      od = sb(dec, [12, 128], F32, "od"); rl = sb(dec, [12, 1], F32, "rl")
                      oT = sb(dec, [128, 12], F32, "oT"); o6 = sb(dec, [128, 6], F32, "o6")
                      o6t = sb(dec, [6, 128], F32, "o6t"); o6s = sb(dec, [6, 1], F32, "o6s"); o6j = sb(dec, [6, 128], F32, "o6j")
                      gsr = sb(dec, [6, 128], F32, "gsr")
                      dma('sync', gsr[:, :], gsub.rearrange("p o -> o p").to_broadcast([6, 128]), [], ['gsr'])
                      V_(lambda e: e.tensor_scalar(out=gsr[:, :], in0=gsr[:, :], scalar1=(1.0 - lam_init), scalar2=None, op0=ALU.mult), ['gsr'], ['gsr'])
                      mK = sb(dec, [128, 2, 256], F32, "mK"); mV = sb(dec, [128, 2, 256], F32, "mV")
                      mVb = sb(dec, [128, 257], BF16, "mVb")
                      G_(lambda e: e.memset(mVb[:, 256:257], 1.0), [], ['mVb'])
                      qmb = sb(dec, [128, 256], F32, "qmb"); mS = sb(dec, [128, 2, 4], F32, "mS"); mP = sb(dec, [128, 2, 4], BF16, "mP")

                      def softmax_pv(Sv, skey, Pv, pkey, npages, H, vload, vdim, bmt, bmkey, tagk):
                          V_(lambda e: e.tensor_reduce(out=mx[:, 0:H], in_=Sv.rearrange("p n h -> p h n"), axis=AX.X, op=ALU.max), [skey], ['mx'])
                          T_(lambda e: e.transpose(pA[0:H, 0:128], mx[:, 0:H], ident_f[:, :]), ['mx', 'identf'], ['pA'])
                          V_(lambda e: e.tensor_reduce(out=mrow[0:H, :], in_=pA[0:H, 0:128], axis=AX.X, op=ALU.max), ['pA'], ['mrow'])
                          V_(lambda e: e.tensor_scalar(out=dg[0:H, 0:H], in0=ident_f[0:H, 0:H], scalar1=mrow[0:H, :], scalar2=-1.0, op0=ALU.mult, op1=ALU.mult), ['mrow', 'identf'], ['dg'])
                          mm(pB[:, 0:H], ones_f[0:H, :], dg[0:H, 0:H], True, True, ['onesf', 'dg'], ['pB'])
                          V_(lambda e: e.tensor_copy(out=nmb[:, 0:H], in_=pB[:, 0:H]), ['pB'], ['nmb'])
                          for h in range(H):
                              A_(lambda e, h=h: e.activation(out=Pv[:, :, h], in_=Sv[:, :, h], func=AF.Exp, bias=nmb[:, h:h + 1]), [skey, 'nmb'], [pkey])
                          nv = vdim + 1
                          halves = [(0, min(512, nv))] + ([(512, nv)] if nv > 512 else [])
                          pacc = [pO[0], pO[1]]
                          for j in range(npages):
                              vb, vkey = vload(j)
                              for hi, (a, b) in enumerate(halves):
                                  mm(pacc[hi][0:H, 0:b - a], Pv[:, j, :], vb[:, a:b], j == 0, j == npages - 1, [pkey, vkey], ['pO%d' % hi])
                          for hi, (a, b) in enumerate(halves):
                              V_(lambda e, hi=hi, a=a, b=b: e.tensor_copy(out=ob[0:H, a:b], in_=pacc[hi][0:H, 0:b - a]), ['pO%d' % hi], ['ob'])
                          V_(lambda e: e.reciprocal(out=rl[0:H, :], in_=ob[0:H, vdim:vdim + 1]), ['ob'], ['rl'])
                          nblk = bmt[1]; bw = vdim // nblk
                          V_(lambda e: e.tensor_tensor(out=ob2[0:H, 0:vdim], in0=ob[0:H, 0:vdim], in1=bmt[0][0:H, 0:vdim], op=ALU.mult), ['ob', bmkey], ['ob2'])
                          V_(lambda e: e.tensor_reduce(out=od[0:H, 0:bw], in_=ob2[0:H, 0:vdim].rearrange("p (b d) -> p d b", b=nblk), axis=AX.X, op=ALU.add), ['ob2'], ['od'])
                          V_(lambda e: e.tensor_scalar(out=od[0:H, 0:bw], in0=od[0:H, 0:bw], scalar1=rl[0:H, :], scalar2=None, op0=ALU.mult), ['od', 'rl'], ['od'])
                          return bw

                      for s in range(NS):
                          dma('sync', ptb[:, :], pt[s:s + 1, :].to_broadcast([128, NPG]), [], ['ptb'])
                          V_(lambda e: e.tensor_copy(out=ptf[:, :], in_=ptb[:, :]), ['ptb'], ['ptf'])
                          V_(lambda e: e.tensor_scalar(out=ptf[:, :], in0=ptf[:, :], scalar1=128.0, scalar2=iota_c[:, :], op0=ALU.mult, op1=ALU.add), ['ptf', 'iota'], ['ptf'])
                          V_(lambda e: e.tensor_copy(out=idx[:, :], in_=ptf[:, :]), ['ptf'], ['idx'])
                          dma('sync', qbc[:, :], zs_d[s:s + 1, 0:SW].to_broadcast([128, SW]), ['zs_d'], ['qbc'])
                          if fox:
                              for j in range(NPG):
                                  sch.op('gpsimd', lambda e, j=j: e.indirect_dma_start(out=LF[:, j, :], out_offset=None, in_=cfl, in_offset=bass.IndirectOffsetOnAxis(ap=idx[:, j:j + 1], axis=0)), ['idx'], ['LF'], dma=True)
                              dma('sync', lfn[:, :], zs_d[s:s + 1, 3 * SW:3 * SW + 12].to_broadcast([128, 12]), ['zs_d'], ['lfn'])
                              LF2 = LF[:, :, :].rearrange("p n h -> p (n h)")
                              nn = NPG * 12
                              for (a, b) in [(i, min(i + 512, nn)) for i in range(0, nn, 512)]:
                                  mm(pA[:, 0:b - a], suf_f[:, :], LF2[:, a:b], True, True, ['suf', 'LF'], ['pA'])
                                  V_(lambda e, a=a, b=b: e.tensor_copy(out=Dd[:, 0:NPG, :].rearrange("p n h -> p (n h)")[:, a:b], in_=pA[:, 0:b - a]), ['pA'], ['Dd'])
                                  mm(pB[:, 0:b - a], ones_f[:, :], LF2[:, a:b], True, True, ['onesf', 'LF'], ['pB'])
                                  V_(lambda e, a=a, b=b: e.tensor_copy(out=E1[:, :, :].rearrange("p n h -> p (n h)")[:, a:b], in_=pB[:, 0:b - a]), ['pB'], ['E1'])
                              V_(lambda e: e.memset(E0[:, :, :], 0.0), [], ['E0'])
                              if NPG > 1:
                                  V_(lambda e: e.tensor_copy(out=E0[:, 0:NPG - 1, :], in_=E1[:, 1:NPG, :]), ['E1', 'E0'], ['E0'])
                              cur, oth, ck_, ok_ = E0, E1, 'E0', 'E1'
                              sh = 1
                              while sh < NPG:
                                  V_(lambda e, cur=cur, oth=oth: e.tensor_copy(out=oth[:, :, :], in_=cur[:, :, :]), [ck_], [ok_])
                                  V_(lambda e, cur=cur, oth=oth, sh=sh: e.tensor_tensor(out=oth[:, 0:NPG - sh, :], in0=cur[:, 0:NPG - sh, :], in1=cur[:, sh:NPG, :], op=ALU.add), [ck_, ok_], [ok_])
                                  cur, oth, ck_, ok_ = oth, cur, ok_, ck_
                                  sh *= 2
                              V_(lambda e, cur=cur: e.tensor_tensor(out=Dd[:, 0:NPG, :], in0=Dd[:, 0:NPG, :], in1=cur[:, :, :], op=ALU.add), ['Dd', ck_], ['Dd'])
                              for j in range(NPG):
                                  V_(lambda e, j=j: e.tensor_tensor(out=Dd[:, j, :], in0=Dd[:, j, :], in1=lfn[:, :], op=ALU.add), ['Dd', 'lfn'], ['Dd'])
                          else:
                              V_(lambda e: e.memset(Dd[:, 0:NPG, :], 0.0), [], ['Dd'])
                          V_(lambda e: e.tensor_copy(out=Dd[:, NPG, :], in_=selfb[:, :]), ['selfb', 'Dd'], ['Dd'])

                          nch = NPG // PG
                          for chn in range(nch + 1):
                              kb = Kb[chn % 2]; kkey = 'Kb%d' % (chn % 2)
                              if chn < nch:
                                  for p in range(PG):
                                      j = chn * PG + p
                                      sch.op('gpsimd', lambda e, j=j, p=p, kb=kb: e.indirect_dma_start(out=kb[:, p, :], out_offset=None, in_=ck, in_offset=bass.IndirectOffsetOnAxis(ap=idx[:, j:j + 1], axis=0)), ['idx'], [kkey], dma=True)
                                  pages = [(chn * PG + p, p) for p in range(PG)]
                              else:
                                  G_(lambda e, kb=kb: e.memset(kb[:, 0, :], 0.0), [], [kkey])
                                  dma('sync', kb[0:1, 0, :], zs_d[s:s + 1, SW:2 * SW], ['zs_d', kkey], [kkey])
                                  pages = [(NPG, 0)]
                              for (j, p) in pages:
                                  tm_ = tmp[j % 2]; tk = 'dtmp%d' % (j % 2)
                                  G_(lambda e, kb=kb, p=p, tm_=tm_: e.tensor_tensor(out=tm_[:, :], in0=kb[:, p, :], in1=qbc[:, :], op=ALU.mult), [kkey, 'qbc'], [tk])
                                  V_(lambda e, j=j, tm_=tm_: e.tensor_reduce(out=Ssc[:, j, :], in_=tm_[:, :].rearrange("p (h d) -> p h d", d=64), axis=AX.X, op=ALU.add), [tk], ['Ssc'])
                          V_(lambda e: e.scalar_tensor_tensor(out=Ssc[:, :, :], in0=Ssc[:, :, :], scalar=0.125, in1=Dd[:, :, :], op0=ALU.mult, op1=ALU.add), ['Ssc', 'Dd'], ['Ssc'])

                          def vload(j, s=s):
                              chn, p = (j // PG, j % PG) if j < NPG else (nch, 0)
                              vb = Vb[chn % 2]; vkey = 'Vb%d' % (chn % 2)
                              if p == 0:
                                  if j < NPG:
                                      for pp_ in range(PG):
                                          jj = chn * PG + pp_
                                          sch.op('gpsimd', lambda e, jj=jj, pp_=pp_, vb=vb: e.indirect_dma_start(out=vb[:, pp_, :], out_offset=None, in_=cv, in_offset=bass.IndirectOffsetOnAxis(ap=idx[:, jj:jj + 1], axis=0)), ['idx'], [vkey], dma=True)
                                  else:
                                      G_(lambda e, vb=vb: e.memset(vb[:, 0, :], 0.0), [], [vkey])
                                      dma('sync', vb[0:1, 0, :], zs_d[s:s + 1, 2 * SW:3 * SW], ['zs_d', vkey], [vkey])
                              vf = Vbf[j % 2]; fk = 'Vbf%d' % (j % 2)
                              A_(lambda e, vb=vb, p=p, vf=vf: e.activation(out=vf[:, 0:SW], in_=vb[:, p, :], func=AF.Identity), [vkey], [fk])
                              return vf, fk

                          bw = softmax_pv(Ssc[:, :, :], 'Ssc', Pp, 'Pp', NP1, 12, vload, SW, (bm[L], 12 if fox else 6), 'bm%d' % L, 'a')
                          if fox:
                              T_(lambda e: e.transpose(pA[0:64, 0:12], od[0:12, 0:64], ident_f[0:12, 0:12]), ['od', 'identf'], ['pA'])
                              V_(lambda e, s=s: e.tensor_copy(out=OTs_self[0:64, :, s], in_=pA[0:64, 0:12]), ['pA'], ['OTs_self'])
                          else:
                              T_(lambda e: e.transpose(pA[:, 0:12], od[0:12, 0:128], ident_f[0:12, 0:12]), ['od', 'identf'], ['pA'])
                              V_(lambda e: e.tensor_copy(out=oT[:, :], in_=pA[:, 0:12]), ['pA'], ['oT'])
                              oTv = oT[:, :].rearrange("p (h two) -> p h two", two=2)
                              V_(lambda e: e.scalar_tensor_tensor(out=o6[:, :], in0=oTv[:, :, 1], scalar=nlam[:, :], in1=oTv[:, :, 0], op0=ALU.mult, op1=ALU.add), ['oT', 'nlam'], ['o6'])
                              T_(lambda e: e.transpose(pB[0:6, 0:128], o6[:, :], ident_f[:, :]), ['o6', 'identf'], ['pB'])
                              V_(lambda e: e.tensor_copy(out=o6t[:, :], in_=pB[0:6, 0:128]), ['pB'], ['o6t'])
                              V_(lambda e: e.tensor_tensor(out=o6j[:, :], in0=o6t[:, :], in1=o6t[:, :], op=ALU.mult), ['o6t'], ['o6j'])
                              V_(lambda e: e.tensor_reduce(out=o6s[:, :], in_=o6j[:, :], axis=AX.X, op=ALU.add), ['o6j'], ['o6s'])
                              V_(lambda e: e.tensor_scalar(out=o6s[:, :], in0=o6s[:, :], scalar1=1.0 / 128, scalar2=EPS, op0=ALU.mult, op1=ALU.add), ['o6s'], ['o6s'])
                              A_(lambda e: e.activation(out=o6s[:, :], in_=o6s[:, :], func=AF.Sqrt), ['o6s'], ['o6s'])
                              V_(lambda e: e.reciprocal(out=o6s[:, :], in_=o6s[:, :]), ['o6s'], ['o6s'])
                              V_(lambda e: e.scalar_tensor_tensor(out=o6t[:, :], in0=o6t[:, :], scalar=o6s[:, :], in1=gsr[:, :], op0=ALU.mult, op1=ALU.mult), ['o6t', 'o6s', 'gsr'], ['o6t'])
                              T_(lambda e: e.transpose(pA[:, 0:6], o6t[:, :], ident_f[0:6, 0:6]), ['o6t', 'identf'], ['pA'])
                              V_(lambda e, s=s: e.tensor_copy(out=OTs_self[:, 0:6, s], in_=pA[:, 0:6]), ['pA'], ['OTs_self'])

                          dma('sync', qmb[:, :], zs_d[s:s + 1, QMC:QMC + 256].to_broadcast([128, 256]), ['zs_d'], ['qmb'])
                          dma('sync', mK[:, :, :], cmk[L, s].rearrange("(t p) f -> p t f", p=128), [], ['mK'])
                          dma('sync', mV[:, :, :], cmv[L, s].rearrange("(t p) f -> p t f", p=128), [], ['mV'])
                          for mt in range(2):
                              G_(lambda e, mt=mt: e.tensor_tensor(out=tmp[mt][:, 0:256], in0=mK[:, mt, :], in1=qmb[:, :], op=ALU.mult), ['mK', 'qmb'], ['dtmp%d' % mt])
                              V_(lambda e, mt=mt: e.tensor_reduce(out=mS[:, mt, :], in_=tmp[mt][:, 0:256].rearrange("p (h d) -> p h d", d=64), axis=AX.X, op=ALU.add), ['dtmp%d' % mt], ['mS'])
                          V_(lambda e: e.tensor_scalar(out=mS[:, :, :], in0=mS[:, :, :], scalar1=0.125, scalar2=None, op0=ALU.mult), ['mS'], ['mS'])

                          def mvload(j):
                              A_(lambda e, j=j: e.activation(out=mVb[:, 0:256], in_=mV[:, j, :], func=AF.Identity), ['mV', 'mVb'], ['mVb'])
                              return mVb, 'mVb'
                          softmax_pv(mS[:, :, :], 'mS', mP, 'mP', 2, 4, mvload, 256, (bm[0], 4), 'bm0', 'm')
                          T_(lambda e: e.transpose(pA[0:64, 0:4], od[0:4, 0:64], ident_f[0:4, 0:4]), ['od', 'identf'], ['pA'])
                          V_(lambda e, s=s: e.tensor_copy(out=OTs_mem[0:64, :, s], in_=pA[0:64, 0:4]), ['pA'], ['OTs_mem'])

                  stop_if('DEC_%d' % L)
                  sch.barrier()
                  p2 = ExitStack()
                  with p2:
                      nsc = 12 if fox else 6
                      pk_ = 64 if fox else 128
                      Wo = sb(p2, [128, nsc, D], BF16, "Wo"); Wom = sb(p2, [64, 4, D], BF16, "Wom")
                      for c in range(nsc):
                          dma('gpsimd', Wo[0:pk_, c, :], w_out[L, c * pk_:(c + 1) * pk_, :], [], ['Wo'])
                      for c in range(4):
                          dma('gpsimd', Wom[0:64, c, :], w_out[L, SW + c * 64:SW + (c + 1) * 64, :], [], ['Wom'])
                      gp = sb(p2, [128, D], F32, "gp")
                      dma('sync', gp[:, :], gpost[L, 0:1, :].to_broadcast([128, D]), [], ['gp'])
                      QT = sb(p2, [128, 8, 512], BF16, "QT")
                      PT = [sb(p2, [128, 512], BF16, "PT%d" % i) for i in range(3)]
                      OT = sb(p2, [128, nsc, 512], BF16, "OT"); OTm = sb(p2, [64, 4, 512], BF16, "OTm")
                      Bq = sb(p2, [128, 4, 12, max(NT, 1)], F32, "Bq")
                      rlt = sb(p2, [128, 512], F32, "rlt"); rbc = sb(p2, [128, 512], F32, "rbc")
                      on0 = sb(p2, [128, 512], F32, "on0"); on1 = sb(p2, [128, 512], F32, "on1"); sq = sb(p2, [128, 512], BF16, "sq")
                      hres = sb(p2, [128, D], F32, "hres"); hout = sb(p2, [128, D], F32, "hout")
                      sc_ = {'i': 0, 'o': 0}

                      def attend(qb, Kt, kpart, kg, vfn, nkt, bias_fn, causal, qsrc, qg, out_fn, aug):
                          sc_['o'] += 1
                          po = pO[sc_['o'] % 3]; pok = 'pO%d' % (sc_['o'] % 3)
                          if not aug:
                              pl, plk = next_pab()
                          for kt in range(nkt):
                              qlo = max(0, kt - 4 * qb) if causal else 0
                              c0 = qlo * 128
                              sc_['i'] += 1
                              psb = pS[sc_['i'] % 2]; psk = 'pS%d' % (sc_['i'] % 2)
                              ptb_ = PT[sc_['i'] % 3]; ptk = 'PT%d' % (sc_['i'] % 3)
                              mm(psb[:, c0:512], Kt[kpart, kg, kt * 128:(kt + 1) * 128], qsrc[kpart, qg, c0:512], True, True,
                                 ['KT%d' % (kt // 4) if causal else 'MKT', 'QT'], [psk])
                              if bias_fn is None:
                                  A_(lambda e, psb=psb, ptb_=ptb_, c0=c0: e.activation(out=ptb_[:, c0:512], in_=psb[:, c0:512], func=AF.Exp, scale=0.125), [psk], [ptk])
                              else:
                                  for qt in range(qlo, 4):
                                      A_(lambda e, psb=psb, ptb_=ptb_, qt=qt, kt=kt: e.activation(out=ptb_[:, qt * 128:(qt + 1) * 128], in_=psb[:, qt * 128:(qt + 1) * 128], func=AF.Exp, scale=0.125, bias=bias_fn(qt, kt)), [psk, 'Bq'], [ptk])
                              if causal and kt >= 4 * qb:
                                  G_(lambda e, ptb_=ptb_, c0=c0: e.tensor_tensor(out=ptb_[:, c0:c0 + 128], in0=ptb_[:, c0:c0 + 128], in1=maskT[:, :], op=ALU.mult), [ptk, 'maskT'], [ptk])
                              vl, vk = vfn(kt)
                              mm(po[0:(65 if aug else 128), c0:512], vl, ptb_[:, c0:512], kt == 0, kt == nkt - 1, [vk, ptk], [pok])
                              if not aug:
                                  mm(pl[:, c0:512], ones_b[:, :], ptb_[:, c0:512], kt == 0, kt == nkt - 1, ['onesb', ptk], [plk])
                          if aug:
                              V_(lambda e: e.reciprocal(out=rlt[64:65, :], in_=po[64:65, :]), [pok], ['rlt'])
                              px, pxk = next_pab()
                              mm(px[0:64, :], ones_f[64:65, 0:64], rlt[64:65, :], True, True, ['onesf', 'rlt'], [pxk])
                              V_(lambda e, px=px: e.tensor_copy(out=rbc[0:64, :], in_=px[0:64, :]), [pxk], ['rbc'])
                              out_fn(po, pok)
                          else:
                              V_(lambda e, pl=pl: e.reciprocal(out=rbc[:, :], in_=pl[:, :]), [plk], ['rbc'])
                              out_fn(po, pok)

                      for qb in range(NB + 1):
                          smp = (qb == NB)
                          if not smp:
                              dma('sync', QT[:, :, :], qt_d[:, :, qb * 512:(qb + 1) * 512].rearrange("c p t -> p c t"), ['qt_d%d' % qb], ['QT'])
                              nkt = 4 * qb + 4
                              if fox:
                                  for qt in range(4):
                                      tq = 4 * qb + qt
                                      for h in range(12):
                                          V_(lambda e, qt=qt, tq=tq, h=h: e.tensor_scalar(out=Bq[:, qt, h, 0:tq + 1], in0=negC[:, 0:tq + 1, h], scalar1=Cend[:, tq, h:h + 1], scalar2=None, op0=ALU.add), ['negC', 'Cend'], ['Bq'])
                                  for h in range(12):
                                      kp = slice((h % 2) * 64, (h % 2) * 64 + 64)

                                      def outf(po, pok, h=h):
                                          V_(lambda e: e.tensor_tensor(out=OT[0:64, h, :], in0=po[0:64, :], in1=rbc[0:64, :], op=ALU.mult), [pok, 'rbc'], ['OT'])
                                      attend(qb, KT, kp, h // 2, lambda kt, h=h: (VV[:, kt, h, :], 'VV%d' % (kt // 4)), nkt,
                                             lambda qt, kt, h=h: Bq[:, qt, h, kt:kt + 1], True, QT, h // 2, outf, True)
                              else:
                                  for hh in range(12):
                                      h, c = hh // 2, hh % 2
                                      kp = slice(c * 64, c * 64 + 64)

                                      def outf(po, pok, h=h, c=c):
                                          if c == 0:
                                              V_(lambda e: e.tensor_tensor(out=on0[:, :], in0=po[:, :], in1=rbc[:, :], op=ALU.mult), [pok, 'rbc'], ['on0'])
                                              return
                                          V_(lambda e: e.tensor_tensor(out=on1[:, :], in0=po[:, :], in1=rbc[:, :], op=ALU.mult), [pok, 'rbc'], ['on1'])
                                          V_(lambda e: e.scalar_tensor_tensor(out=on0[:, :], in0=on1[:, :], scalar=nlam[:, :], in1=on0[:, :], op0=ALU.mult, op1=ALU.add), ['on0', 'on1', 'nlam'], ['on0'])
                                          G_(lambda e: e.tensor_tensor(out=sq[:, :], in0=on0[:, :], in1=on0[:, :], op=ALU.mult), ['on0'], ['sq'])
                                          px, pxk = next_pab()
                                          mm(px[:, :], ones_b[:, :], sq[:, :], True, True, ['onesb', 'sq'], [pxk])
                                          A_(lambda e, px=px: e.activation(out=on1[:, :], in_=px[:, :], func=AF.Sqrt, scale=1.0 / 128, bias=eps_c[:, :]), [pxk, 'epsc'], ['on1'])
                                          V_(lambda e: e.reciprocal(out=on1[:, :], in_=on1[:, :]), ['on1'], ['on1'])
                                          V_(lambda e: e.scalar_tensor_tensor(out=OT[:, h, :], in0=on0[:, :], scalar=gsub_c[:, :], in1=on1[:, :], op0=ALU.mult, op1=ALU.mult), ['on0', 'on1', 'gsubc'], ['OT'])
                                      attend(qb, KT, kp, h, lambda kt, h=h: (VV[:, kt, h * 128:(h + 1) * 128], 'VV%d' % (kt // 4)), nkt,
                                             None, True, QT, h, outf, False)
                              for hm in range(4):
                                  kp = slice((hm % 2) * 64, (hm % 2) * 64 + 64)

                                  def outm(po, pok, hm=hm):
                                      V_(lambda e: e.tensor_tensor(out=OTm[0:64, hm, :], in0=po[0:64, :], in1=rbc[0:64, :], op=ALU.mult), [pok, 'rbc'], ['OTm'])
                                  attend(qb, MKT, kp, hm // 2, lambda kt, hm=hm: (MV[:, kt, hm, :], 'MV'), 2, None, False, QT, 6 + hm // 2, outm, True)
                          tiles = [(None, NS)] if smp else [(4 * qb + i, 128) for i in range(4)]
                          for ti, (t, r) in enumerate(tiles):
                              if smp:
                                  os_, om_ = OTs_self, OTs_mem; osk, omk = 'OTs_self', 'OTs_mem'; cs_ = slice(0, NS)
                                  src = hs_src; dst = hmid_d[S:S + NS, :]
                              else:
                                  os_, om_ = OT, OTm; osk, omk = 'OT', 'OTm'; cs_ = slice(ti * 128, (ti + 1) * 128)
                                  src = h_src[t * 128:(t + 1) * 128, :]; dst = hmid_d[t * 128:(t + 1) * 128, :]
                              dma(ldq(), hres[0:r, :], src, ['h1_d'], ['hres'])
                              for hf, (py, pyk) in enumerate(((pA, 'pA'), (pB, 'pB'))):
                                  for c in range(nsc):
                                      mm(py[0:r, :], os_[0:pk_, c, cs_], Wo[0:pk_, c, hf * 512:(hf + 1) * 512], c == 0, False, [osk, 'Wo'], [pyk])
                                  for c in range(4):
                                      mm(py[0:r, :], om_[0:64, c, cs_], Wom[0:64, c, hf * 512:(hf + 1) * 512], False, c == 3, [omk, 'Wom'], [pyk])
                              if True:
                                  st = None
                                  post_norm_residual(st, (pA, pB), ('pA', 'pB'), r, gp, 'gp', hres, 'hres', hout, 'hout')
                              dma('sync', dst, hout[0:r, :], ['hout0', 'hout1'], ['hmid_d'])

              stop_if('P2_%d' % L)
              sch.barrier()
              p3 = ExitStack()
              with p3:
                  Wg = sb(p3, [128, 8, 2 * DFF], BF16, "Wg"); Wd = sb(p3, [128, 22, D], BF16, "Wd")
                  load_w_bf16(Wg, 'Wg', w_gu[L], D, 2 * DFF, 8)
                  load_w_bf16(Wd, 'Wd', w_dn[L], DFF, D, 22)
                  gp = sb(p3, [128, D], F32, "gp3"); gc3 = sb(p3, [128, 8], F32, "gc3")
                  dma('sync', gp[:, :], gpost[L, 1:2, :].to_broadcast([128, D]), [], ['gp3'])
                  dma('sync', gc3[:, :], gcols[L, 1], [], ['gc3'])
                  xT = sb(p3, [128, 8, 512], BF16, "xT3"); fT = sb(p3, [128, 22, 512], BF16, "fT")
                  hm = [sb(p3, [128, D], F32, "hm%d" % i) for i in range(4)]
                  sg = sb(p3, [128, 512], F32, "sg"); hout = sb(p3, [128, D], F32, "hout3")
                  for qb in range(NB + 1):
                      smp = (qb == NB)
                      tiles = [(None, NS)] if smp else [(4 * qb + i, 128) for i in range(4)]
                      T = NS if smp else 512
                      for ti, (t, r) in enumerate(tiles):
                          src = hmid_d[S:S + NS, :] if smp else hmid_d[t * 128:(t + 1) * 128, :]
                          sch.op(ldq(), lambda e, ti=ti, r=r, src=src: e.dma_start(out=hm[ti][0:r, :], in_=src), ['hmid_d'], ['hm%d' % ti], dma=True)
                          norm_from(hm[ti], r, gc3, 'gc3', xT, 'xT3', ti * 128, 'hm%d' % ti)
                      for j in range(22):
                          for c in range(8):
                              mm(pA[:, 0:T], Wg[:, c, j * 128:(j + 1) * 128], xT[:, c, 0:T], c == 0, c == 7, ['Wg'] + xkeys('xT3'), ['pA'])
                          for c in range(8):
                              mm(pB[:, 0:T], Wg[:, c, DFF + j * 128:DFF + (j + 1) * 128], xT[:, c, 0:T], c == 0, c == 7, ['Wg'] + xkeys('xT3'), ['pB'])
                          A_(lambda e, T=T: e.activation(out=sg[:, 0:T], in_=pA[:, 0:T], func=AF.Silu), ['pA'], ['sg'])
                          V_(lambda e, j=j, T=T: e.tensor_tensor(out=fT[:, j, 0:T], in0=sg[:, 0:T], in1=pB[:, 0:T], op=ALU.mult), ['sg', 'pB'], ['fT'])
                      for ti, (t, r) in enumerate(tiles):
                          cs_ = slice(ti * 128, ti * 128 + r)
                          for hf, (py, pyk) in enumerate(((pO[0], 'pO0'), (pO[1], 'pO1'))):
                              for j in range(22):
                                  mm(py[0:r, :], fT[:, j, cs_], Wd[:, j, hf * 512:(hf + 1) * 512], j == 0, j == 21, ['fT', 'Wd'], [pyk])
                          if True:
                              st = None
                              post_norm_residual(st, (pO[0], pO[1]), ('pO0', 'pO1'), r, gp, 'gp3', hm[ti], 'hm%d' % ti, hout, 'hout3')
                          dst = hs_dst if smp else h_dst[t * 128:(t + 1) * 128, :]
                          dma('sync', dst, hout[0:r, :], ['hout30', 'hout31'], ['h1_d'])
              sch.barrier()

          except _Stop:
            break
        sch.emit()
    except AssertionError:
        if not KSTOP:
            raise
    return nc


_CACHE = {}


def _consts(S, NPG):
    c = {}
    c["c_ident"] = np.eye(128, dtype=np.float32)
    pp = np.arange(128)
    c["c_tri"] = (pp[:, None] <= pp[None, :]).astype(np.float32)
    c["c_suf"] = (pp[:, None] > pp[None, :]).astype(np.float32)
    c["c_maskT"] = (pp[None, :] >= pp[:, None]).astype(np.float32)
    c["c_iota"] = pp.astype(np.float32)[:, None]
    inv = (np.float32(10000.0) ** (-np.arange(0, 64, 2, dtype=np.float32) / np.float32(64))).astype(np.float32)
    pos = np.arange(S, dtype=np.float32)
    ang = (pos[:, None] * inv[None, :]).astype(np.float32)
    c["c_cos"] = np.ascontiguousarray(np.tile(np.cos(ang).astype(np.float32), (1, 12)))
    c["c_sin"] = np.ascontiguousarray(np.tile(np.sin(ang).astype(np.float32), (1, 12)))
    angs = (np.full((NS, 1), NPG * 128, dtype=np.float32) * inv[None, :]).astype(np.float32)
    c["c_cos_s"] = np.ascontiguousarray(np.tile(np.cos(angs).astype(np.float32), (1, 12)))
    c["c_sin_s"] = np.ascontiguousarray(np.tile(np.sin(angs).astype(np.float32), (1, 12)))
    sbias = np.full((128, 12), NEG, dtype=np.float32); sbias[0, :] = 0.0
    c["c_selfb"] = sbias
    bmf = np.zeros((12, 12, 64), np.float32)
    for h in range(12):
        bmf[h, h, :] = 1.0
    c["c_bm_fox"] = bmf.reshape(12, SW)
    bmd = np.zeros((12, 6, 128), np.float32)
    for h in range(12):
        bmd[h, h // 2, :] = 1.0
    c["c_bm_diff"] = bmd.reshape(12, SW)
    return c


def kernel(x_prompt, x_sample, cache_fox_k, cache_fox_v, cache_fox_logf, cache_diff_k, cache_diff_v,
           cache_mem_k, cache_mem_v, page_table, mem_prompt, w_in_fox, b_f_fox, w_in_diff,
           lam_q1, lam_k1, lam_q2, lam_k2, g_subln, g_pre_mix, g_post_mix, g_pre_ffn, g_post_ffn,
           g_mem, w_mem_kv, w_out, w_gate_up, w_down):
    f = lambda a: np.ascontiguousarray(np.asarray(a, dtype=np.float32))
    x_prompt = f(x_prompt); x_sample = f(x_sample)
    B, S, _ = x_prompt.shape
    DB = x_sample.shape[0]
    NPOOL = cache_fox_k.shape[1]
    page_table = np.ascontiguousarray(np.asarray(page_table, dtype=np.int32))
    NPG = page_table.shape[1]
    key = (S, NPG, NPOOL)
    if key not in _CACHE:
        _CACHE[key] = build(S, NPG, NPOOL)
    nc = _CACHE[key]
    cst = _consts(S, NPG)
    cfk = f(cache_fox_k).reshape(NPOOL * 128, SW); cfv = f(cache_fox_v).reshape(NPOOL * 128, SW)
    cfl = f(cache_fox_logf).reshape(NPOOL * 128, 12)
    cdk = f(cache_diff_k).reshape(NPOOL * 128, SW); cdv = f(cache_diff_v).reshape(NPOOL * 128, SW)
    cmk = f(cache_mem_k).reshape(2, DB, 256, 256); cmv = f(cache_mem_v).reshape(2, DB, 256, 256)
    gl = [f(g_pre_mix), f(g_pre_ffn), f(g_mem)]
    gcols = np.zeros((2, 3, 128, 8), np.float32)
    for L in range(2):
        for j in range(3):
            gcols[L, j] = gl[j][L].reshape(8, 128).T
    gpost = np.ascontiguousarray(np.stack([f(g_post_mix), f(g_post_ffn)], axis=1))
    lamv = np.ascontiguousarray(np.stack([f(lam_q1)[0], f(lam_k1)[0], f(lam_q2)[0], f(lam_k2)[0]], axis=0))
    shared = dict(cfk=cfk, cfv=cfv, cfl=cfl, cdk=cdk, cdv=cdv,
                  w_in_fox=f(w_in_fox)[0], w_in_diff=f(w_in_diff)[0], b_f=f(b_f_fox).reshape(1, 12), lamv=lamv,
                  gsub=f(g_subln)[0].reshape(128, 1), gcols=gcols, gpost=gpost,
                  w_mem=f(w_mem_kv), w_out=f(w_out), w_gu=f(w_gate_up), w_dn=f(w_down))
    shared.update(cst)
    in_maps = []
    for c in range(NCORES):
        b = c % B
        m = dict(shared)
        m["xp"] = x_prompt[b]
        m["xs"] = np.ascontiguousarray(x_sample[NS * c:NS * (c + 1), 0, :])
        m["cmk"] = np.ascontiguousarray(cmk[:, NS * c:NS * (c + 1)])
        m["cmv"] = np.ascontiguousarray(cmv[:, NS * c:NS * (c + 1)])
        m["pt"] = np.ascontiguousarray(page_table[NS * c:NS * (c + 1)])
        m["memp"] = f(mem_prompt)[b]
        in_maps.append(m)
    res = run_bass_kernel_spmd(nc, in_maps, core_ids=list(range(NCORES)))
    R = res.results
    g = lambda name, cores: np.stack([np.asarray(R[c][name], dtype=np.float32) for c in cores], axis=0)
    pc = list(range(B)); ac = list(range(NCORES))
    y_p = g("o_yp", pc)
    y_s = g("o_ys", ac).reshape(DB, 1, D)
    fk_p = g("o_fkp", pc).reshape(1, B, S, 12, 64); fv_p = g("o_fvp", pc).reshape(1, B, S, 12, 64)
    fl_p = g("o_flp", pc).reshape(1, B, S, 12)
    fk_s = g("o_fks", ac).reshape(1, DB, 1, 12, 64); fv_s = g("o_fvs", ac).reshape(1, DB, 1, 12, 64)
    fl_s = g("o_fls", ac).reshape(1, DB, 1, 12)
    dk_p = g("o_dkp", pc).reshape(1, B, S, 6, 128); dv_p = g("o_dvp", pc).reshape(1, B, S, 6, 128)
    dk_s = g("o_dks", ac).reshape(1, DB, 1, 6, 128); dv_s = g("o_dvs", ac).reshape(1, DB, 1, 6, 128)
    mk = np.ascontiguousarray(g("o_mk", pc).transpose(1, 0, 2, 3)).reshape(2, B, 256, 4, 64)
    mv = np.ascontiguousarray(g("o_mv", pc).transpose(1, 0, 2, 3)).reshape(2, B, 256, 4, 64)
    return (y_p, y_s, fk_p, fv_p, fl_p, fk_s, fv_s, fl_s, dk_p, dv_p, dk_s, dv_s, mk, mv)
```

```python
import math
import os
from contextlib import ExitStack
import numpy as np
import concourse.bass as bass
import concourse.mybir as mybir
from concourse.bass_utils import run_bass_kernel_spmd

F32 = mybir.dt.float32
BF16 = mybir.dt.bfloat16
I32 = mybir.dt.int32
AF = mybir.ActivationFunctionType
ALU = mybir.AluOpType
AX = mybir.AxisListType

D = 1024
NCORES = 8
SW = 768
DFF = 2816
EPS = 1e-6
NEG = -1e30
NS = 4
KSTOP = os.environ.get('KSTOP', '')


class _Stop(Exception):
    pass

SAME_ENG_SYNC = ('noself' not in os.environ.get('KVAR', ''))


class _Rec:
    def __getattr__(self, name):
        def f(*a, **k):
            self.__dict__['call'] = (name, a, k)
            return self
        return f


class Sched:
    ENG = ['tensor', 'vector', 'scalar', 'gpsimd', 'sync']
    DMAQ = ['sync', 'gpsimd', 'scalar']
    PSUM_KEYS = frozenset(['pT', 'pA', 'pB', 'pS0', 'pS1', 'pO0', 'pO1', 'pO2'])

    def __init__(self, nc, stack, nslots=12):
        self.nc = nc
        self.prog = {e: [] for e in self.ENG}
        self.sem = {e: stack.enter_context(nc.semaphore("s_" + e)) for e in self.ENG[:4]}
        self.cnt = {e: 0 for e in self.ENG}
        self.ns = nslots
        self.dsl = {q: [stack.enter_context(nc.semaphore("d_%s_%d" % (q, i))) for i in range(nslots)]
                    for q in self.DMAQ}
        self.dn = {q: 0 for q in self.DMAQ}
        self.waited = {}
        self.lastw = {}
        self.readers = {}

    def op(self, eng, fn, reads=(), writes=(), dma=False):
        deps = []
        for k in reads:
            if k in self.lastw:
                deps.append(self.lastw[k])
            if k in self.PSUM_KEYS:
                deps.extend(t for t in self.readers.get(k, ()) if t[3] != eng)
        wdeps = []
        for k in writes:
            if k in self.lastw:
                wdeps.append(self.lastw[k])
            wdeps.extend(self.readers.get(k, ()))
        deps.extend(wdeps)
        waits = []

        def need(semname, sem, val):
            key = (eng, semname)
            if self.waited.get(key, 0) >= val:
                return
            self.waited[key] = val
            waits.append((sem, val))

        for (semname, sem, val, peng, pdma) in deps:
            if peng == eng and not pdma:
                if eng == 'tensor' or not SAME_ENG_SYNC:
                    continue
            need(semname, sem, val)
        if dma:
            n = self.dn[eng]
            slot, rnd = n % self.ns, n // self.ns
            sem = self.dsl[eng][slot]
            semname = "d_%s_%d" % (eng, slot)
            if rnd > 0:
                need(semname, sem, 16 * rnd)
            val = 16 * (rnd + 1)
            inc = 16
            self.dn[eng] = n + 1
        else:
            sem = self.sem[eng]
            semname = "s_" + eng
            self.cnt[eng] += 1
            val = self.cnt[eng]
            inc = 1
        rec = _Rec()
        fn(rec)
        self.prog[eng].append((waits, rec.__dict__['call'], sem, inc))
        tok = (semname, sem, val, eng, dma)
        for k in writes:
            self.lastw[k] = tok
            self.readers[k] = []
        for k in reads:
            if k not in writes:
                self.readers.setdefault(k, []).append(tok)

    def barrier(self):
        for e in self.ENG:
            waits = []
            for p in self.ENG[:4]:
                v = self.cnt[p]
                if v > self.waited.get((e, 's_' + p), 0):
                    self.waited[(e, 's_' + p)] = v
                    waits.append((self.sem[p], v))
            for q in self.DMAQ:
                n = self.dn[q]
                for slot in range(self.ns):
                    c = n // self.ns + (1 if slot < n % self.ns else 0)
                    nm = "d_%s_%d" % (q, slot)
                    if c > 0 and 16 * c > self.waited.get((e, nm), 0):
                        self.waited[(e, nm)] = 16 * c
                        waits.append((self.dsl[q][slot], 16 * c))
            self.prog[e].append((waits, None, None, 0))

    def emit(self):
        fin = {}
        for q in self.DMAQ:
            lst = []
            for slot in range(self.ns):
                n = self.dn[q]
                cntslot = n // self.ns + (1 if slot < n % self.ns else 0)
                if cntslot > 0:
                    lst.append((self.dsl[q][slot], 16 * cntslot))
            fin[q] = lst
        with self.nc.Block() as block:
            for e in self.ENG:
                def body(eng, e=e):
                    for waits, fn, sem, inc in self.prog[e]:
                        for (s, v) in waits:
                            eng.wait_ge(s, v)
                        if fn is not None:
                            name, a, k = fn
                            getattr(eng, name)(*a, **k).then_inc(sem, inc)
                    for (s, v) in fin.get(e, ()):
                        eng.wait_ge(s, v)
                getattr(block, e)(body)


def build(S, NPG, NPOOL):
    NT = S // 128
    NB = S // 512
    nc = bass.Bass("TRN2", target_bir_lowering=False)

    def din(name, shape, dt=F32):
        return nc.dram_tensor(name, list(shape), dt, kind="ExternalInput").ap()

    def dout(name, shape, dt=F32):
        return nc.dram_tensor(name, list(shape), dt, kind="ExternalOutput").ap()

    def dscr(name, shape, dt=F32):
        return nc.dram_tensor(name, list(shape), dt, kind="Internal").ap()

    xp = din("xp", [S, D]); xs = din("xs", [NS, D])
    cfk = din("cfk", [NPOOL * 128, SW]); cfv = din("cfv", [NPOOL * 128, SW]); cfl = din("cfl", [NPOOL * 128, 12])
    cdk = din("cdk", [NPOOL * 128, SW]); cdv = din("cdv", [NPOOL * 128, SW])
    cmk = din("cmk", [2, NS, 256, 256]); cmv = din("cmv", [2, NS, 256, 256])
    pt = din("pt", [NS, NPG], I32)
    memp = din("memp", [256, D])
    w_in = [din("w_in_fox", [D, 2572]), din("w_in_diff", [D, 2560])]
    b_f = din("b_f", [1, 12])
    lamv = din("lamv", [4, 64])
    gsub = din("gsub", [128, 1])
    gcols = din("gcols", [2, 3, 128, 8])
    gpost = din("gpost", [2, 2, D])
    w_mem = din("w_mem", [2, D, 512]); w_out = din("w_out", [2, D, D])
    w_gu = din("w_gu", [2, D, 2 * DFF]); w_dn = din("w_dn", [2, DFF, D])
    c_ident = din("c_ident", [128, 128]); c_tri = din("c_tri", [128, 128]); c_suf = din("c_suf", [128, 128])
    c_maskT = din("c_maskT", [128, 128]); c_iota = din("c_iota", [128, 1])
    c_cos = din("c_cos", [S, 384]); c_sin = din("c_sin", [S, 384])
    c_cos_s = din("c_cos_s", [NS, 384]); c_sin_s = din("c_sin_s", [NS, 384])
    c_selfb = din("c_selfb", [128, 12])
    c_bm_fox = din("c_bm_fox", [12, SW]); c_bm_diff = din("c_bm_diff", [12, SW])

    o_yp = dout("o_yp", [S, D]); o_ys = dout("o_ys", [NS, D])
    o_kp = [dout("o_fkp", [S, SW]), dout("o_dkp", [S, SW])]
    o_vp = [dout("o_fvp", [S, SW]), dout("o_dvp", [S, SW])]
    o_ks = [dout("o_fks", [NS, SW]), dout("o_dks", [NS, SW])]
    o_vs = [dout("o_fvs", [NS, SW]), dout("o_dvs", [NS, SW])]
    o_flp = dout("o_flp", [S, 12]); o_fls = dout("o_fls", [NS, 12])
    o_mk = dout("o_mk", [2, 256, 256]); o_mv = dout("o_mv", [2, 256, 256])

    hmid_d = dscr("hmid_d", [S + NS, D]); h1_d = dscr("h1_d", [S + NS, D])
    qt_d = dscr("qt_d", [8, 128, S], BF16)
    zs_d = dscr("zs_d", [NS, 2572])

    top = ExitStack()
    try:
      with top:
        sch = Sched(nc, top)
        uid = [0]

        def sb(stack, shape, dt=F32, name=None):
            uid[0] += 1
            return stack.enter_context(nc.sbuf_tensor("%s_%d" % (name or "t", uid[0]), list(shape), dt))

        def ps(stack, shape, dt=F32, name=None):
            uid[0] += 1
            return stack.enter_context(nc.psum_tensor("%s_%d" % (name or "p", uid[0]), list(shape), dt))

        def dma(q, out, in_, reads, writes):
            sch.op(q, lambda e: e.dma_start(out=out, in_=in_), reads, writes, dma=True)

        def V_(fn, reads, writes):
            sch.op('vector', fn, reads, writes)

        def A_(fn, reads, writes):
            sch.op('scalar', fn, reads, writes)

        def G_(fn, reads, writes):
            sch.op('gpsimd', fn, reads, writes)

        def T_(fn, reads, writes):
            sch.op('tensor', fn, reads, writes)

        def mm(out, lhsT, rhs, start, stop, reads, writes):
            T_(lambda e: e.matmul(out, lhsT, rhs, start=start, stop=stop, skip_group_check=True), reads, writes)

        pT = ps(top, [128, 1024], BF16, "pT")
        pA = ps(top, [128, 512], F32, "pA"); pB = ps(top, [128, 512], F32, "pB")
        pS = [ps(top, [128, 512], F32, "pS%d" % i) for i in range(2)]
        pO = [ps(top, [128, 512], F32, "pO%d" % i) for i in range(3)]
        pT3 = pT[:].rearrange("p (c t) -> p c t", c=8)

        ident_f = sb(top, [128, 128], F32, "identf"); ident_b = sb(top, [128, 128], BF16, "identb")
        tri_f = sb(top, [128, 128], F32, "tri"); suf_f = sb(top, [128, 128], F32, "suf")
        ones_f = sb(top, [128, 128], F32, "onesf"); ones_b = sb(top, [128, 128], BF16, "onesb")
        maskT = sb(top, [128, 128], BF16, "maskT"); iota_c = sb(top, [128, 1], F32, "iota")
        selfb = sb(top, [128, 12], F32, "selfb")
        bm = [sb(top, [12, SW], F32, "bmf"), sb(top, [12, SW], F32, "bmd")]
        eps_c = sb(top, [128, 1], F32, "epsc")
        nlam = sb(top, [128, 1], F32, "nlam")
        gsub_c = sb(top, [128, 1], F32, "gsubc")
        bf_bc = sb(top, [128, 12], F32, "bfbc")
        OTs_self = sb(top, [128, 12, NS], BF16, "OTs_self")
        OTs_mem = sb(top, [64, 4, NS], BF16, "OTs_mem")

        dma('sync', ident_f[:], c_ident, [], ['identf'])
        if 'nocast' in os.environ.get('KVAR', ''):
            V_(lambda e: e.tensor_copy(out=ident_b[:], in_=ident_f[:]), ['identf'], ['identb'])
        else:
            dma('gpsimd', ident_b[:], c_ident, [], ['identb'])
        dma('sync', tri_f[:], c_tri, [], ['tri'])
        dma('sync', suf_f[:], c_suf, [], ['suf'])
        if 'nocast' in os.environ.get('KVAR', ''):
            dma('sync', tri_f[:], c_maskT, [], ['tri'])
            V_(lambda e: e.tensor_copy(out=maskT[:], in_=tri_f[:]), ['tri'], ['maskT'])
        else:
            dma('gpsimd', maskT[:], c_maskT, [], ['maskT'])
        dma('sync', iota_c[:], c_iota, [], ['iota'])
        dma('sync', selfb[:], c_selfb, [], ['selfb'])
        dma('sync', bm[0][:], c_bm_fox, [], ['bm0'])
        dma('sync', bm[1][:], c_bm_diff, [], ['bm1'])
        dma('sync', gsub_c[:], gsub, [], ['gsubc'])
        dma('sync', bf_bc[:], b_f.to_broadcast([128, 12]), [], ['bfbc'])
        V_(lambda e: e.memset(ones_f[:], 1.0), [], ['onesf'])
        V_(lambda e: e.memset(ones_b[:], 1.0), [], ['onesb'])
        V_(lambda e: e.memset(eps_c[:], EPS), [], ['epsc'])

        lam_init = 0.8 - 0.6 * math.exp(-0.3 * 1)
        if True:
            st = top
            lv = sb(st, [128, 4, 64], F32, "lv")
            for j in range(4):
                dma('sync', lv[:, j, :], lamv[j:j + 1, :].to_broadcast([128, 64]), [], ['lv%d' % j])
            pr = sb(st, [128, 2, 64], F32, "lpr"); sm = sb(st, [128, 2], F32, "lsm"); ex = sb(st, [128, 2], F32, "lex")
            V_(lambda e: e.tensor_tensor(out=pr[:, 0, :], in0=lv[:, 0, :], in1=lv[:, 1, :], op=ALU.mult), ['lv0', 'lv1'], ['lpr0'])
            V_(lambda e: e.tensor_tensor(out=pr[:, 1, :], in0=lv[:, 2, :], in1=lv[:, 3, :], op=ALU.mult), ['lv2', 'lv3'], ['lpr1'])
            V_(lambda e: e.tensor_reduce(out=sm[:], in_=pr[:], axis=AX.X, op=ALU.add), ['lpr0', 'lpr1'], ['lsm'])
            A_(lambda e: e.activation(out=ex[:], in_=sm[:], func=AF.Exp), ['lsm'], ['lex'])
            V_(lambda e: e.tensor_tensor(out=nlam[:], in0=ex[:, 1:2], in1=ex[:, 0:1], op=ALU.subtract), ['lex'], ['nlam'])
            V_(lambda e: e.tensor_scalar(out=nlam[:], in0=nlam[:], scalar1=-lam_init, scalar2=None, op0=ALU.add), ['nlam'], ['nlam'])
            V_(lambda e: e.tensor_scalar(out=gsub_c[:], in0=gsub_c[:], scalar1=(1.0 - lam_init), scalar2=None, op0=ALU.mult), ['gsubc'], ['gsubc'])

        cnt = {'ev': 0, 'pab': 0, 'q': 0}

        def ldq():
            cnt['q'] += 1
            return 'sync'

        n_junk = sb(top, [128, D], BF16, "njunk"); n_ss = sb(top, [128, 1], F32, "nss")
        n_rstd = sb(top, [128, 1], F32, "nrstd"); n_xn = sb(top, [128, D], BF16, "nxn")
        PTK = ['pT'] * 8

        def norm_from(st_h, r, gcol, gkey, xnT, xkey, tok0, hkey):
            junk, ss, rstd, xn = n_junk, n_ss, n_rstd, n_xn
            if 'noaccum' not in os.environ.get('KVAR', ''):
                A_(lambda e: e.activation(out=junk[0:r, :], in_=st_h[0:r, :], func=AF.Square, accum_out=ss[0:r, :]), [hkey], ['nss', 'njunk'])
            else:
                A_(lambda e: e.activation(out=junk[0:r, :], in_=st_h[0:r, :], func=AF.Square), [hkey], ['njunk'])
                V_(lambda e: e.tensor_reduce(out=ss[0:r, :], in_=junk[0:r, :], axis=AX.X, op=ALU.add), ['njunk'], ['nss'])
            V_(lambda e: e.tensor_scalar(out=ss[0:r, :], in0=ss[0:r, :], scalar1=1.0 / D, scalar2=EPS, op0=ALU.mult, op1=ALU.add), ['nss'], ['nss'])
            A_(lambda e: e.activation(out=ss[0:r, :], in_=ss[0:r, :], func=AF.Sqrt), ['nss'], ['nss'])
            V_(lambda e: e.reciprocal(out=rstd[0:r, :], in_=ss[0:r, :]), ['nss'], ['nrstd'])
            V_(lambda e: e.tensor_scalar(out=xn[0:r, :], in0=st_h[0:r, :], scalar1=rstd[0:r, :], scalar2=None, op0=ALU.mult), [hkey, 'nrstd'], ['nxn'])
            for c in range(8):
                T_(lambda e, c=c: e.transpose(pT3[:, c, 0:r], xn[0:r, c * 128:(c + 1) * 128], ident_b[0:r, 0:r]),
                   ['nxn', 'identb'], [PTK[c]])
            for c in range(8):
                if False:
                    A_(lambda e, c=c: e.activation(out=xnT[:, c, tok0:tok0 + r], in_=pT3[:, c, 0:r], func=AF.Identity, scale=gcol[:, c:c + 1]),
                       [PTK[c], gkey], [xkey + '_%d' % c])
                else:
                    V_(lambda e, c=c: e.tensor_scalar(out=xnT[:, c, tok0:tok0 + r], in0=pT3[:, c, 0:r], scalar1=gcol[:, c:c + 1], scalar2=None, op0=ALU.mult),
                       [PTK[c], gkey], [xkey + '_%d' % c])

        def norm_T(st_h, src_ap, r, gcol, gkey, xnT, xkey, tok0, hkey, srckeys=()):
            dma(ldq(), st_h[0:r, :], src_ap, list(srckeys), [hkey])
            norm_from(st_h, r, gcol, gkey, xnT, xkey, tok0, hkey)

        def xkeys(xkey):
            return [xkey + '_%d' % c for c in range(8)]

        def next_pab():
            cnt['pab'] += 1
            return (pA, 'pA') if cnt['pab'] % 2 else (pB, 'pB')

        def gemm_tm(xnT, xkey, tok0, r, W, wkey, col0, n, pt_, pkey):
            for c in range(8):
                mm(pt_[0:r, 0:n], xnT[:, c, tok0:tok0 + r], W[:, c, col0:col0 + n], c == 0, c == 7,
                   xkeys(xkey) + [wkey], [pkey])

        def load_w_bf16(W, wkey, src, rows_total, ncols, kchunks, prow=128):
            for c in range(kchunks):
                dma('gpsimd', W[0:prow, c, 0:ncols], src[c * prow:(c + 1) * prow, 0:ncols], [], [wkey])

        pn_ss = sb(top, [128, 2], F32, "pss"); pn_junk = sb(top, [128, 512], BF16, "pj"); pn_rstd = sb(top, [128, 1], F32, "prs")

        def post_norm_residual(st, y_halves, ykeys, r, gp, gpkey, hres, hkey, outt, okey):
            ss, junk, rstd = pn_ss, pn_junk, pn_rstd
            k = 'pn'
            for hf in range(2):
                A_(lambda e, hf=hf: e.activation(out=junk[0:r, :], in_=y_halves[hf][0:r, :], func=AF.Square, accum_out=ss[0:r, hf:hf + 1]),
                   [ykeys[hf]], [k + 'ss%d' % hf, k + 'j'])
            V_(lambda e: e.tensor_tensor(out=rstd[0:r, :], in0=ss[0:r, 0:1], in1=ss[0:r, 1:2], op=ALU.add), [k + 'ss0', k + 'ss1'], [k + 'r'])
            V_(lambda e: e.tensor_scalar(out=rstd[0:r, :], in0=rstd[0:r, :], scalar1=1.0 / D, scalar2=EPS, op0=ALU.mult, op1=ALU.add), [k + 'r'], [k + 'r'])
            A_(lambda e: e.activation(out=rstd[0:r, :], in_=rstd[0:r, :], func=AF.Sqrt), [k + 'r'], [k + 'r'])
            V_(lambda e: e.reciprocal(out=rstd[0:r, :], in_=rstd[0:r, :]), [k + 'r'], [k + 'r'])
            for hf in range(2):
                sl = slice(hf * 512, (hf + 1) * 512)
                V_(lambda e, hf=hf, sl=sl: e.scalar_tensor_tensor(out=outt[0:r, sl], in0=y_halves[hf][0:r, :], scalar=rstd[0:r, :], in1=gp[0:r, sl], op0=ALU.mult, op1=ALU.mult),
                   [ykeys[hf], k + 'r', gpkey], [okey + '%d' % hf])
                G_(lambda e, sl=sl: e.tensor_tensor(out=outt[0:r, sl], in0=outt[0:r, sl], in1=hres[0:r, sl], op=ALU.add),
                   [okey + '%d' % hf, hkey], [okey + '%d' % hf])

        def stop_if(tag):
            if KSTOP == tag:
                raise _Stop()

        for L in (range(2) if not KSTOP.startswith('pre') else []):
          try:
              fox = (L == 0)
              NIN = 2572 if fox else 2560
              QMC = 2316 if fox else 2304
              h_src = xp if L == 0 else h1_d[0:S, :]
              hs_src = xs if L == 0 else h1_d[S:S + NS, :]
              h_dst = h1_d[0:S, :] if L == 0 else o_yp
              hs_dst = h1_d[S:S + NS, :] if L == 0 else o_ys
              ck, cv = (cfk, cfv) if fox else (cdk, cdv)
              lay = ExitStack()
              with lay:
                  KT = sb(lay, [128, 6, S], BF16, "KT")
                  if fox:
                      VV = sb(lay, [128, NT, 12, 65], BF16, "VV")
                      G_(lambda e: e.memset(VV[:, :, :, 64:65], 1.0), [], ['VVones'])
                      negC = sb(lay, [128, NT, 12], F32, "negC"); Cend = sb(lay, [128, NT, 12], F32, "Cend")
                      Rrun = sb(lay, [128, 12], F32, "Rrun")
                      V_(lambda e: e.memset(Rrun[:], 0.0), [], ['Rrun'])
                  else:
                      VV = sb(lay, [128, NT, SW], BF16, "VV")
                  MKT = sb(lay, [128, 2, 256], BF16, "MKT"); MV = sb(lay, [128, 2, 4, 65], BF16, "MV")
                  G_(lambda e: e.memset(MV[:, :, :, 64:65], 1.0), [], ['MVones'])
                  gc = sb(lay, [128, 3, 8], F32, "gc")
                  for j in range(3):
                      dma('sync', gc[:, j, :], gcols[L, j], [], ['gc%d' % j])

                  p1 = ExitStack()
                  with p1:
                      Win = sb(p1, [128, 8, NIN], BF16, "Win")
                      load_w_bf16(Win, 'Win', w_in[L], D, NIN, 8)
                      Wm = sb(p1, [128, 8, 512], BF16, "Wm")
                      load_w_bf16(Wm, 'Wm', w_mem[L], D, 512, 8)
                      xnT = sb(p1, [128, 8, 128], BF16, "xnT")
                      ht = sb(p1, [128, D], F32, "ht")
                      zq = sb(p1, [128, SW], F32, "zq"); zk = sb(p1, [128, SW], F32, "zk"); zv = sb(p1, [128, SW], F32, "zv")
                      zr = sb(p1, [128, 268], F32, "zr")
                      qb_ = sb(p1, [128, SW], BF16, "qb"); kb_ = sb(p1, [128, SW], BF16, "kb"); mb_ = sb(p1, [128, 256], BF16, "mb")
                      qT = sb(p1, [128, 8, 128], BF16, "qTt")
                      cs = sb(p1, [128, 384], F32, "cs"); sn = sb(p1, [128, 384], F32, "sn")
                      t1 = sb(p1, [128, 384], F32, "t1"); t2 = sb(p1, [128, 384], F32, "t2")
                      lf = sb(p1, [128, 12], F32, "lf"); la = sb(p1, [128, 12], F32, "la"); lb = sb(p1, [128, 12], F32, "lb")

                      stop_if('P1w')
                      for mt in range(2):
                          norm_T(ht, memp[mt * 128:(mt + 1) * 128, :], 128, gc[:, 2, :], 'gc2', xnT, 'xnT', 0, 'ht')
                          stop_if('P1n')
                          gemm_tm(xnT, 'xnT', 0, 128, Wm, 'Wm', 0, 512, pA, 'pA')
                          stop_if('P1g')
                          V_(lambda e: e.tensor_copy(out=zq[:, 0:512], in_=pA[:, :]), ['pA'], ['zq'])
                          dma('sync', o_mk[L, mt * 128:(mt + 1) * 128, :], zq[:, 0:256], ['zq'], [])
                          dma('sync', o_mv[L, mt * 128:(mt + 1) * 128, :], zq[:, 256:512], ['zq'], [])
                          stop_if('P1o')
                          if os.environ.get('KACT') == 'dve':
                              V_(lambda e: e.tensor_copy(out=mb_[:, :], in_=pA[:, 0:256]), ['pA'], ['mb'])
                              V_(lambda e, mt=mt: e.tensor_copy(out=MV[:, mt, :, 0:64], in_=pA[:, 256:512].rearrange("p (h d) -> p h d", h=4)), ['pA', 'MVones'], ['MV'])
                          elif os.environ.get('KACT') == 'zq':
                              A_(lambda e: e.activation(out=mb_[:, :], in_=zq[:, 0:256], func=AF.Identity), ['zq'], ['mb'])
                              A_(lambda e, mt=mt: e.activation(out=MV[:, mt, :, 0:64], in_=zq[:, 256:512].rearrange("p (h d) -> p h d", h=4), func=AF.Identity), ['zq', 'MVones'], ['MV'])
                          else:
                              A_(lambda e: e.activation(out=mb_[:, :], in_=pA[:, 0:256], func=AF.Identity), ['pA'], ['mb'])
                              stop_if('P1v1')
                              A_(lambda e, mt=mt: e.activation(out=MV[:, mt, :, 0:64], in_=pA[:, 256:512].rearrange("p (h d) -> p h d", h=4), func=AF.Identity), ['pA', 'MVones'], ['MV'])
                          stop_if('P1v')
                          for g in range(2):
                              T_(lambda e, g=g: e.transpose(pT3[:, g, :], mb_[:, g * 128:(g + 1) * 128], ident_b[:, :]), ['mb', 'identb'], [PTK[g]])
                          V_(lambda e, mt=mt: e.tensor_copy(out=MKT[:, :, mt * 128:(mt + 1) * 128], in_=pT3[:, 0:2, :]), PTK[0:2], ['MKT'])

                      stop_if('P1m')
                      for t in range(NT + 1):
                          smp = (t == NT)
                          r = NS if smp else 128
                          src = hs_src if smp else h_src[t * 128:(t + 1) * 128, :]
                          norm_T(ht, src, r, gc[:, 0, :], 'gc0', xnT, 'xnT', 0, 'ht', ['h1_d'])
                          if not fox:
                              dma(ldq(), cs[0:r, :], (c_cos_s if smp else c_cos[t * 128:(t + 1) * 128, :]), [], ['cs'])
                              dma(ldq(), sn[0:r, :], (c_sin_s if smp else c_sin[t * 128:(t + 1) * 128, :]), [], ['sn'])
                          for which in range(3):
                              zt = (zq, zk, zv)[which]; zkey = ('zq', 'zk', 'zv')[which]
                              for (c0, n) in ((0, 512), (512, 256)):
                                  pp, pk = next_pab()
                                  gemm_tm(xnT, 'xnT', 0, r, Win, 'Win', which * SW + c0, n, pp, pk)
                                  if fox or which == 2:
                                      A_(lambda e, pp=pp, zt=zt, c0=c0, n=n: e.activation(out=zt[0:r, c0:c0 + n], in_=pp[0:r, 0:n], func=AF.Identity), [pk], [zkey])
                                  else:
                                      nh = n // 64; h0 = c0 // 64
                                      x = pp[0:r, 0:n].rearrange("p (h two d) -> p h two d", two=2, d=32)
                                      o = zt[0:r, c0:c0 + n].rearrange("p (h two d) -> p h two d", two=2, d=32)
                                      cc = cs[0:r, h0 * 32:(h0 + nh) * 32].rearrange("p (h d) -> p h d", d=32)
                                      sc = sn[0:r, h0 * 32:(h0 + nh) * 32].rearrange("p (h d) -> p h d", d=32)
                                      a1 = t1[0:r, 0:nh * 32].rearrange("p (h d) -> p h d", d=32)
                                      a2 = t2[0:r, 0:nh * 32].rearrange("p (h d) -> p h d", d=32)
                                      V_(lambda e, x=x, cc=cc, a1=a1: e.tensor_tensor(out=a1, in0=x[:, :, 0, :], in1=cc, op=ALU.mult), [pk, 'cs'], ['t1'])
                                      V_(lambda e, x=x, sc=sc, a2=a2: e.tensor_tensor(out=a2, in0=x[:, :, 1, :], in1=sc, op=ALU.mult), [pk, 'sn'], ['t2'])
                                      V_(lambda e, o=o, a1=a1, a2=a2: e.tensor_tensor(out=o[:, :, 0, :], in0=a1, in1=a2, op=ALU.subtract), ['t1', 't2'], [zkey])
                                      V_(lambda e, x=x, sc=sc, a1=a1: e.tensor_tensor(out=a1, in0=x[:, :, 0, :], in1=sc, op=ALU.mult), [pk, 'sn'], ['t1'])
                                      V_(lambda e, x=x, cc=cc, a2=a2: e.tensor_tensor(out=a2, in0=x[:, :, 1, :], in1=cc, op=ALU.mult), [pk, 'cs'], ['t2'])
                                      V_(lambda e, o=o, a1=a1, a2=a2: e.tensor_tensor(out=o[:, :, 1, :], in0=a1, in1=a2, op=ALU.add), ['t1', 't2'], [zkey])
                          nrest = NIN - 3 * SW
                          pp, pk = next_pab()
                          gemm_tm(xnT, 'xnT', 0, r, Win, 'Win', 3 * SW, nrest, pp, pk)
                          A_(lambda e, pp=pp: e.activation(out=zr[0:r, 0:nrest], in_=pp[0:r, 0:nrest], func=AF.Identity), [pk], ['zr'])
                          if fox:
                              V_(lambda e: e.tensor_tensor(out=la[0:r, :], in0=zr[0:r, 0:12], in1=bf_bc[0:r, :], op=ALU.add), ['zr', 'bfbc'], ['la'])
                              V_(lambda e: e.tensor_scalar(out=lb[0:r, :], in0=la[0:r, :], scalar1=-1.0, scalar2=None, op0=ALU.mult), ['la'], ['lb'])
                              V_(lambda e: e.tensor_tensor(out=lb[0:r, :], in0=lb[0:r, :], in1=la[0:r, :], op=ALU.min), ['la', 'lb'], ['lb'])
                              A_(lambda e: e.activation(out=lb[0:r, :], in_=lb[0:r, :], func=AF.Exp), ['lb'], ['lb'])
                              A_(lambda e: e.activation(out=lb[0:r, :], in_=lb[0:r, :], func=AF.Ln, bias=1.0), ['lb'], ['lb'])
                              V_(lambda e: e.tensor_scalar(out=la[0:r, :], in0=la[0:r, :], scalar1=0.0, scalar2=None, op0=ALU.min), ['la'], ['la'])
                              V_(lambda e: e.tensor_tensor(out=lf[0:r, :], in0=la[0:r, :], in1=lb[0:r, :], op=ALU.subtract), ['la', 'lb'], ['lf'])
                          kcol = 0
                          if smp:
                              dma('sync', zs_d[:, 0:SW], zq[0:NS, :], ['zq'], ['zs_d'])
                              dma('sync', zs_d[:, SW:2 * SW], zk[0:NS, :], ['zk'], ['zs_d'])
                              dma('sync', zs_d[:, 2 * SW:3 * SW], zv[0:NS, :], ['zv'], ['zs_d'])
                              dma('sync', zs_d[:, 3 * SW:NIN], zr[0:NS, 0:nrest], ['zr'], ['zs_d'])
                              dma('sync', o_ks[L], zk[0:NS, :], ['zk'], [])
                              dma('sync', o_vs[L], zv[0:NS, :], ['zv'], [])
                              if fox:
                                  dma('sync', zs_d[:, 3 * SW:3 * SW + 12], lf[0:NS, :], ['lf', 'zs_d'], ['zs_d'])
                                  dma('sync', o_fls, lf[0:NS, :], ['lf'], [])
                              continue
                          ts_ = slice(t * 128, (t + 1) * 128)
                          dma('sync', o_kp[L][ts_, :], zk[:, :], ['zk'], [])
                          dma('sync', o_vp[L][ts_, :], zv[:, :], ['zv'], [])
                          V_(lambda e: e.tensor_copy(out=qb_[:, :], in_=zq[:, :]), ['zq'], ['qb'])
                          G_(lambda e: e.tensor_copy(out=kb_[:, :], in_=zk[:, :]), ['zk'], ['kb'])
                          V_(lambda e: e.tensor_copy(out=mb_[:, :], in_=zr[:, nrest - 256:nrest]), ['zr'], ['mb'])
                          if fox:
                              G_(lambda e, t=t: e.tensor_copy(out=VV[:, t, :, 0:64], in_=zv[:, :].rearrange("p (h d) -> p h d", h=12)), ['zv', 'VVones'], ['VV%d' % (t // 4)])
                          else:
                              G_(lambda e, t=t: e.tensor_copy(out=VV[:, t, :], in_=zv[:, :]), ['zv'], ['VV%d' % (t // 4)])
                          for g in range(6):
                              T_(lambda e, g=g: e.transpose(pT3[:, g, :], kb_[:, g * 128:(g + 1) * 128], ident_b[:, :]), ['kb', 'identb'], [PTK[g]])
                          V_(lambda e, t=t: e.tensor_copy(out=KT[:, :, t * 128:(t + 1) * 128], in_=pT3[:, 0:6, :]), PTK[0:6], ['KT%d' % (t // 4)])
                          for g in range(6):
                              T_(lambda e, g=g: e.transpose(pT3[:, g, :], qb_[:, g * 128:(g + 1) * 128], ident_b[:, :]), ['qb', 'identb'], [PTK[g]])
                          for g in range(2):
                              T_(lambda e, g=g: e.transpose(pT3[:, 6 + g, :], mb_[:, g * 128:(g + 1) * 128], ident_b[:, :]), ['mb', 'identb'], [PTK[6 + g]])
                          A_(lambda e: e.activation(out=qT[:, :, :], in_=pT3[:, :, :], func=AF.Identity), PTK, ['qT'])
                          dma('sync', qt_d[:, :, t * 128:(t + 1) * 128].rearrange("c p t -> p c t"), qT[:, :, :], ['qT'], ['qt_d%d' % (t // 4)])
                          if fox:
                              dma('sync', o_flp[ts_, :], lf[:, :], ['lf'], [])
                              mm(pO[0][:, 0:12], tri_f[:, :], lf[:, :], True, False, ['tri', 'lf'], ['pO0'])
                              mm(pO[0][:, 0:12], ones_f[:, :], Rrun[:, :], False, True, ['onesf', 'Rrun'], ['pO0'])
                              V_(lambda e, t=t: e.tensor_scalar(out=negC[:, t, :], in0=pO[0][:, 0:12], scalar1=-1.0, scalar2=None, op0=ALU.mult), ['pO0'], ['negC'])
                              V_(lambda e: e.tensor_tensor(out=Rrun[:, :], in0=Rrun[:, :], in1=lf[:, :], op=ALU.add), ['Rrun', 'lf'], ['Rrun'])
                              mm(pO[1][:, 0:12], ones_f[:, :], Rrun[:, :], True, True, ['onesf', 'Rrun'], ['pO1'])
                              V_(lambda e, t=t: e.tensor_copy(out=Cend[:, t, :], in_=pO[1][:, 0:12]), ['pO1'], ['Cend'])

                  stop_if('P1_%d' % L)
                  sch.barrier()
                  dec = ExitStack()
                  with dec:
                      PG = 2 if NPG % 2 == 0 else 1
                      NP1 = NPG + 1
                      Kb = [sb(dec, [128, PG, SW], F32, "Kb%d" % i) for i in range(2)]
                      Vb = [sb(dec, [128, PG, SW], F32, "Vb%d" % i) for i in range(2)]
                      Vbf = [sb(dec, [128, SW + 1], BF16, "Vbf%d" % i) for i in range(2)]
                      for i in range(2):
                          G_(lambda e, i=i: e.memset(Vbf[i][:, SW:SW + 1], 1.0), [], ['Vbf%d' % i])
                      qbc = sb(dec, [128, SW], F32, "qbc"); tmp = [sb(dec, [128, SW], F32, "dtmp%d" % i) for i in range(2)]
                      Ssc = sb(dec, [128, NP1, 12], F32, "Ssc"); Dd = sb(dec, [128, NP1, 12], F32, "Dd")
                      Pp = sb(dec, [128, NP1, 12], BF16, "Pp")
                      LF = sb(dec, [128, NPG, 12], F32, "LF"); E0 = sb(dec, [128, NPG, 12], F32, "E0"); E1 = sb(dec, [128, NPG, 12], F32, "E1")
                      ptb = sb(dec, [128, NPG], I32, "ptb"); ptf = sb(dec, [128, NPG], F32, "ptf"); idx = sb(dec, [128, NPG], I32, "idx")
                      lfn = sb(dec, [128, 12], F32, "lfn")
                      mx = sb(dec, [128, 12], F32, "mx"); mrow = sb(dec, [12, 1], F32, "mrow"); dg = sb(dec, [12, 12], F32, "dg")
                      nmb = sb(dec, [128, 12], F32, "nmb")
                      ob = sb(dec, [12, SW + 1], F32, "ob"); ob2 = sb(dec, [12, SW], F32, "ob2")
                      od = sb(dec, [12, 128], F32, "od"); rl = sb(dec, [12, 1], F32, "rl")
                      oT = sb(dec, [128, 12], F32, "oT"); o6 = sb(dec, [128, 6], F32, "o6")
                      o6t = sb(dec, [6, 128], F32, "o6t"); o6s = sb(dec, [6, 1], F32, "o6s"); o6j = sb(dec, [6, 128], F32, "o6j")
                      gsr = sb(dec, [6, 128], F32, "gsr")
                      dma('sync', gsr[:, :], gsub.rearrange("p o -> o p").to_broadcast([6, 128]), [], ['gsr'])
                      V_(lambda e: e.tensor_scalar(out=gsr[:, :], in0=gsr[:, :], scalar1=(1.0 - lam_init), scalar2=None, op0=ALU.mult), ['gsr'], ['gsr'])
                      mK = sb(dec, [128, 2, 256], F32, "mK"); mV = sb(dec, [128, 2, 256], F32, "mV")
                      mVb = sb(dec, [128, 257], BF16, "mVb")
                      G_(lambda e: e.memset(mVb[:, 256:257], 1.0), [], ['mVb'])
                      qmb = sb(dec, [128, 256], F32, "qmb"); mS = sb(dec, [128, 2, 4], F32, "mS"); mP = sb(dec, [128, 2, 4], BF16, "mP")

                      def softmax_pv(Sv, skey, Pv, pkey, npages, H, vload, vdim, bmt, bmkey, tagk):
                          V_(lambda e: e.tensor_reduce(out=mx[:, 0:H], in_=Sv.rearrange("p n h -> p h n"), axis=AX.X, op=ALU.max), [skey], ['mx'])
                          T_(lambda e: e.transpose(pA[0:H, 0:128], mx[:, 0:H], ident_f[:, :]), ['mx', 'identf'], ['pA'])
                          V_(lambda e: e.tensor_reduce(out=mrow[0:H, :], in_=pA[0:H, 0:128], axis=AX.X, op=ALU.max), ['pA'], ['mrow'])
                          V_(lambda e: e.tensor_scalar(out=dg[0:H, 0:H], in0=ident_f[0:H, 0:H], scalar1=mrow[0:H, :], scalar2=-1.0, op0=ALU.mult, op1=ALU.mult), ['mrow', 'identf'], ['dg'])
                          mm(pB[:, 0:H], ones_f[0:H, :], dg[0:H, 0:H], True, True, ['onesf', 'dg'], ['pB'])
                          V_(lambda e: e.tensor_copy(out=nmb[:, 0:H], in_=pB[:, 0:H]), ['pB'], ['nmb'])
                          for h in range(H):
                              A_(lambda e, h=h: e.activation(out=Pv[:, :, h], in_=Sv[:, :, h], func=AF.Exp, bias=nmb[:, h:h + 1]), [skey, 'nmb'], [pkey])
                          nv = vdim + 1
                          halves = [(0, min(512, nv))] + ([(512, nv)] if nv > 512 else [])
                          pacc = [pO[0], pO[1]]
                          for j in range(npages):
                              vb, vkey = vload(j)
                              for hi, (a, b) in enumerate(halves):
                                  mm(pacc[hi][0:H, 0:b - a], Pv[:, j, :], vb[:, a:b], j == 0, j == npages - 1, [pkey, vkey], ['pO%d' % hi])
                          for hi, (a, b) in enumerate(halves):
                              V_(lambda e, hi=hi, a=a, b=b: e.tensor_copy(out=ob[0:H, a:b], in_=pacc[hi][0:H, 0:b - a]), ['pO%d' % hi], ['ob'])
                          V_(lambda e: e.reciprocal(out=rl[0:H, :], in_=ob[0:H, vdim:vdim + 1]), ['ob'], ['rl'])
                          nblk = bmt[1]; bw = vdim // nblk
                          V_(lambda e: e.tensor_tensor(out=ob2[0:H, 0:vdim], in0=ob[0:H, 0:vdim], in1=bmt[0][0:H, 0:vdim], op=ALU.mult), ['ob', bmkey], ['ob2'])
                          V_(lambda e: e.tensor_reduce(out=od[0:H, 0:bw], in_=ob2[0:H, 0:vdim].rearrange("p (b d) -> p d b", b=nblk), axis=AX.X, op=ALU.add), ['ob2'], ['od'])
                          V_(lambda e: e.tensor_scalar(out=od[0:H, 0:bw], in0=od[0:H, 0:bw], scalar1=rl[0:H, :], scalar2=None, op0=ALU.mult), ['od', 'rl'], ['od'])
                          return bw

                      for s in range(NS):
                          dma('sync', ptb[:, :], pt[s:s + 1, :].to_broadcast([128, NPG]), [], ['ptb'])
                          V_(lambda e: e.tensor_copy(out=ptf[:, :], in_=ptb[:, :]), ['ptb'], ['ptf'])
                          V_(lambda e: e.tensor_scalar(out=ptf[:, :], in0=ptf[:, :], scalar1=128.0, scalar2=iota_c[:, :], op0=ALU.mult, op1=ALU.add), ['ptf', 'iota'], ['ptf'])
                          V_(lambda e: e.tensor_copy(out=idx[:, :], in_=ptf[:, :]), ['ptf'], ['idx'])
                          dma('sync', qbc[:, :], zs_d[s:s + 1, 0:SW].to_broadcast([128, SW]), ['zs_d'], ['qbc'])
                          if fox:
                              for j in range(NPG):
                                  sch.op('gpsimd', lambda e, j=j: e.indirect_dma_start(out=LF[:, j, :], out_offset=None, in_=cfl, in_offset=bass.IndirectOffsetOnAxis(ap=idx[:, j:j + 1], axis=0)), ['idx'], ['LF'], dma=True)
                              dma('sync', lfn[:, :], zs_d[s:s + 1, 3 * SW:3 * SW + 12].to_broadcast([128, 12]), ['zs_d'], ['lfn'])
                              LF2 = LF[:, :, :].rearrange("p n h -> p (n h)")
                              nn = NPG * 12
                              for (a, b) in [(i, min(i + 512, nn)) for i in range(0, nn, 512)]:
                                  mm(pA[:, 0:b - a], suf_f[:, :], LF2[:, a:b], True, True, ['suf', 'LF'], ['pA'])
                                  V_(lambda e, a=a, b=b: e.tensor_copy(out=Dd[:, 0:NPG, :].rearrange("p n h -> p (n h)")[:, a:b], in_=pA[:, 0:b - a]), ['pA'], ['Dd'])
                                  mm(pB[:, 0:b - a], ones_f[:, :], LF2[:, a:b], True, True, ['onesf', 'LF'], ['pB'])
                                  V_(lambda e, a=a, b=b: e.tensor_copy(out=E1[:, :, :].rearrange("p n h -> p (n h)")[:, a:b], in_=pB[:, 0:b - a]), ['pB'], ['E1'])
                              V_(lambda e: e.memset(E0[:, :, :], 0.0), [], ['E0'])
                              if NPG > 1:
                                  V_(lambda e: e.tensor_copy(out=E0[:, 0:NPG - 1, :], in_=E1[:, 1:NPG, :]), ['E1', 'E0'], ['E0'])
                              cur, oth, ck_, ok_ = E0, E1, 'E0', 'E1'
                              sh = 1
                              while sh < NPG:
                                  V_(lambda e, cur=cur, oth=oth: e.tensor_copy(out=oth[:, :, :], in_=cur[:, :, :]), [ck_], [ok_])
                                  V_(lambda e, cur=cur, oth=oth, sh=sh: e.tensor_tensor(out=oth[:, 0:NPG - sh, :], in0=cur[:, 0:NPG - sh, :], in1=cur[:, sh:NPG, :], op=ALU.add), [ck_, ok_], [ok_])
                                  cur, oth, ck_, ok_ = oth, cur, ok_, ck_
                                  sh *= 2
                              V_(lambda e, cur=cur: e.tensor_tensor(out=Dd[:, 0:NPG, :], in0=Dd[:, 0:NPG, :], in1=cur[:, :, :], op=ALU.add), ['Dd', ck_], ['Dd'])
                              for j in range(NPG):
                                  V_(lambda e, j=j: e.tensor_tensor(out=Dd[:, j, :], in0=Dd[:, j, :], in1=lfn[:, :], op=ALU.add), ['Dd', 'lfn'], ['Dd'])
                          else:
                              V_(lambda e: e.memset(Dd[:, 0:NPG, :], 0.0), [], ['Dd'])
                          V_(lambda e: e.tensor_copy(out=Dd[:, NPG, :], in_=selfb[:, :]), ['selfb', 'Dd'], ['Dd'])

                          nch = NPG // PG
                          for chn in range(nch + 1):
                              kb = Kb[chn % 2]; kkey = 'Kb%d' % (chn % 2)
                              if chn < nch:
                                  for p in range(PG):
                                      j = chn * PG + p
                                      sch.op('gpsimd', lambda e, j=j, p=p, kb=kb: e.indirect_dma_start(out=kb[:, p, :], out_offset=None, in_=ck, in_offset=bass.IndirectOffsetOnAxis(ap=idx[:, j:j + 1], axis=0)), ['idx'], [kkey], dma=True)
                                  pages = [(chn * PG + p, p) for p in range(PG)]
                              else:
                                  G_(lambda e, kb=kb: e.memset(kb[:, 0, :], 0.0), [], [kkey])
                                  dma('sync', kb[0:1, 0, :], zs_d[s:s + 1, SW:2 * SW], ['zs_d', kkey], [kkey])
                                  pages = [(NPG, 0)]
                              for (j, p) in pages:
                                  tm_ = tmp[j % 2]; tk = 'dtmp%d' % (j % 2)
                                  V_(lambda e, kb=kb, p=p, tm_=tm_: e.tensor_tensor(out=tm_[:, :], in0=kb[:, p, :], in1=qbc[:, :], op=ALU.mult), [kkey, 'qbc'], [tk])
                                  V_(lambda e, j=j, tm_=tm_: e.tensor_reduce(out=Ssc[:, j, :], in_=tm_[:, :].rearrange("p (h d) -> p h d", d=64), axis=AX.X, op=ALU.add), [tk], ['Ssc'])
                          V_(lambda e: e.scalar_tensor_tensor(out=Ssc[:, :, :], in0=Ssc[:, :, :], scalar=0.125, in1=Dd[:, :, :], op0=ALU.mult, op1=ALU.add), ['Ssc', 'Dd'], ['Ssc'])

                          def vload(j, s=s):
                              chn, p = (j // PG, j % PG) if j < NPG else (nch, 0)
                              vb = Vb[chn % 2]; vkey = 'Vb%d' % (chn % 2)
                              if p == 0:
                                  if j < NPG:
                                      for pp_ in range(PG):
                                          jj = chn * PG + pp_
                                          sch.op('gpsimd', lambda e, jj=jj, pp_=pp_, vb=vb: e.indirect_dma_start(out=vb[:, pp_, :], out_offset=None, in_=cv, in_offset=bass.IndirectOffsetOnAxis(ap=idx[:, jj:jj + 1], axis=0)), ['idx'], [vkey], dma=True)
                                  else:
                                      G_(lambda e, vb=vb: e.memset(vb[:, 0, :], 0.0), [], [vkey])
                                      dma('sync', vb[0:1, 0, :], zs_d[s:s + 1, 2 * SW:3 * SW], ['zs_d', vkey], [vkey])
                              vf = Vbf[j % 2]; fk = 'Vbf%d' % (j % 2)
                              A_(lambda e, vb=vb, p=p, vf=vf: e.activation(out=vf[:, 0:SW], in_=vb[:, p, :], func=AF.Identity), [vkey], [fk])
                              return vf, fk

                          bw = softmax_pv(Ssc[:, :, :], 'Ssc', Pp, 'Pp', NP1, 12, vload, SW, (bm[L], 12 if fox else 6), 'bm%d' % L, 'a')
                          if fox:
                              T_(lambda e: e.transpose(pA[0:64, 0:12], od[0:12, 0:64], ident_f[0:12, 0:12]), ['od', 'identf'], ['pA'])
                              V_(lambda e, s=s: e.tensor_copy(out=OTs_self[0:64, :, s], in_=pA[0:64, 0:12]), ['pA'], ['OTs_self'])
                          else:
                              T_(lambda e: e.transpose(pA[:, 0:12], od[0:12, 0:128], ident_f[0:12, 0:12]), ['od', 'identf'], ['pA'])
                              V_(lambda e: e.tensor_copy(out=oT[:, :], in_=pA[:, 0:12]), ['pA'], ['oT'])
                              oTv = oT[:, :].rearrange("p (h two) -> p h two", two=2)
                              V_(lambda e: e.scalar_tensor_tensor(out=o6[:, :], in0=oTv[:, :, 1], scalar=nlam[:, :], in1=oTv[:, :, 0], op0=ALU.mult, op1=ALU.add), ['oT', 'nlam'], ['o6'])
                              T_(lambda e: e.transpose(pB[0:6, 0:128], o6[:, :], ident_f[:, :]), ['o6', 'identf'], ['pB'])
                              V_(lambda e: e.tensor_copy(out=o6t[:, :], in_=pB[0:6, 0:128]), ['pB'], ['o6t'])
                              V_(lambda e: e.tensor_tensor(out=o6j[:, :], in0=o6t[:, :], in1=o6t[:, :], op=ALU.mult), ['o6t'], ['o6j'])
                              V_(lambda e: e.tensor_reduce(out=o6s[:, :], in_=o6j[:, :], axis=AX.X, op=ALU.add), ['o6j'], ['o6s'])
                              V_(lambda e: e.tensor_scalar(out=o6s[:, :], in0=o6s[:, :], scalar1=1.0 / 128, scalar2=EPS, op0=ALU.mult, op1=ALU.add), ['o6s'], ['o6s'])
                              A_(lambda e: e.activation(out=o6s[:, :], in_=o6s[:, :], func=AF.Sqrt), ['o6s'], ['o6s'])
                              V_(lambda e: e.reciprocal(out=o6s[:, :], in_=o6s[:, :]), ['o6s'], ['o6s'])
                              V_(lambda e: e.scalar_tensor_tensor(out=o6t[:, :], in0=o6t[:, :], scalar=o6s[:, :], in1=gsr[:, :], op0=ALU.mult, op1=ALU.mult), ['o6t', 'o6s', 'gsr'], ['o6t'])
                              T_(lambda e: e.transpose(pA[:, 0:6], o6t[:, :], ident_f[0:6, 0:6]), ['o6t', 'identf'], ['pA'])
                              V_(lambda e, s=s: e.tensor_copy(out=OTs_self[:, 0:6, s], in_=pA[:, 0:6]), ['pA'], ['OTs_self'])

                          dma('sync', qmb[:, :], zs_d[s:s + 1, QMC:QMC + 256].to_broadcast([128, 256]), ['zs_d'], ['qmb'])
                          dma('sync', mK[:, :, :], cmk[L, s].rearrange("(t p) f -> p t f", p=128), [], ['mK'])
                          dma('sync', mV[:, :, :], cmv[L, s].rearrange("(t p) f -> p t f", p=128), [], ['mV'])
                          for mt in range(2):
                              G_(lambda e, mt=mt: e.tensor_tensor(out=tmp[mt][:, 0:256], in0=mK[:, mt, :], in1=qmb[:, :], op=ALU.mult), ['mK', 'qmb'], ['dtmp%d' % mt])
                              V_(lambda e, mt=mt: e.tensor_reduce(out=mS[:, mt, :], in_=tmp[mt][:, 0:256].rearrange("p (h d) -> p h d", d=64), axis=AX.X, op=ALU.add), ['dtmp%d' % mt], ['mS'])
                          V_(lambda e: e.tensor_scalar(out=mS[:, :, :], in0=mS[:, :, :], scalar1=0.125, scalar2=None, op0=ALU.mult), ['mS'], ['mS'])

                          def mvload(j):
                              A_(lambda e, j=j: e.activation(out=mVb[:, 0:256], in_=mV[:, j, :], func=AF.Identity), ['mV', 'mVb'], ['mVb'])
                              return mVb, 'mVb'
                          softmax_pv(mS[:, :, :], 'mS', mP, 'mP', 2, 4, mvload, 256, (bm[0], 4), 'bm0', 'm')
                          T_(lambda e: e.transpose(pA[0:64, 0:4], od[0:4, 0:64], ident_f[0:4, 0:4]), ['od', 'identf'], ['pA'])
                          V_(lambda e, s=s: e.tensor_copy(out=OTs_mem[0:64, :, s], in_=pA[0:64, 0:4]), ['pA'], ['OTs_mem'])

                  stop_if('DEC_%d' % L)
                  sch.barrier()
                  p2 = ExitStack()
                  with p2:
                      nsc = 12 if fox else 6
                      pk_ = 64 if fox else 128
                      Wo = sb(p2, [128, nsc, D], BF16, "Wo"); Wom = sb(p2, [64, 4, D], BF16, "Wom")
                      for c in range(nsc):
                          dma('gpsimd', Wo[0:pk_, c, :], w_out[L, c * pk_:(c + 1) * pk_, :], [], ['Wo'])
                      for c in range(4):
                          dma('gpsimd', Wom[0:64, c, :], w_out[L, SW + c * 64:SW + (c + 1) * 64, :], [], ['Wom'])
                      gp = sb(p2, [128, D], F32, "gp")
                      dma('sync', gp[:, :], gpost[L, 0:1, :].to_broadcast([128, D]), [], ['gp'])
                      QT = sb(p2, [128, 8, 512], BF16, "QT")
                      PT = [sb(p2, [128, 512], BF16, "PT%d" % i) for i in range(3)]
                      OT = sb(p2, [128, nsc, 512], BF16, "OT"); OTm = sb(p2, [64, 4, 512], BF16, "OTm")
                      Bq = sb(p2, [128, 4, 12, max(NT, 1)], F32, "Bq")
                      rlt = sb(p2, [128, 512], F32, "rlt"); rbc = sb(p2, [128, 512], F32, "rbc")
                      on0 = sb(p2, [128, 512], F32, "on0"); on1 = sb(p2, [128, 512], F32, "on1"); sq = sb(p2, [128, 512], BF16, "sq")
                      hres = sb(p2, [128, D], F32, "hres"); hout = sb(p2, [128, D], F32, "hout")
                      sc_ = {'i': 0, 'o': 0}

                      def attend(qb, Kt, kpart, kg, vfn, nkt, bias_fn, causal, qsrc, qg, out_fn, aug):
                          sc_['o'] += 1
                          po = pO[sc_['o'] % 2]; pok = 'pO%d' % (sc_['o'] % 2)
                          pSS = [pS[0], pS[1], pO[2]]; pSK = ['pS0', 'pS1', 'pO2']
                          if not aug:
                              pl, plk = next_pab()
                          base = sc_['i']
                          sc_['i'] += nkt

                          def emitS(kt):
                              u = base + kt + 1
                              qlo = max(0, kt - 4 * qb) if causal else 0
                              c0 = qlo * 128
                              psb = pSS[u % 3]; psk = pSK[u % 3]
                              mm(psb[:, c0:512], Kt[kpart, kg, kt * 128:(kt + 1) * 128], qsrc[kpart, qg, c0:512], True, True,
                                 ['KT%d' % (kt // 4) if causal else 'MKT', 'QT'], [psk])

                          emitS(0)
                          if nkt > 1:
                              emitS(1)
                          for kt in range(nkt):
                              if kt + 2 < nkt:
                                  emitS(kt + 2)
                              u = base + kt + 1
                              qlo = max(0, kt - 4 * qb) if causal else 0
                              c0 = qlo * 128
                              psb = pSS[u % 3]; psk = pSK[u % 3]
                              ptb_ = PT[u % 3]; ptk = 'PT%d' % (u % 3)
                              pks = [ptk + '_%d' % q_ for q_ in range(qlo, 4)]
                              if bias_fn is None:
                                  A_(lambda e, psb=psb, ptb_=ptb_, c0=c0: e.activation(out=ptb_[:, c0:512], in_=psb[:, c0:512], func=AF.Exp, scale=0.125), [psk], pks)
                              else:
                                  for qt in range(qlo, 4):
                                      A_(lambda e, psb=psb, ptb_=ptb_, qt=qt, kt=kt: e.activation(out=ptb_[:, qt * 128:(qt + 1) * 128], in_=psb[:, qt * 128:(qt + 1) * 128], func=AF.Exp, scale=0.125, bias=bias_fn(qt, kt)), [psk, 'Bq'], [ptk + '_%d' % qt])
                              if causal and kt >= 4 * qb:
                                  G_(lambda e, ptb_=ptb_, c0=c0: e.tensor_tensor(out=ptb_[:, c0:c0 + 128], in0=ptb_[:, c0:c0 + 128], in1=maskT[:, :], op=ALU.mult), [pks[0], 'maskT'], [pks[0]])
                              vl, vk = vfn(kt)
                              mm(po[0:(65 if aug else 128), c0:512], vl, ptb_[:, c0:512], kt == 0, kt == nkt - 1, [vk] + pks, [pok])
                              if not aug:
                                  mm(pl[:, c0:512], ones_b[:, :], ptb_[:, c0:512], kt == 0, kt == nkt - 1, ['onesb'] + pks, [plk])
                          if aug:
                              V_(lambda e: e.reciprocal(out=rlt[64:65, :], in_=po[64:65, :]), [pok], ['rlt'])
                              px, pxk = next_pab()
                              mm(px[0:64, :], ones_f[64:65, 0:64], rlt[64:65, :], True, True, ['onesf', 'rlt'], [pxk])
                              V_(lambda e, px=px: e.tensor_copy(out=rbc[0:64, :], in_=px[0:64, :]), [pxk], ['rbc'])
                              out_fn(po, pok)
                          else:
                              V_(lambda e, pl=pl: e.reciprocal(out=rbc[:, :], in_=pl[:, :]), [plk], ['rbc'])
                              out_fn(po, pok)

                      for qb in range(NB + 1):
                          smp = (qb == NB)
                          if not smp:
                              dma('sync', QT[:, :, :], qt_d[:, :, qb * 512:(qb + 1) * 512].rearrange("c p t -> p c t"), ['qt_d%d' % qb], ['QT'])
                              nkt = 4 * qb + 4
                              if fox:
                                  for qt in range(4):
                                      tq = 4 * qb + qt
                                      for h in range(12):
                                          V_(lambda e, qt=qt, tq=tq, h=h: e.tensor_scalar(out=Bq[:, qt, h, 0:tq + 1], in0=negC[:, 0:tq + 1, h], scalar1=Cend[:, tq, h:h + 1], scalar2=None, op0=ALU.add), ['negC', 'Cend'], ['Bq'])
                                  for h in range(12):
                                      kp = slice((h % 2) * 64, (h % 2) * 64 + 64)

                                      def outf(po, pok, h=h):
                                          V_(lambda e: e.tensor_tensor(out=OT[0:64, h, :], in0=po[0:64, :], in1=rbc[0:64, :], op=ALU.mult), [pok, 'rbc'], ['OT'])
                                      attend(qb, KT, kp, h // 2, lambda kt, h=h: (VV[:, kt, h, :], 'VV%d' % (kt // 4)), nkt,
                                             lambda qt, kt, h=h: Bq[:, qt, h, kt:kt + 1], True, QT, h // 2, outf, True)
                              else:
                                  for hh in range(12):
                                      h, c = hh // 2, hh % 2
                                      kp = slice(c * 64, c * 64 + 64)

                                      def outf(po, pok, h=h, c=c):
                                          if c == 0:
                                              V_(lambda e: e.tensor_tensor(out=on0[:, :], in0=po[:, :], in1=rbc[:, :], op=ALU.mult), [pok, 'rbc'], ['on0'])
                                              return
                                          V_(lambda e: e.tensor_tensor(out=on1[:, :], in0=po[:, :], in1=rbc[:, :], op=ALU.mult), [pok, 'rbc'], ['on1'])
                                          V_(lambda e: e.scalar_tensor_tensor(out=on0[:, :], in0=on1[:, :], scalar=nlam[:, :], in1=on0[:, :], op0=ALU.mult, op1=ALU.add), ['on0', 'on1', 'nlam'], ['on0'])
                                          G_(lambda e: e.tensor_tensor(out=sq[:, :], in0=on0[:, :], in1=on0[:, :], op=ALU.mult), ['on0'], ['sq'])
                                          px, pxk = next_pab()
                                          mm(px[:, :], ones_b[:, :], sq[:, :], True, True, ['onesb', 'sq'], [pxk])
                                          A_(lambda e, px=px: e.activation(out=on1[:, :], in_=px[:, :], func=AF.Sqrt, scale=1.0 / 128, bias=eps_c[:, :]), [pxk, 'epsc'], ['on1'])
                                          V_(lambda e: e.reciprocal(out=on1[:, :], in_=on1[:, :]), ['on1'], ['on1'])
                                          V_(lambda e: e.scalar_tensor_tensor(out=OT[:, h, :], in0=on0[:, :], scalar=gsub_c[:, :], in1=on1[:, :], op0=ALU.mult, op1=ALU.mult), ['on0', 'on1', 'gsubc'], ['OT'])
                                      attend(qb, KT, kp, h, lambda kt, h=h: (VV[:, kt, h * 128:(h + 1) * 128], 'VV%d' % (kt // 4)), nkt,
                                             None, True, QT, h, outf, False)
                              for hm in range(4):
                                  kp = slice((hm % 2) * 64, (hm % 2) * 64 + 64)

                                  def outm(po, pok, hm=hm):
                                      V_(lambda e: e.tensor_tensor(out=OTm[0:64, hm, :], in0=po[0:64, :], in1=rbc[0:64, :], op=ALU.mult), [pok, 'rbc'], ['OTm'])
                                  attend(qb, MKT, kp, hm // 2, lambda kt, hm=hm: (MV[:, kt, hm, :], 'MV'), 2, None, False, QT, 6 + hm // 2, outm, True)
                          tiles = [(None, NS)] if smp else [(4 * qb + i, 128) for i in range(4)]
                          for ti, (t, r) in enumerate(tiles):
                              if smp:
                                  os_, om_ = OTs_self, OTs_mem; osk, omk = 'OTs_self', 'OTs_mem'; cs_ = slice(0, NS)
                                  src = hs_src; dst = hmid_d[S:S + NS, :]
                              else:
                                  os_, om_ = OT, OTm; osk, omk = 'OT', 'OTm'; cs_ = slice(ti * 128, (ti + 1) * 128)
                                  src = h_src[t * 128:(t + 1) * 128, :]; dst = hmid_d[t * 128:(t + 1) * 128, :]
                              dma(ldq(), hres[0:r, :], src, ['h1_d'], ['hres'])
                              for hf, (py, pyk) in enumerate(((pA, 'pA'), (pB, 'pB'))):
                                  for c in range(nsc):
                                      mm(py[0:r, :], os_[0:pk_, c, cs_], Wo[0:pk_, c, hf * 512:(hf + 1) * 512], c == 0, False, [osk, 'Wo'], [pyk])
                                  for c in range(4):
                                      mm(py[0:r, :], om_[0:64, c, cs_], Wom[0:64, c, hf * 512:(hf + 1) * 512], False, c == 3, [omk, 'Wom'], [pyk])
                              if True:
                                  st = None
                                  post_norm_residual(st, (pA, pB), ('pA', 'pB'), r, gp, 'gp', hres, 'hres', hout, 'hout')
                              dma('sync', dst, hout[0:r, :], ['hout0', 'hout1'], ['hmid_d'])

              stop_if('P2_%d' % L)
              sch.barrier()
              p3 = ExitStack()
              with p3:
                  Wg = sb(p3, [128, 8, 2 * DFF], BF16, "Wg"); Wd = sb(p3, [128, 22, D], BF16, "Wd")
                  load_w_bf16(Wg, 'Wg', w_gu[L], D, 2 * DFF, 8)
                  load_w_bf16(Wd, 'Wd', w_dn[L], DFF, D, 22)
                  gp = sb(p3, [128, D], F32, "gp3"); gc3 = sb(p3, [128, 8], F32, "gc3")
                  dma('sync', gp[:, :], gpost[L, 1:2, :].to_broadcast([128, D]), [], ['gp3'])
                  dma('sync', gc3[:, :], gcols[L, 1], [], ['gc3'])
                  xT = sb(p3, [128, 8, 512], BF16, "xT3"); fT = sb(p3, [128, 22, 512], BF16, "fT")
                  hm = [sb(p3, [128, D], F32, "hm%d" % i) for i in range(4)]
                  sg = sb(p3, [128, 512], F32, "sg"); hout = sb(p3, [128, D], F32, "hout3")
                  for qb in range(NB + 1):
                      smp = (qb == NB)
                      tiles = [(None, NS)] if smp else [(4 * qb + i, 128) for i in range(4)]
                      T = NS if smp else 512
                      for ti, (t, r) in enumerate(tiles):
                          src = hmid_d[S:S + NS, :] if smp else hmid_d[t * 128:(t + 1) * 128, :]
                          sch.op(ldq(), lambda e, ti=ti, r=r, src=src: e.dma_start(out=hm[ti][0:r, :], in_=src), ['hmid_d'], ['hm%d' % ti], dma=True)
                          norm_from(hm[ti], r, gc3, 'gc3', xT, 'xT3', ti * 128, 'hm%d' % ti)
                      for j in range(22):
                          for c in range(8):
                              mm(pA[:, 0:T], Wg[:, c, j * 128:(j + 1) * 128], xT[:, c, 0:T], c == 0, c == 7, ['Wg'] + xkeys('xT3'), ['pA'])
                          for c in range(8):
                              mm(pB[:, 0:T], Wg[:, c, DFF + j * 128:DFF + (j + 1) * 128], xT[:, c, 0:T], c == 0, c == 7, ['Wg'] + xkeys('xT3'), ['pB'])
                          A_(lambda e, T=T: e.activation(out=sg[:, 0:T], in_=pA[:, 0:T], func=AF.Silu), ['pA'], ['sg'])
                          V_(lambda e, j=j, T=T: e.tensor_tensor(out=fT[:, j, 0:T], in0=sg[:, 0:T], in1=pB[:, 0:T], op=ALU.mult), ['sg', 'pB'], ['fT'])
                      for ti, (t, r) in enumerate(tiles):
                          cs_ = slice(ti * 128, ti * 128 + r)
                          for hf, (py, pyk) in enumerate(((pO[0], 'pO0'), (pO[1], 'pO1'))):
                              for j in range(22):
                                  mm(py[0:r, :], fT[:, j, cs_], Wd[:, j, hf * 512:(hf + 1) * 512], j == 0, j == 21, ['fT', 'Wd'], [pyk])
                          if True:
                              st = None
                              post_norm_residual(st, (pO[0], pO[1]), ('pO0', 'pO1'), r, gp, 'gp3', hm[ti], 'hm%d' % ti, hout, 'hout3')
                          dst = hs_dst if smp else h_dst[t * 128:(t + 1) * 128, :]
                          dma('sync', dst, hout[0:r, :], ['hout30', 'hout31'], ['h1_d'])
              sch.barrier()

          except _Stop:
            break
        sch.emit()
    except AssertionError:
        if not KSTOP:
            raise
    return nc


_CACHE = {}


def _consts(S, NPG):
    c = {}
    c["c_ident"] = np.eye(128, dtype=np.float32)
    pp = np.arange(128)
    c["c_tri"] = (pp[:, None] <= pp[None, :]).astype(np.float32)
    c["c_suf"] = (pp[:, None] > pp[None, :]).astype(np.float32)
    c["c_maskT"] = (pp[None, :] >= pp[:, None]).astype(np.float32)
    c["c_iota"] = pp.astype(np.float32)[:, None]
    inv = (np.float32(10000.0) ** (-np.arange(0, 64, 2, dtype=np.float32) / np.float32(64))).astype(np.float32)
    pos = np.arange(S, dtype=np.float32)
    ang = (pos[:, None] * inv[None, :]).astype(np.float32)
    c["c_cos"] = np.ascontiguousarray(np.tile(np.cos(ang).astype(np.float32), (1, 12)))
    c["c_sin"] = np.ascontiguousarray(np.tile(np.sin(ang).astype(np.float32), (1, 12)))
    angs = (np.full((NS, 1), NPG * 128, dtype=np.float32) * inv[None, :]).astype(np.float32)
    c["c_cos_s"] = np.ascontiguousarray(np.tile(np.cos(angs).astype(np.float32), (1, 12)))
    c["c_sin_s"] = np.ascontiguousarray(np.tile(np.sin(angs).astype(np.float32), (1, 12)))
    sbias = np.full((128, 12), NEG, dtype=np.float32); sbias[0, :] = 0.0
    c["c_selfb"] = sbias
    bmf = np.zeros((12, 12, 64), np.float32)
    for h in range(12):
        bmf[h, h, :] = 1.0
    c["c_bm_fox"] = bmf.reshape(12, SW)
    bmd = np.zeros((12, 6, 128), np.float32)
    for h in range(12):
        bmd[h, h // 2, :] = 1.0
    c["c_bm_diff"] = bmd.reshape(12, SW)
    return c


def kernel(x_prompt, x_sample, cache_fox_k, cache_fox_v, cache_fox_logf, cache_diff_k, cache_diff_v,
           cache_mem_k, cache_mem_v, page_table, mem_prompt, w_in_fox, b_f_fox, w_in_diff,
           lam_q1, lam_k1, lam_q2, lam_k2, g_subln, g_pre_mix, g_post_mix, g_pre_ffn, g_post_ffn,
           g_mem, w_mem_kv, w_out, w_gate_up, w_down):
    f = lambda a: np.ascontiguousarray(np.asarray(a, dtype=np.float32))
    x_prompt = f(x_prompt); x_sample = f(x_sample)
    B, S, _ = x_prompt.shape
    DB = x_sample.shape[0]
    NPOOL = cache_fox_k.shape[1]
    page_table = np.ascontiguousarray(np.asarray(page_table, dtype=np.int32))
    NPG = page_table.shape[1]
    key = (S, NPG, NPOOL)
    if key not in _CACHE:
        _CACHE[key] = build(S, NPG, NPOOL)
    nc = _CACHE[key]
    cst = _consts(S, NPG)
    cfk = f(cache_fox_k).reshape(NPOOL * 128, SW); cfv = f(cache_fox_v).reshape(NPOOL * 128, SW)
    cfl = f(cache_fox_logf).reshape(NPOOL * 128, 12)
    cdk = f(cache_diff_k).reshape(NPOOL * 128, SW); cdv = f(cache_diff_v).reshape(NPOOL * 128, SW)
    cmk = f(cache_mem_k).reshape(2, DB, 256, 256); cmv = f(cache_mem_v).reshape(2, DB, 256, 256)
    gl = [f(g_pre_mix), f(g_pre_ffn), f(g_mem)]
    gcols = np.zeros((2, 3, 128, 8), np.float32)
    for L in range(2):
        for j in range(3):
            gcols[L, j] = gl[j][L].reshape(8, 128).T
    gpost = np.ascontiguousarray(np.stack([f(g_post_mix), f(g_post_ffn)], axis=1))
    lamv = np.ascontiguousarray(np.stack([f(lam_q1)[0], f(lam_k1)[0], f(lam_q2)[0], f(lam_k2)[0]], axis=0))
    shared = dict(cfk=cfk, cfv=cfv, cfl=cfl, cdk=cdk, cdv=cdv,
                  w_in_fox=f(w_in_fox)[0], w_in_diff=f(w_in_diff)[0], b_f=f(b_f_fox).reshape(1, 12), lamv=lamv,
                  gsub=f(g_subln)[0].reshape(128, 1), gcols=gcols, gpost=gpost,
                  w_mem=f(w_mem_kv), w_out=f(w_out), w_gu=f(w_gate_up), w_dn=f(w_down))
    shared.update(cst)
    in_maps = []
    for c in range(NCORES):
        b = c % B
        m = dict(shared)
        m["xp"] = x_prompt[b]
        m["xs"] = np.ascontiguousarray(x_sample[NS * c:NS * (c + 1), 0, :])
        m["cmk"] = np.ascontiguousarray(cmk[:, NS * c:NS * (c + 1)])
        m["cmv"] = np.ascontiguousarray(cmv[:, NS * c:NS * (c + 1)])
        m["pt"] = np.ascontiguousarray(page_table[NS * c:NS * (c + 1)])
        m["memp"] = f(mem_prompt)[b]
        in_maps.append(m)
    res = run_bass_kernel_spmd(nc, in_maps, core_ids=list(range(NCORES)))
    R = res.results
    g = lambda name, cores: np.stack([np.asarray(R[c][name], dtype=np.float32) for c in cores], axis=0)
    pc = list(range(B)); ac = list(range(NCORES))
    y_p = g("o_yp", pc)
    y_s = g("o_ys", ac).reshape(DB, 1, D)
    fk_p = g("o_fkp", pc).reshape(1, B, S, 12, 64); fv_p = g("o_fvp", pc).reshape(1, B, S, 12, 64)
    fl_p = g("o_flp", pc).reshape(1, B, S, 12)
    fk_s = g("o_fks", ac).reshape(1, DB, 1, 12, 64); fv_s = g("o_fvs", ac).reshape(1, DB, 1, 12, 64)
    fl_s = g("o_fls", ac).reshape(1, DB, 1, 12)
    dk_p = g("o_dkp", pc).reshape(1, B, S, 6, 128); dv_p = g("o_dvp", pc).reshape(1, B, S, 6, 128)
    dk_s = g("o_dks", ac).reshape(1, DB, 1, 6, 128); dv_s = g("o_dvs", ac).reshape(1, DB, 1, 6, 128)
    mk = np.ascontiguousarray(g("o_mk", pc).transpose(1, 0, 2, 3)).reshape(2, B, 256, 4, 64)
    mv = np.ascontiguousarray(g("o_mv", pc).transpose(1, 0, 2, 3)).reshape(2, B, 256, 4, 64)
    return (y_p, y_s, fk_p, fv_p, fl_p, fk_s, fv_s, fl_s, dk_p, dv_p, dk_s, dv_s, mk, mv)
```

```python
import math
import os
from contextlib import ExitStack
import numpy as np
import concourse.bass as bass
import concourse.mybir as mybir
from concourse.bass_utils import run_bass_kernel_spmd

F32 = mybir.dt.float32
BF16 = mybir.dt.bfloat16
I32 = mybir.dt.int32
AF = mybir.ActivationFunctionType
ALU = mybir.AluOpType
AX = mybir.AxisListType

D = 1024
NCORES = 8
SW = 768
DFF = 2816
EPS = 1e-6
NEG = -1e30
NS = 4
KSTOP = os.environ.get('KSTOP', '')


class _Stop(Exception):
    pass

SAME_ENG_SYNC = ('noself' not in os.environ.get('KVAR', ''))


class _Rec:
    def __getattr__(self, name):
        def f(*a, **k):
            self.__dict__['call'] = (name, a, k)
            return self
        return f


class Sched:
    ENG = ['tensor', 'vector', 'scalar', 'gpsimd', 'sync']
    DMAQ = ['sync', 'gpsimd', 'scalar']
    PSUM_KEYS = frozenset(['pT', 'pA', 'pB', 'pS0', 'pS1', 'pO0', 'pO1', 'pO2'])

    def __init__(self, nc, stack, nslots=12):
        self.nc = nc
        self.prog = {e: [] for e in self.ENG}
        self.sem = {e: stack.enter_context(nc.semaphore("s_" + e)) for e in self.ENG[:4]}
        self.cnt = {e: 0 for e in self.ENG}
        self.ns = nslots
        self.dsl = {q: [stack.enter_context(nc.semaphore("d_%s_%d" % (q, i))) for i in range(nslots)]
                    for q in self.DMAQ}
        self.dn = {q: 0 for q in self.DMAQ}
        self.waited = {}
        self.lastw = {}
        self.readers = {}

    def op(self, eng, fn, reads=(), writes=(), dma=False):
        deps = []
        for k in reads:
            if k in self.lastw:
                deps.append(self.lastw[k])
            if k in self.PSUM_KEYS:
                deps.extend(t for t in self.readers.get(k, ()) if t[3] != eng)
        wdeps = []
        for k in writes:
            if k in self.lastw:
                wdeps.append(self.lastw[k])
            wdeps.extend(self.readers.get(k, ()))
        deps.extend(wdeps)
        waits = []

        def need(semname, sem, val):
            key = (eng, semname)
            if self.waited.get(key, 0) >= val:
                return
            self.waited[key] = val
            waits.append((sem, val))

        for (semname, sem, val, peng, pdma) in deps:
            if peng == eng and not pdma:
                if eng == 'tensor' or not SAME_ENG_SYNC:
                    continue
            need(semname, sem, val)
        if dma:
            n = self.dn[eng]
            slot, rnd = n % self.ns, n // self.ns
            sem = self.dsl[eng][slot]
            semname = "d_%s_%d" % (eng, slot)
            if rnd > 0:
                need(semname, sem, 16 * rnd)
            val = 16 * (rnd + 1)
            inc = 16
            self.dn[eng] = n + 1
        else:
            sem = self.sem[eng]
            semname = "s_" + eng
            self.cnt[eng] += 1
            val = self.cnt[eng]
            inc = 1
        rec = _Rec()
        fn(rec)
        self.prog[eng].append((waits, rec.__dict__['call'], sem, inc))
        tok = (semname, sem, val, eng, dma)
        for k in writes:
            self.lastw[k] = tok
            self.readers[k] = []
        for k in reads:
            if k not in writes:
                self.readers.setdefault(k, []).append(tok)

    def barrier(self):
        for e in self.ENG:
            waits = []
            for p in self.ENG[:4]:
                v = self.cnt[p]
                if v > self.waited.get((e, 's_' + p), 0):
                    self.waited[(e, 's_' + p)] = v
                    waits.append((self.sem[p], v))
            for q in self.DMAQ:
                n = self.dn[q]
                for slot in range(self.ns):
                    c = n // self.ns + (1 if slot < n % self.ns else 0)
                    nm = "d_%s_%d" % (q, slot)
                    if c > 0 and 16 * c > self.waited.get((e, nm), 0):
                        self.waited[(e, nm)] = 16 * c
                        waits.append((self.dsl[q][slot], 16 * c))
            self.prog[e].append((waits, None, None, 0))

    def emit(self):
        fin = {}
        for q in self.DMAQ:
            lst = []
            for slot in range(self.ns):
                n = self.dn[q]
                cntslot = n // self.ns + (1 if slot < n % self.ns else 0)
                if cntslot > 0:
                    lst.append((self.dsl[q][slot], 16 * cntslot))
            fin[q] = lst
        with self.nc.Block() as block:
            for e in self.ENG:
                def body(eng, e=e):
                    for waits, fn, sem, inc in self.prog[e]:
                        for (s, v) in waits:
                            eng.wait_ge(s, v)
                        if fn is not None:
                            name, a, k = fn
                            getattr(eng, name)(*a, **k).then_inc(sem, inc)
                    for (s, v) in fin.get(e, ()):
                        eng.wait_ge(s, v)
                getattr(block, e)(body)


def build(S, NPG, NPOOL):
    NT = S // 128
    NB = S // 512
    nc = bass.Bass("TRN2", target_bir_lowering=False)

    def din(name, shape, dt=F32):
        return nc.dram_tensor(name, list(shape), dt, kind="ExternalInput").ap()

    def dout(name, shape, dt=F32):
        return nc.dram_tensor(name, list(shape), dt, kind="ExternalOutput").ap()

    def dscr(name, shape, dt=F32):
        return nc.dram_tensor(name, list(shape), dt, kind="Internal").ap()

    xp = din("xp", [S, D]); xs = din("xs", [NS, D])
    cfk = din("cfk", [NPOOL * 128, SW]); cfv = din("cfv", [NPOOL * 128, SW]); cfl = din("cfl", [NPOOL * 128, 12])
    cdk = din("cdk", [NPOOL * 128, SW]); cdv = din("cdv", [NPOOL * 128, SW])
    cmk = din("cmk", [2, NS, 256, 256]); cmv = din("cmv", [2, NS, 256, 256])
    pt = din("pt", [NS, NPG], I32)
    memp = din("memp", [256, D])
    w_in = [din("w_in_fox", [D, 2572]), din("w_in_diff", [D, 2560])]
    b_f = din("b_f", [1, 12])
    lamv = din("lamv", [4, 64])
    gsub = din("gsub", [128, 1])
    gcols = din("gcols", [2, 3, 128, 8])
    gpost = din("gpost", [2, 2, D])
    w_mem = din("w_mem", [2, D, 512]); w_out = din("w_out", [2, D, D])
    w_gu = din("w_gu", [2, D, 2 * DFF]); w_dn = din("w_dn", [2, DFF, D])
    c_ident = din("c_ident", [128, 128]); c_tri = din("c_tri", [128, 128]); c_suf = din("c_suf", [128, 128])
    c_maskT = din("c_maskT", [128, 128]); c_iota = din("c_iota", [128, 1])
    c_cos = din("c_cos", [S, 384]); c_sin = din("c_sin", [S, 384])
    c_cos_s = din("c_cos_s", [NS, 384]); c_sin_s = din("c_sin_s", [NS, 384])
    c_selfb = din("c_selfb", [128, 12])
    c_bm_fox = din("c_bm_fox", [12, SW]); c_bm_diff = din("c_bm_diff", [12, SW])

    o_yp = dout("o_yp", [S, D]); o_ys = dout("o_ys", [NS, D])
    o_kp = [dout("o_fkp", [S, SW]), dout("o_dkp", [S, SW])]
    o_vp = [dout("o_fvp", [S, SW]), dout("o_dvp", [S, SW])]
    o_ks = [dout("o_fks", [NS, SW]), dout("o_dks", [NS, SW])]
    o_vs = [dout("o_fvs", [NS, SW]), dout("o_dvs", [NS, SW])]
    o_flp = dout("o_flp", [S, 12]); o_fls = dout("o_fls", [NS, 12])
    o_mk = dout("o_mk", [2, 256, 256]); o_mv = dout("o_mv", [2, 256, 256])

    hmid_d = dscr("hmid_d", [S + NS, D]); h1_d = dscr("h1_d", [S + NS, D])
    qt_d = dscr("qt_d", [8, 128, S], BF16)
    zs_d = dscr("zs_d", [NS, 2572])

    top = ExitStack()
    try:
      with top:
        sch = Sched(nc, top)
        uid = [0]

        def sb(stack, shape, dt=F32, name=None):
            uid[0] += 1
            return stack.enter_context(nc.sbuf_tensor("%s_%d" % (name or "t", uid[0]), list(shape), dt))

        def ps(stack, shape, dt=F32, name=None):
            uid[0] += 1
            return stack.enter_context(nc.psum_tensor("%s_%d" % (name or "p", uid[0]), list(shape), dt))

        def dma(q, out, in_, reads, writes):
            sch.op(q, lambda e: e.dma_start(out=out, in_=in_), reads, writes, dma=True)

        def V_(fn, reads, writes):
            sch.op('vector', fn, reads, writes)

        def A_(fn, reads, writes):
            sch.op('scalar', fn, reads, writes)

        def G_(fn, reads, writes):
            sch.op('gpsimd', fn, reads, writes)

        def T_(fn, reads, writes):
            sch.op('tensor', fn, reads, writes)

        def mm(out, lhsT, rhs, start, stop, reads, writes):
            T_(lambda e: e.matmul(out, lhsT, rhs, start=start, stop=stop, skip_group_check=True), reads, writes)

        pT = ps(top, [128, 1024], BF16, "pT")
        pA = ps(top, [128, 512], F32, "pA"); pB = ps(top, [128, 512], F32, "pB")
        pS = [ps(top, [128, 512], F32, "pS%d" % i) for i in range(2)]
        pO = [ps(top, [128, 512], F32, "pO%d" % i) for i in range(3)]
        pT3 = pT[:].rearrange("p (c t) -> p c t", c=8)

        ident_f = sb(top, [128, 128], F32, "identf"); ident_b = sb(top, [128, 128], BF16, "identb")
        tri_f = sb(top, [128, 128], F32, "tri"); suf_f = sb(top, [128, 128], F32, "suf")
        ones_f = sb(top, [128, 128], F32, "onesf"); ones_b = sb(top, [128, 128], BF16, "onesb")
        maskT = sb(top, [128, 128], BF16, "maskT"); iota_c = sb(top, [128, 1], F32, "iota")
        selfb = sb(top, [128, 12], F32, "selfb")
        bm = [sb(top, [12, SW], F32, "bmf"), sb(top, [12, SW], F32, "bmd")]
        eps_c = sb(top, [128, 1], F32, "epsc")
        nlam = sb(top, [128, 1], F32, "nlam")
        gsub_c = sb(top, [128, 1], F32, "gsubc")
        bf_bc = sb(top, [128, 12], F32, "bfbc")
        OTs_self = sb(top, [128, 12, NS], BF16, "OTs_self")
        OTs_mem = sb(top, [64, 4, NS], BF16, "OTs_mem")

        dma('sync', ident_f[:], c_ident, [], ['identf'])
        if 'nocast' in os.environ.get('KVAR', ''):
            V_(lambda e: e.tensor_copy(out=ident_b[:], in_=ident_f[:]), ['identf'], ['identb'])
        else:
            dma('gpsimd', ident_b[:], c_ident, [], ['identb'])
        dma('sync', tri_f[:], c_tri, [], ['tri'])
        dma('sync', suf_f[:], c_suf, [], ['suf'])
        if 'nocast' in os.environ.get('KVAR', ''):
            dma('sync', tri_f[:], c_maskT, [], ['tri'])
            V_(lambda e: e.tensor_copy(out=maskT[:], in_=tri_f[:]), ['tri'], ['maskT'])
        else:
            dma('gpsimd', maskT[:], c_maskT, [], ['maskT'])
        dma('sync', iota_c[:], c_iota, [], ['iota'])
        dma('sync', selfb[:], c_selfb, [], ['selfb'])
        dma('sync', bm[0][:], c_bm_fox, [], ['bm0'])
        dma('sync', bm[1][:], c_bm_diff, [], ['bm1'])
        dma('sync', gsub_c[:], gsub, [], ['gsubc'])
        dma('sync', bf_bc[:], b_f.to_broadcast([128, 12]), [], ['bfbc'])
        V_(lambda e: e.memset(ones_f[:], 1.0), [], ['onesf'])
        V_(lambda e: e.memset(ones_b[:], 1.0), [], ['onesb'])
        V_(lambda e: e.memset(eps_c[:], EPS), [], ['epsc'])

        lam_init = 0.8 - 0.6 * math.exp(-0.3 * 1)
        if True:
            st = top
            lv = sb(st, [128, 4, 64], F32, "lv")
            for j in range(4):
                dma('sync', lv[:, j, :], lamv[j:j + 1, :].to_broadcast([128, 64]), [], ['lv%d' % j])
            pr = sb(st, [128, 2, 64], F32, "lpr"); sm = sb(st, [128, 2], F32, "lsm"); ex = sb(st, [128, 2], F32, "lex")
            V_(lambda e: e.tensor_tensor(out=pr[:, 0, :], in0=lv[:, 0, :], in1=lv[:, 1, :], op=ALU.mult), ['lv0', 'lv1'], ['lpr0'])
            V_(lambda e: e.tensor_tensor(out=pr[:, 1, :], in0=lv[:, 2, :], in1=lv[:, 3, :], op=ALU.mult), ['lv2', 'lv3'], ['lpr1'])
            V_(lambda e: e.tensor_reduce(out=sm[:], in_=pr[:], axis=AX.X, op=ALU.add), ['lpr0', 'lpr1'], ['lsm'])
            A_(lambda e: e.activation(out=ex[:], in_=sm[:], func=AF.Exp), ['lsm'], ['lex'])
            V_(lambda e: e.tensor_tensor(out=nlam[:], in0=ex[:, 1:2], in1=ex[:, 0:1], op=ALU.subtract), ['lex'], ['nlam'])
            V_(lambda e: e.tensor_scalar(out=nlam[:], in0=nlam[:], scalar1=-lam_init, scalar2=None, op0=ALU.add), ['nlam'], ['nlam'])
            V_(lambda e: e.tensor_scalar(out=gsub_c[:], in0=gsub_c[:], scalar1=(1.0 - lam_init), scalar2=None, op0=ALU.mult), ['gsubc'], ['gsubc'])

        cnt = {'ev': 0, 'pab': 0, 'q': 0}

        def ldq():
            cnt['q'] += 1
            return 'sync'

        n_junk = sb(top, [128, D], BF16, "njunk"); n_ss = sb(top, [128, 1], F32, "nss")
        n_rstd = sb(top, [128, 1], F32, "nrstd"); n_xn = sb(top, [128, D], BF16, "nxn")
        PTK = ['pT'] * 8

        def norm_from(st_h, r, gcol, gkey, xnT, xkey, tok0, hkey):
            junk, ss, rstd, xn = n_junk, n_ss, n_rstd, n_xn
            if 'noaccum' not in os.environ.get('KVAR', ''):
                A_(lambda e: e.activation(out=junk[0:r, :], in_=st_h[0:r, :], func=AF.Square, accum_out=ss[0:r, :]), [hkey], ['nss', 'njunk'])
            else:
                A_(lambda e: e.activation(out=junk[0:r, :], in_=st_h[0:r, :], func=AF.Square), [hkey], ['njunk'])
                V_(lambda e: e.tensor_reduce(out=ss[0:r, :], in_=junk[0:r, :], axis=AX.X, op=ALU.add), ['njunk'], ['nss'])
            V_(lambda e: e.tensor_scalar(out=ss[0:r, :], in0=ss[0:r, :], scalar1=1.0 / D, scalar2=EPS, op0=ALU.mult, op1=ALU.add), ['nss'], ['nss'])
            A_(lambda e: e.activation(out=ss[0:r, :], in_=ss[0:r, :], func=AF.Sqrt), ['nss'], ['nss'])
            V_(lambda e: e.reciprocal(out=rstd[0:r, :], in_=ss[0:r, :]), ['nss'], ['nrstd'])
            V_(lambda e: e.tensor_scalar(out=xn[0:r, :], in0=st_h[0:r, :], scalar1=rstd[0:r, :], scalar2=None, op0=ALU.mult), [hkey, 'nrstd'], ['nxn'])
            for c in range(8):
                T_(lambda e, c=c: e.transpose(pT3[:, c, 0:r], xn[0:r, c * 128:(c + 1) * 128], ident_b[0:r, 0:r]),
                   ['nxn', 'identb'], [PTK[c]])
            for c in range(8):
                if False:
                    A_(lambda e, c=c: e.activation(out=xnT[:, c, tok0:tok0 + r], in_=pT3[:, c, 0:r], func=AF.Identity, scale=gcol[:, c:c + 1]),
                       [PTK[c], gkey], [xkey + '_%d' % c])
                else:
                    V_(lambda e, c=c: e.tensor_scalar(out=xnT[:, c, tok0:tok0 + r], in0=pT3[:, c, 0:r], scalar1=gcol[:, c:c + 1], scalar2=None, op0=ALU.mult),
                       [PTK[c], gkey], [xkey + '_%d' % c])

        def norm_T(st_h, src_ap, r, gcol, gkey, xnT, xkey, tok0, hkey, srckeys=()):
            dma(ldq(), st_h[0:r, :], src_ap, list(srckeys), [hkey])
            norm_from(st_h, r, gcol, gkey, xnT, xkey, tok0, hkey)

        def xkeys(xkey):
            return [xkey + '_%d' % c for c in range(8)]

        def next_pab():
            cnt['pab'] += 1
            return (pA, 'pA') if cnt['pab'] % 2 else (pB, 'pB')

        def gemm_tm(xnT, xkey, tok0, r, W, wkey, col0, n, pt_, pkey):
            for c in range(8):
                mm(pt_[0:r, 0:n], xnT[:, c, tok0:tok0 + r], W[:, c, col0:col0 + n], c == 0, c == 7,
                   xkeys(xkey) + [wkey], [pkey])

        def load_w_bf16(W, wkey, src, rows_total, ncols, kchunks, prow=128):
            for c in range(kchunks):
                dma('gpsimd', W[0:prow, c, 0:ncols], src[c * prow:(c + 1) * prow, 0:ncols], [], [wkey])

        pn_ss = sb(top, [128, 2], F32, "pss"); pn_junk = sb(top, [128, 512], BF16, "pj"); pn_rstd = sb(top, [128, 1], F32, "prs")

        def post_norm_residual(st, y_halves, ykeys, r, gp, gpkey, hres, hkey, outt, okey):
            ss, junk, rstd = pn_ss, pn_junk, pn_rstd
            k = 'pn'
            for hf in range(2):
                A_(lambda e, hf=hf: e.activation(out=junk[0:r, :], in_=y_halves[hf][0:r, :], func=AF.Square, accum_out=ss[0:r, hf:hf + 1]),
                   [ykeys[hf]], [k + 'ss%d' % hf, k + 'j'])
            V_(lambda e: e.tensor_tensor(out=rstd[0:r, :], in0=ss[0:r, 0:1], in1=ss[0:r, 1:2], op=ALU.add), [k + 'ss0', k + 'ss1'], [k + 'r'])
            V_(lambda e: e.tensor_scalar(out=rstd[0:r, :], in0=rstd[0:r, :], scalar1=1.0 / D, scalar2=EPS, op0=ALU.mult, op1=ALU.add), [k + 'r'], [k + 'r'])
            A_(lambda e: e.activation(out=rstd[0:r, :], in_=rstd[0:r, :], func=AF.Sqrt), [k + 'r'], [k + 'r'])
            V_(lambda e: e.reciprocal(out=rstd[0:r, :], in_=rstd[0:r, :]), [k + 'r'], [k + 'r'])
            for hf in range(2):
                sl = slice(hf * 512, (hf + 1) * 512)
                V_(lambda e, hf=hf, sl=sl: e.scalar_tensor_tensor(out=outt[0:r, sl], in0=y_halves[hf][0:r, :], scalar=rstd[0:r, :], in1=gp[0:r, sl], op0=ALU.mult, op1=ALU.mult),
                   [ykeys[hf], k + 'r', gpkey], [okey + '%d' % hf])
                G_(lambda e, sl=sl: e.tensor_tensor(out=outt[0:r, sl], in0=outt[0:r, sl], in1=hres[0:r, sl], op=ALU.add),
                   [okey + '%d' % hf, hkey], [okey + '%d' % hf])

        def stop_if(tag):
            if KSTOP == tag:
                raise _Stop()

        for L in (range(2) if not KSTOP.startswith('pre') else []):
          try:
              fox = (L == 0)
              NIN = 2572 if fox else 2560
              QMC = 2316 if fox else 2304
              h_src = xp if L == 0 else h1_d[0:S, :]
              hs_src = xs if L == 0 else h1_d[S:S + NS, :]
              h_dst = h1_d[0:S, :] if L == 0 else o_yp
              hs_dst = h1_d[S:S + NS, :] if L == 0 else o_ys
              ck, cv = (cfk, cfv) if fox else (cdk, cdv)
              lay = ExitStack()
              with lay:
                  KT = sb(lay, [128, 6, S], BF16, "KT")
                  if fox:
                      VV = sb(lay, [128, NT, 12, 65], BF16, "VV")
                      G_(lambda e: e.memset(VV[:, :, :, 64:65], 1.0), [], ['VVones'])
                      negC = sb(lay, [128, NT, 12], F32, "negC"); Cend = sb(lay, [128, NT, 12], F32, "Cend")
                      Rrun = sb(lay, [128, 12], F32, "Rrun")
                      V_(lambda e: e.memset(Rrun[:], 0.0), [], ['Rrun'])
                  else:
                      VV = sb(lay, [128, NT, SW], BF16, "VV")
                  MKT = sb(lay, [128, 2, 256], BF16, "MKT"); MV = sb(lay, [128, 2, 4, 65], BF16, "MV")
                  G_(lambda e: e.memset(MV[:, :, :, 64:65], 1.0), [], ['MVones'])
                  gc = sb(lay, [128, 3, 8], F32, "gc")
                  for j in range(3):
                      dma('sync', gc[:, j, :], gcols[L, j], [], ['gc%d' % j])

                  p1 = ExitStack()
                  with p1:
                      Win = sb(p1, [128, 8, NIN], BF16, "Win")
                      load_w_bf16(Win, 'Win', w_in[L], D, NIN, 8)
                      Wm = sb(p1, [128, 8, 512], BF16, "Wm")
                      load_w_bf16(Wm, 'Wm', w_mem[L], D, 512, 8)
                      xnT = sb(p1, [128, 8, 128], BF16, "xnT")
                      ht = sb(p1, [128, D], F32, "ht")
                      zq = sb(p1, [128, SW], F32, "zq"); zk = sb(p1, [128, SW], F32, "zk"); zv = sb(p1, [128, SW], F32, "zv")
                      zr = sb(p1, [128, 268], F32, "zr")
                      qb_ = sb(p1, [128, SW], BF16, "qb"); kb_ = sb(p1, [128, SW], BF16, "kb"); mb_ = sb(p1, [128, 256], BF16, "mb")
                      qT = sb(p1, [128, 8, 128], BF16, "qTt")
                      cs = sb(p1, [128, 384], F32, "cs"); sn = sb(p1, [128, 384], F32, "sn")
                      t1 = sb(p1, [128, 384], F32, "t1"); t2 = sb(p1, [128, 384], F32, "t2")
                      lf = sb(p1, [128, 12], F32, "lf"); la = sb(p1, [128, 12], F32, "la"); lb = sb(p1, [128, 12], F32, "lb")

                      stop_if('P1w')
                      for mt in range(2):
                          norm_T(ht, memp[mt * 128:(mt + 1) * 128, :], 128, gc[:, 2, :], 'gc2', xnT, 'xnT', 0, 'ht')
                          stop_if('P1n')
                          gemm_tm(xnT, 'xnT', 0, 128, Wm, 'Wm', 0, 512, pA, 'pA')
                          stop_if('P1g')
                          V_(lambda e: e.tensor_copy(out=zq[:, 0:512], in_=pA[:, :]), ['pA'], ['zq'])
                          dma('sync', o_mk[L, mt * 128:(mt + 1) * 128, :], zq[:, 0:256], ['zq'], [])
                          dma('sync', o_mv[L, mt * 128:(mt + 1) * 128, :], zq[:, 256:512], ['zq'], [])
                          stop_if('P1o')
                          if os.environ.get('KACT') == 'dve':
                              V_(lambda e: e.tensor_copy(out=mb_[:, :], in_=pA[:, 0:256]), ['pA'], ['mb'])
                              V_(lambda e, mt=mt: e.tensor_copy(out=MV[:, mt, :, 0:64], in_=pA[:, 256:512].rearrange("p (h d) -> p h d", h=4)), ['pA', 'MVones'], ['MV'])
                          elif os.environ.get('KACT') == 'zq':
                              A_(lambda e: e.activation(out=mb_[:, :], in_=zq[:, 0:256], func=AF.Identity), ['zq'], ['mb'])
                              A_(lambda e, mt=mt: e.activation(out=MV[:, mt, :, 0:64], in_=zq[:, 256:512].rearrange("p (h d) -> p h d", h=4), func=AF.Identity), ['zq', 'MVones'], ['MV'])
                          else:
                              A_(lambda e: e.activation(out=mb_[:, :], in_=pA[:, 0:256], func=AF.Identity), ['pA'], ['mb'])
                              stop_if('P1v1')
                              A_(lambda e, mt=mt: e.activation(out=MV[:, mt, :, 0:64], in_=pA[:, 256:512].rearrange("p (h d) -> p h d", h=4), func=AF.Identity), ['pA', 'MVones'], ['MV'])
                          stop_if('P1v')
                          for g in range(2):
                              T_(lambda e, g=g: e.transpose(pT3[:, g, :], mb_[:, g * 128:(g + 1) * 128], ident_b[:, :]), ['mb', 'identb'], [PTK[g]])
                          V_(lambda e, mt=mt: e.tensor_copy(out=MKT[:, :, mt * 128:(mt + 1) * 128], in_=pT3[:, 0:2, :]), PTK[0:2], ['MKT'])

                      stop_if('P1m')
                      for t in range(NT + 1):
                          smp = (t == NT)
                          r = NS if smp else 128
                          src = hs_src if smp else h_src[t * 128:(t + 1) * 128, :]
                          norm_T(ht, src, r, gc[:, 0, :], 'gc0', xnT, 'xnT', 0, 'ht', ['h1_d'])
                          if not fox:
                              dma(ldq(), cs[0:r, :], (c_cos_s if smp else c_cos[t * 128:(t + 1) * 128, :]), [], ['cs'])
                              dma(ldq(), sn[0:r, :], (c_sin_s if smp else c_sin[t * 128:(t + 1) * 128, :]), [], ['sn'])
                          for which in range(3):
                              zt = (zq, zk, zv)[which]; zkey = ('zq', 'zk', 'zv')[which]
                              for (c0, n) in ((0, 512), (512, 256)):
                                  pp, pk = next_pab()
                                  gemm_tm(xnT, 'xnT', 0, r, Win, 'Win', which * SW + c0, n, pp, pk)
                                  if fox or which == 2:
                                      A_(lambda e, pp=pp, zt=zt, c0=c0, n=n: e.activation(out=zt[0:r, c0:c0 + n], in_=pp[0:r, 0:n], func=AF.Identity), [pk], [zkey])
                                  else:
                                      nh = n // 64; h0 = c0 // 64
                                      x = pp[0:r, 0:n].rearrange("p (h two d) -> p h two d", two=2, d=32)
                                      o = zt[0:r, c0:c0 + n].rearrange("p (h two d) -> p h two d", two=2, d=32)
                                      cc = cs[0:r, h0 * 32:(h0 + nh) * 32].rearrange("p (h d) -> p h d", d=32)
                                      sc = sn[0:r, h0 * 32:(h0 + nh) * 32].rearrange("p (h d) -> p h d", d=32)
                                      a1 = t1[0:r, 0:nh * 32].rearrange("p (h d) -> p h d", d=32)
                                      a2 = t2[0:r, 0:nh * 32].rearrange("p (h d) -> p h d", d=32)
                                      V_(lambda e, x=x, cc=cc, a1=a1: e.tensor_tensor(out=a1, in0=x[:, :, 0, :], in1=cc, op=ALU.mult), [pk, 'cs'], ['t1'])
                                      V_(lambda e, x=x, sc=sc, a2=a2: e.tensor_tensor(out=a2, in0=x[:, :, 1, :], in1=sc, op=ALU.mult), [pk, 'sn'], ['t2'])
                                      V_(lambda e, o=o, a1=a1, a2=a2: e.tensor_tensor(out=o[:, :, 0, :], in0=a1, in1=a2, op=ALU.subtract), ['t1', 't2'], [zkey])
                                      V_(lambda e, x=x, sc=sc, a1=a1: e.tensor_tensor(out=a1, in0=x[:, :, 0, :], in1=sc, op=ALU.mult), [pk, 'sn'], ['t1'])
                                      V_(lambda e, x=x, cc=cc, a2=a2: e.tensor_tensor(out=a2, in0=x[:, :, 1, :], in1=cc, op=ALU.mult), [pk, 'cs'], ['t2'])
                                      V_(lambda e, o=o, a1=a1, a2=a2: e.tensor_tensor(out=o[:, :, 1, :], in0=a1, in1=a2, op=ALU.add), ['t1', 't2'], [zkey])
                          nrest = NIN - 3 * SW
                          pp, pk = next_pab()
                          gemm_tm(xnT, 'xnT', 0, r, Win, 'Win', 3 * SW, nrest, pp, pk)
                          A_(lambda e, pp=pp: e.activation(out=zr[0:r, 0:nrest], in_=pp[0:r, 0:nrest], func=AF.Identity), [pk], ['zr'])
                          if fox:
                              V_(lambda e: e.tensor_tensor(out=la[0:r, :], in0=zr[0:r, 0:12], in1=bf_bc[0:r, :], op=ALU.add), ['zr', 'bfbc'], ['la'])
                              V_(lambda e: e.tensor_scalar(out=lb[0:r, :], in0=la[0:r, :], scalar1=-1.0, scalar2=None, op0=ALU.mult), ['la'], ['lb'])
                              V_(lambda e: e.tensor_tensor(out=lb[0:r, :], in0=lb[0:r, :], in1=la[0:r, :], op=ALU.min), ['la', 'lb'], ['lb'])
                              A_(lambda e: e.activation(out=lb[0:r, :], in_=lb[0:r, :], func=AF.Exp), ['lb'], ['lb'])
                              A_(lambda e: e.activation(out=lb[0:r, :], in_=lb[0:r, :], func=AF.Ln, bias=1.0), ['lb'], ['lb'])
                              V_(lambda e: e.tensor_scalar(out=la[0:r, :], in0=la[0:r, :], scalar1=0.0, scalar2=None, op0=ALU.min), ['la'], ['la'])
                              V_(lambda e: e.tensor_tensor(out=lf[0:r, :], in0=la[0:r, :], in1=lb[0:r, :], op=ALU.subtract), ['la', 'lb'], ['lf'])
                          kcol = 0
                          if smp:
                              dma('sync', zs_d[:, 0:SW], zq[0:NS, :], ['zq'], ['zs_d'])
                              dma('sync', zs_d[:, SW:2 * SW], zk[0:NS, :], ['zk'], ['zs_d'])
                              dma('sync', zs_d[:, 2 * SW:3 * SW], zv[0:NS, :], ['zv'], ['zs_d'])
                              dma('sync', zs_d[:, 3 * SW:NIN], zr[0:NS, 0:nrest], ['zr'], ['zs_d'])
                              dma('sync', o_ks[L], zk[0:NS, :], ['zk'], [])
                              dma('sync', o_vs[L], zv[0:NS, :], ['zv'], [])
                              if fox:
                                  dma('sync', zs_d[:, 3 * SW:3 * SW + 12], lf[0:NS, :], ['lf', 'zs_d'], ['zs_d'])
                                  dma('sync', o_fls, lf[0:NS, :], ['lf'], [])
                              continue
                          ts_ = slice(t * 128, (t + 1) * 128)
                          dma('sync', o_kp[L][ts_, :], zk[:, :], ['zk'], [])
                          dma('sync', o_vp[L][ts_, :], zv[:, :], ['zv'], [])
                          V_(lambda e: e.tensor_copy(out=qb_[:, :], in_=zq[:, :]), ['zq'], ['qb'])
                          G_(lambda e: e.tensor_copy(out=kb_[:, :], in_=zk[:, :]), ['zk'], ['kb'])
                          V_(lambda e: e.tensor_copy(out=mb_[:, :], in_=zr[:, nrest - 256:nrest]), ['zr'], ['mb'])
                          if fox:
                              G_(lambda e, t=t: e.tensor_copy(out=VV[:, t, :, 0:64], in_=zv[:, :].rearrange("p (h d) -> p h d", h=12)), ['zv', 'VVones'], ['VV%d' % (t // 4)])
                          else:
                              G_(lambda e, t=t: e.tensor_copy(out=VV[:, t, :], in_=zv[:, :]), ['zv'], ['VV%d' % (t // 4)])
                          for g in range(6):
                              T_(lambda e, g=g: e.transpose(pT3[:, g, :], kb_[:, g * 128:(g + 1) * 128], ident_b[:, :]), ['kb', 'identb'], [PTK[g]])
                          V_(lambda e, t=t: e.tensor_copy(out=KT[:, :, t * 128:(t + 1) * 128], in_=pT3[:, 0:6, :]), PTK[0:6], ['KT%d' % (t // 4)])
                          for g in range(6):
                              T_(lambda e, g=g: e.transpose(pT3[:, g, :], qb_[:, g * 128:(g + 1) * 128], ident_b[:, :]), ['qb', 'identb'], [PTK[g]])
                          for g in range(2):
                              T_(lambda e, g=g: e.transpose(pT3[:, 6 + g, :], mb_[:, g * 128:(g + 1) * 128], ident_b[:, :]), ['mb', 'identb'], [PTK[6 + g]])
                          A_(lambda e: e.activation(out=qT[:, :, :], in_=pT3[:, :, :], func=AF.Identity), PTK, ['qT'])
                          dma('sync', qt_d[:, :, t * 128:(t + 1) * 128].rearrange("c p t -> p c t"), qT[:, :, :], ['qT'], ['qt_d%d' % (t // 4)])
                          if fox:
                              dma('sync', o_flp[ts_, :], lf[:, :], ['lf'], [])
                              mm(pO[0][:, 0:12], tri_f[:, :], lf[:, :], True, False, ['tri', 'lf'], ['pO0'])
                              mm(pO[0][:, 0:12], ones_f[:, :], Rrun[:, :], False, True, ['onesf', 'Rrun'], ['pO0'])
                              V_(lambda e, t=t: e.tensor_scalar(out=negC[:, t, :], in0=pO[0][:, 0:12], scalar1=-1.0, scalar2=None, op0=ALU.mult), ['pO0'], ['negC'])
                              V_(lambda e: e.tensor_tensor(out=Rrun[:, :], in0=Rrun[:, :], in1=lf[:, :], op=ALU.add), ['Rrun', 'lf'], ['Rrun'])
                              mm(pO[1][:, 0:12], ones_f[:, :], Rrun[:, :], True, True, ['onesf', 'Rrun'], ['pO1'])
                              V_(lambda e, t=t: e.tensor_copy(out=Cend[:, t, :], in_=pO[1][:, 0:12]), ['pO1'], ['Cend'])

                  stop_if('P1_%d' % L)
                  sch.barrier()
                  dec = ExitStack()
                  with dec:
                      PG = 2 if NPG % 2 == 0 else 1
                      NP1 = NPG + 1
                      Kb = [sb(dec, [128, PG, SW], F32, "Kb%d" % i) for i in range(3)]
                      Vb = [sb(dec, [128, PG, SW], F32, "Vb%d" % i) for i in range(3)]
                      Vbf = [sb(dec, [128, SW + 1], BF16, "Vbf%d" % i) for i in range(2)]
                      for i in range(2):
                          G_(lambda e, i=i: e.memset(Vbf[i][:, SW:SW + 1], 1.0), [], ['Vbf%d' % i])
                      qbc = sb(dec, [128, SW], F32, "qbc"); tmp = [sb(dec, [128, SW], F32, "dtmp%d" % i) for i in range(2)]
                      Ssc = sb(dec, [128, NP1, 12], F32, "Ssc"); Dd = sb(dec, [128, NP1, 12], F32, "Dd")
                      Pp = sb(dec, [128, NP1, 12], BF16, "Pp")
                      LF = sb(dec, [128, NPG, 12], F32, "LF"); E0 = sb(dec, [128, NPG, 12], F32, "E0"); E1 = sb(dec, [128, NPG, 12], F32, "E1")
                      ptb = sb(dec, [128, NPG], I32, "ptb"); ptf = sb(dec, [128, NPG], F32, "ptf"); idx = sb(dec, [128, NPG], I32, "idx")
                      lfn = sb(dec, [128, 12], F32, "lfn")
                      mx = sb(dec, [128, 12], F32, "mx"); mrow = sb(dec, [12, 1], F32, "mrow"); dg = sb(dec, [12, 12], F32, "dg")
                      nmb = sb(dec, [128, 12], F32, "nmb")
                      ob = sb(dec, [12, SW + 1], F32, "ob"); ob2 = sb(dec, [12, SW], F32, "ob2")
                      od = sb(dec, [12, 128], F32, "od"); rl = sb(dec, [12, 1], F32, "rl")
                      oT = sb(dec, [128, 12], F32, "oT"); o6 = sb(dec, [128, 6], F32, "o6")
                      o6t = sb(dec, [6, 128], F32, "o6t"); o6s = sb(dec, [6, 1], F32, "o6s"); o6j = sb(dec, [6, 128], F32, "o6j")
                      gsr = sb(dec, [6, 128], F32, "gsr")
                      dma('sync', gsr[:, :], gsub.rearrange("p o -> o p").to_broadcast([6, 128]), [], ['gsr'])
                      V_(lambda e: e.tensor_scalar(out=gsr[:, :], in0=gsr[:, :], scalar1=(1.0 - lam_init), scalar2=None, op0=ALU.mult), ['gsr'], ['gsr'])
                      mK = sb(dec, [128, 2, 256], F32, "mK"); mV = sb(dec, [128, 2, 256], F32, "mV")
                      mVb = sb(dec, [128, 257], BF16, "mVb")
                      G_(lambda e: e.memset(mVb[:, 256:257], 1.0), [], ['mVb'])
                      qmb = sb(dec, [128, 256], F32, "qmb"); mS = sb(dec, [128, 2, 4], F32, "mS"); mP = sb(dec, [128, 2, 4], BF16, "mP")

                      def softmax_pv(Sv, skey, Pv, pkey, npages, H, vload, vdim, bmt, bmkey, tagk):
                          V_(lambda e: e.tensor_reduce(out=mx[:, 0:H], in_=Sv.rearrange("p n h -> p h n"), axis=AX.X, op=ALU.max), [skey], ['mx'])
                          T_(lambda e: e.transpose(pA[0:H, 0:128], mx[:, 0:H], ident_f[:, :]), ['mx', 'identf'], ['pA'])
                          V_(lambda e: e.tensor_reduce(out=mrow[0:H, :], in_=pA[0:H, 0:128], axis=AX.X, op=ALU.max), ['pA'], ['mrow'])
                          V_(lambda e: e.tensor_scalar(out=dg[0:H, 0:H], in0=ident_f[0:H, 0:H], scalar1=mrow[0:H, :], scalar2=-1.0, op0=ALU.mult, op1=ALU.mult), ['mrow', 'identf'], ['dg'])
                          mm(pB[:, 0:H], ones_f[0:H, :], dg[0:H, 0:H], True, True, ['onesf', 'dg'], ['pB'])
                          V_(lambda e: e.tensor_copy(out=nmb[:, 0:H], in_=pB[:, 0:H]), ['pB'], ['nmb'])
                          for h in range(H):
                              A_(lambda e, h=h: e.activation(out=Pv[:, :, h], in_=Sv[:, :, h], func=AF.Exp, bias=nmb[:, h:h + 1]), [skey, 'nmb'], [pkey])
                          nv = vdim + 1
                          halves = [(0, min(512, nv))] + ([(512, nv)] if nv > 512 else [])
                          pacc = [pO[0], pO[1]]
                          for j in range(npages):
                              vb, vkey = vload(j)
                              for hi, (a, b) in enumerate(halves):
                                  mm(pacc[hi][0:H, 0:b - a], Pv[:, j, :], vb[:, a:b], j == 0, j == npages - 1, [pkey, vkey], ['pO%d' % hi])
                          for hi, (a, b) in enumerate(halves):
                              V_(lambda e, hi=hi, a=a, b=b: e.tensor_copy(out=ob[0:H, a:b], in_=pacc[hi][0:H, 0:b - a]), ['pO%d' % hi], ['ob'])
                          V_(lambda e: e.reciprocal(out=rl[0:H, :], in_=ob[0:H, vdim:vdim + 1]), ['ob'], ['rl'])
                          nblk = bmt[1]; bw = vdim // nblk
                          V_(lambda e: e.tensor_tensor(out=ob2[0:H, 0:vdim], in0=ob[0:H, 0:vdim], in1=bmt[0][0:H, 0:vdim], op=ALU.mult), ['ob', bmkey], ['ob2'])
                          V_(lambda e: e.tensor_reduce(out=od[0:H, 0:bw], in_=ob2[0:H, 0:vdim].rearrange("p (b d) -> p d b", b=nblk), axis=AX.X, op=ALU.add), ['ob2'], ['od'])
                          V_(lambda e: e.tensor_scalar(out=od[0:H, 0:bw], in0=od[0:H, 0:bw], scalar1=rl[0:H, :], scalar2=None, op0=ALU.mult), ['od', 'rl'], ['od'])
                          return bw

                      for s in range(NS):
                          dma('sync', ptb[:, :], pt[s:s + 1, :].to_broadcast([128, NPG]), [], ['ptb'])
                          V_(lambda e: e.tensor_copy(out=ptf[:, :], in_=ptb[:, :]), ['ptb'], ['ptf'])
                          V_(lambda e: e.tensor_scalar(out=ptf[:, :], in0=ptf[:, :], scalar1=128.0, scalar2=iota_c[:, :], op0=ALU.mult, op1=ALU.add), ['ptf', 'iota'], ['ptf'])
                          V_(lambda e: e.tensor_copy(out=idx[:, :], in_=ptf[:, :]), ['ptf'], ['idx'])
                          dma('sync', qbc[:, :], zs_d[s:s + 1, 0:SW].to_broadcast([128, SW]), ['zs_d'], ['qbc'])
                          if fox:
                              for j in range(NPG):
                                  sch.op('gpsimd', lambda e, j=j: e.indirect_dma_start(out=LF[:, j, :], out_offset=None, in_=cfl, in_offset=bass.IndirectOffsetOnAxis(ap=idx[:, j:j + 1], axis=0)), ['idx'], ['LF'], dma=True)
                              dma('sync', lfn[:, :], zs_d[s:s + 1, 3 * SW:3 * SW + 12].to_broadcast([128, 12]), ['zs_d'], ['lfn'])
                              LF2 = LF[:, :, :].rearrange("p n h -> p (n h)")
                              nn = NPG * 12
                              for (a, b) in [(i, min(i + 512, nn)) for i in range(0, nn, 512)]:
                                  mm(pA[:, 0:b - a], suf_f[:, :], LF2[:, a:b], True, True, ['suf', 'LF'], ['pA'])
                                  V_(lambda e, a=a, b=b: e.tensor_copy(out=Dd[:, 0:NPG, :].rearrange("p n h -> p (n h)")[:, a:b], in_=pA[:, 0:b - a]), ['pA'], ['Dd'])
                                  mm(pB[:, 0:b - a], ones_f[:, :], LF2[:, a:b], True, True, ['onesf', 'LF'], ['pB'])
                                  V_(lambda e, a=a, b=b: e.tensor_copy(out=E1[:, :, :].rearrange("p n h -> p (n h)")[:, a:b], in_=pB[:, 0:b - a]), ['pB'], ['E1'])
                              V_(lambda e: e.memset(E0[:, :, :], 0.0), [], ['E0'])
                              if NPG > 1:
                                  V_(lambda e: e.tensor_copy(out=E0[:, 0:NPG - 1, :], in_=E1[:, 1:NPG, :]), ['E1', 'E0'], ['E0'])
                              cur, oth, ck_, ok_ = E0, E1, 'E0', 'E1'
                              sh = 1
                              while sh < NPG:
                                  V_(lambda e, cur=cur, oth=oth: e.tensor_copy(out=oth[:, :, :], in_=cur[:, :, :]), [ck_], [ok_])
                                  V_(lambda e, cur=cur, oth=oth, sh=sh: e.tensor_tensor(out=oth[:, 0:NPG - sh, :], in0=cur[:, 0:NPG - sh, :], in1=cur[:, sh:NPG, :], op=ALU.add), [ck_, ok_], [ok_])
                                  cur, oth, ck_, ok_ = oth, cur, ok_, ck_
                                  sh *= 2
                              V_(lambda e, cur=cur: e.tensor_tensor(out=Dd[:, 0:NPG, :], in0=Dd[:, 0:NPG, :], in1=cur[:, :, :], op=ALU.add), ['Dd', ck_], ['Dd'])
                              for j in range(NPG):
                                  V_(lambda e, j=j: e.tensor_tensor(out=Dd[:, j, :], in0=Dd[:, j, :], in1=lfn[:, :], op=ALU.add), ['Dd', 'lfn'], ['Dd'])
                          else:
                              V_(lambda e: e.memset(Dd[:, 0:NPG, :], 0.0), [], ['Dd'])
                          V_(lambda e: e.tensor_copy(out=Dd[:, NPG, :], in_=selfb[:, :]), ['selfb', 'Dd'], ['Dd'])

                          nch = NPG // PG
                          for chn in range(nch + 1):
                              kb = Kb[chn % 3]; kkey = 'Kb%d' % (chn % 3)
                              if chn < nch:
                                  for p in range(PG):
                                      j = chn * PG + p
                                      sch.op('gpsimd', lambda e, j=j, p=p, kb=kb: e.indirect_dma_start(out=kb[:, p, :], out_offset=None, in_=ck, in_offset=bass.IndirectOffsetOnAxis(ap=idx[:, j:j + 1], axis=0)), ['idx'], [kkey], dma=True)
                                  pages = [(chn * PG + p, p) for p in range(PG)]
                              else:
                                  G_(lambda e, kb=kb: e.memset(kb[:, 0, :], 0.0), [], [kkey])
                                  dma('sync', kb[0:1, 0, :], zs_d[s:s + 1, SW:2 * SW], ['zs_d', kkey], [kkey])
                                  pages = [(NPG, 0)]
                              for (j, p) in pages:
                                  tm_ = tmp[j % 2]; tk = 'dtmp%d' % (j % 2)
                                  V_(lambda e, kb=kb, p=p, tm_=tm_: e.tensor_tensor(out=tm_[:, :], in0=kb[:, p, :], in1=qbc[:, :], op=ALU.mult), [kkey, 'qbc'], [tk])
                                  V_(lambda e, j=j, tm_=tm_: e.tensor_reduce(out=Ssc[:, j, :], in_=tm_[:, :].rearrange("p (h d) -> p h d", d=64), axis=AX.X, op=ALU.add), [tk], ['Ssc'])
                          V_(lambda e: e.scalar_tensor_tensor(out=Ssc[:, :, :], in0=Ssc[:, :, :], scalar=0.125, in1=Dd[:, :, :], op0=ALU.mult, op1=ALU.add), ['Ssc', 'Dd'], ['Ssc'])

                          def vload(j, s=s):
                              chn, p = (j // PG, j % PG) if j < NPG else (nch, 0)
                              vb = Vb[chn % 3]; vkey = 'Vb%d' % (chn % 3)
                              if p == 0:
                                  if j < NPG:
                                      for pp_ in range(PG):
                                          jj = chn * PG + pp_
                                          sch.op('gpsimd', lambda e, jj=jj, pp_=pp_, vb=vb: e.indirect_dma_start(out=vb[:, pp_, :], out_offset=None, in_=cv, in_offset=bass.IndirectOffsetOnAxis(ap=idx[:, jj:jj + 1], axis=0)), ['idx'], [vkey], dma=True)
                                  else:
                                      G_(lambda e, vb=vb: e.memset(vb[:, 0, :], 0.0), [], [vkey])
                                      dma('sync', vb[0:1, 0, :], zs_d[s:s + 1, 2 * SW:3 * SW], ['zs_d', vkey], [vkey])
                              vf = Vbf[j % 2]; fk = 'Vbf%d' % (j % 2)
                              A_(lambda e, vb=vb, p=p, vf=vf: e.activation(out=vf[:, 0:SW], in_=vb[:, p, :], func=AF.Identity), [vkey], [fk])
                              return vf, fk

                          bw = softmax_pv(Ssc[:, :, :], 'Ssc', Pp, 'Pp', NP1, 12, vload, SW, (bm[L], 12 if fox else 6), 'bm%d' % L, 'a')
                          if fox:
                              T_(lambda e: e.transpose(pA[0:64, 0:12], od[0:12, 0:64], ident_f[0:12, 0:12]), ['od', 'identf'], ['pA'])
                              V_(lambda e, s=s: e.tensor_copy(out=OTs_self[0:64, :, s], in_=pA[0:64, 0:12]), ['pA'], ['OTs_self'])
                          else:
                              T_(lambda e: e.transpose(pA[:, 0:12], od[0:12, 0:128], ident_f[0:12, 0:12]), ['od', 'identf'], ['pA'])
                              V_(lambda e: e.tensor_copy(out=oT[:, :], in_=pA[:, 0:12]), ['pA'], ['oT'])
                              oTv = oT[:, :].rearrange("p (h two) -> p h two", two=2)
                              V_(lambda e: e.scalar_tensor_tensor(out=o6[:, :], in0=oTv[:, :, 1], scalar=nlam[:, :], in1=oTv[:, :, 0], op0=ALU.mult, op1=ALU.add), ['oT', 'nlam'], ['o6'])
                              T_(lambda e: e.transpose(pB[0:6, 0:128], o6[:, :], ident_f[:, :]), ['o6', 'identf'], ['pB'])
                              V_(lambda e: e.tensor_copy(out=o6t[:, :], in_=pB[0:6, 0:128]), ['pB'], ['o6t'])
                              V_(lambda e: e.tensor_tensor(out=o6j[:, :], in0=o6t[:, :], in1=o6t[:, :], op=ALU.mult), ['o6t'], ['o6j'])
                              V_(lambda e: e.tensor_reduce(out=o6s[:, :], in_=o6j[:, :], axis=AX.X, op=ALU.add), ['o6j'], ['o6s'])
                              V_(lambda e: e.tensor_scalar(out=o6s[:, :], in0=o6s[:, :], scalar1=1.0 / 128, scalar2=EPS, op0=ALU.mult, op1=ALU.add), ['o6s'], ['o6s'])
                              A_(lambda e: e.activation(out=o6s[:, :], in_=o6s[:, :], func=AF.Sqrt), ['o6s'], ['o6s'])
                              V_(lambda e: e.reciprocal(out=o6s[:, :], in_=o6s[:, :]), ['o6s'], ['o6s'])
                              V_(lambda e: e.scalar_tensor_tensor(out=o6t[:, :], in0=o6t[:, :], scalar=o6s[:, :], in1=gsr[:, :], op0=ALU.mult, op1=ALU.mult), ['o6t', 'o6s', 'gsr'], ['o6t'])
                              T_(lambda e: e.transpose(pA[:, 0:6], o6t[:, :], ident_f[0:6, 0:6]), ['o6t', 'identf'], ['pA'])
                              V_(lambda e, s=s: e.tensor_copy(out=OTs_self[:, 0:6, s], in_=pA[:, 0:6]), ['pA'], ['OTs_self'])

                          dma('sync', qmb[:, :], zs_d[s:s + 1, QMC:QMC + 256].to_broadcast([128, 256]), ['zs_d'], ['qmb'])
                          dma('sync', mK[:, :, :], cmk[L, s].rearrange("(t p) f -> p t f", p=128), [], ['mK'])
                          dma('sync', mV[:, :, :], cmv[L, s].rearrange("(t p) f -> p t f", p=128), [], ['mV'])
                          for mt in range(2):
                              G_(lambda e, mt=mt: e.tensor_tensor(out=tmp[mt][:, 0:256], in0=mK[:, mt, :], in1=qmb[:, :], op=ALU.mult), ['mK', 'qmb'], ['dtmp%d' % mt])
                              V_(lambda e, mt=mt: e.tensor_reduce(out=mS[:, mt, :], in_=tmp[mt][:, 0:256].rearrange("p (h d) -> p h d", d=64), axis=AX.X, op=ALU.add), ['dtmp%d' % mt], ['mS'])
                          V_(lambda e: e.tensor_scalar(out=mS[:, :, :], in0=mS[:, :, :], scalar1=0.125, scalar2=None, op0=ALU.mult), ['mS'], ['mS'])

                          def mvload(j):
                              A_(lambda e, j=j: e.activation(out=mVb[:, 0:256], in_=mV[:, j, :], func=AF.Identity), ['mV', 'mVb'], ['mVb'])
                              return mVb, 'mVb'
                          softmax_pv(mS[:, :, :], 'mS', mP, 'mP', 2, 4, mvload, 256, (bm[0], 4), 'bm0', 'm')
                          T_(lambda e: e.transpose(pA[0:64, 0:4], od[0:4, 0:64], ident_f[0:4, 0:4]), ['od', 'identf'], ['pA'])
                          V_(lambda e, s=s: e.tensor_copy(out=OTs_mem[0:64, :, s], in_=pA[0:64, 0:4]), ['pA'], ['OTs_mem'])

                  stop_if('DEC_%d' % L)
                  sch.barrier()
                  p2 = ExitStack()
                  with p2:
                      nsc = 12 if fox else 6
                      pk_ = 64 if fox else 128
                      Wo = sb(p2, [128, nsc, D], BF16, "Wo"); Wom = sb(p2, [64, 4, D], BF16, "Wom")
                      for c in range(nsc):
                          dma('gpsimd', Wo[0:pk_, c, :], w_out[L, c * pk_:(c + 1) * pk_, :], [], ['Wo'])
                      for c in range(4):
                          dma('gpsimd', Wom[0:64, c, :], w_out[L, SW + c * 64:SW + (c + 1) * 64, :], [], ['Wom'])
                      gp = sb(p2, [128, D], F32, "gp")
                      dma('sync', gp[:, :], gpost[L, 0:1, :].to_broadcast([128, D]), [], ['gp'])
                      QT = sb(p2, [128, 8, 512], BF16, "QT")
                      PT = [sb(p2, [128, 512], BF16, "PT%d" % i) for i in range(3)]
                      OT = sb(p2, [128, nsc, 512], BF16, "OT"); OTm = sb(p2, [64, 4, 512], BF16, "OTm")
                      Bq = sb(p2, [128, 4, 12, max(NT, 1)], F32, "Bq")
                      rlt = sb(p2, [128, 512], F32, "rlt"); rbc = sb(p2, [128, 512], F32, "rbc")
                      on0 = sb(p2, [128, 512], F32, "on0"); on1 = sb(p2, [128, 512], F32, "on1"); sq = sb(p2, [128, 512], BF16, "sq")
                      hres = sb(p2, [128, D], F32, "hres"); hout = sb(p2, [128, D], F32, "hout")
                      sc_ = {'i': 0, 'o': 0}

                      def attend(qb, Kt, kpart, kg, vfn, nkt, bias_fn, causal, qsrc, qg, out_fn, aug):
                          sc_['o'] += 1
                          po = pO[sc_['o'] % 2]; pok = 'pO%d' % (sc_['o'] % 2)
                          pSS = [pS[0], pS[1], pO[2]]; pSK = ['pS0', 'pS1', 'pO2']
                          if not aug:
                              pl, plk = next_pab()
                          base = sc_['i']
                          sc_['i'] += nkt

                          def emitS(kt):
                              u = base + kt + 1
                              qlo = max(0, kt - 4 * qb) if causal else 0
                              c0 = qlo * 128
                              psb = pSS[u % 3]; psk = pSK[u % 3]
                              mm(psb[:, c0:512], Kt[kpart, kg, kt * 128:(kt + 1) * 128], qsrc[kpart, qg, c0:512], True, True,
                                 ['KT%d' % (kt // 4) if causal else 'MKT', 'QT'], [psk])

                          emitS(0)
                          if nkt > 1:
                              emitS(1)
                          for kt in range(nkt):
                              if kt + 2 < nkt:
                                  emitS(kt + 2)
                              u = base + kt + 1
                              qlo = max(0, kt - 4 * qb) if causal else 0
                              c0 = qlo * 128
                              psb = pSS[u % 3]; psk = pSK[u % 3]
                              ptb_ = PT[u % 3]; ptk = 'PT%d' % (u % 3)
                              pks = [ptk + '_%d' % q_ for q_ in range(qlo, 4)]
                              if bias_fn is None:
                                  A_(lambda e, psb=psb, ptb_=ptb_, c0=c0: e.activation(out=ptb_[:, c0:512], in_=psb[:, c0:512], func=AF.Exp, scale=0.125), [psk], pks)
                              else:
                                  for qt in range(qlo, 4):
                                      A_(lambda e, psb=psb, ptb_=ptb_, qt=qt, kt=kt: e.activation(out=ptb_[:, qt * 128:(qt + 1) * 128], in_=psb[:, qt * 128:(qt + 1) * 128], func=AF.Exp, scale=0.125, bias=bias_fn(qt, kt)), [psk, 'Bq'], [ptk + '_%d' % qt])
                              if causal and kt >= 4 * qb:
                                  G_(lambda e, ptb_=ptb_, c0=c0: e.tensor_tensor(out=ptb_[:, c0:c0 + 128], in0=ptb_[:, c0:c0 + 128], in1=maskT[:, :], op=ALU.mult), [pks[0], 'maskT'], [pks[0]])
                              vl, vk = vfn(kt)
                              mm(po[0:(65 if aug else 128), c0:512], vl, ptb_[:, c0:512], kt == 0, kt == nkt - 1, [vk] + pks, [pok])
                              if not aug:
                                  mm(pl[:, c0:512], ones_b[:, :], ptb_[:, c0:512], kt == 0, kt == nkt - 1, ['onesb'] + pks, [plk])
                          if aug:
                              V_(lambda e: e.reciprocal(out=rlt[64:65, :], in_=po[64:65, :]), [pok], ['rlt'])
                              px, pxk = next_pab()
                              mm(px[0:64, :], ones_f[64:65, 0:64], rlt[64:65, :], True, True, ['onesf', 'rlt'], [pxk])
                              V_(lambda e, px=px: e.tensor_copy(out=rbc[0:64, :], in_=px[0:64, :]), [pxk], ['rbc'])
                              out_fn(po, pok)
                          else:
                              V_(lambda e, pl=pl: e.reciprocal(out=rbc[:, :], in_=pl[:, :]), [plk], ['rbc'])
                              out_fn(po, pok)

                      for qb in range(NB + 1):
                          smp = (qb == NB)
                          if not smp:
                              dma('sync', QT[:, :, :], qt_d[:, :, qb * 512:(qb + 1) * 512].rearrange("c p t -> p c t"), ['qt_d%d' % qb], ['QT'])
                              nkt = 4 * qb + 4
                              if fox:
                                  for qt in range(4):
                                      tq = 4 * qb + qt
                                      for h in range(12):
                                          V_(lambda e, qt=qt, tq=tq, h=h: e.tensor_scalar(out=Bq[:, qt, h, 0:tq + 1], in0=negC[:, 0:tq + 1, h], scalar1=Cend[:, tq, h:h + 1], scalar2=None, op0=ALU.add), ['negC', 'Cend'], ['Bq'])
                                  for h in range(12):
                                      kp = slice((h % 2) * 64, (h % 2) * 64 + 64)

                                      def outf(po, pok, h=h):
                                          V_(lambda e: e.tensor_tensor(out=OT[0:64, h, :], in0=po[0:64, :], in1=rbc[0:64, :], op=ALU.mult), [pok, 'rbc'], ['OT'])
                                      attend(qb, KT, kp, h // 2, lambda kt, h=h: (VV[:, kt, h, :], 'VV%d' % (kt // 4)), nkt,
                                             lambda qt, kt, h=h: Bq[:, qt, h, kt:kt + 1], True, QT, h // 2, outf, True)
                              else:
                                  for hh in range(12):
                                      h, c = hh // 2, hh % 2
                                      kp = slice(c * 64, c * 64 + 64)

                                      def outf(po, pok, h=h, c=c):
                                          if c == 0:
                                              V_(lambda e: e.tensor_tensor(out=on0[:, :], in0=po[:, :], in1=rbc[:, :], op=ALU.mult), [pok, 'rbc'], ['on0'])
                                              return
                                          V_(lambda e: e.tensor_tensor(out=on1[:, :], in0=po[:, :], in1=rbc[:, :], op=ALU.mult), [pok, 'rbc'], ['on1'])
                                          V_(lambda e: e.scalar_tensor_tensor(out=on0[:, :], in0=on1[:, :], scalar=nlam[:, :], in1=on0[:, :], op0=ALU.mult, op1=ALU.add), ['on0', 'on1', 'nlam'], ['on0'])
                                          G_(lambda e: e.tensor_tensor(out=sq[:, :], in0=on0[:, :], in1=on0[:, :], op=ALU.mult), ['on0'], ['sq'])
                                          px, pxk = next_pab()
                                          mm(px[:, :], ones_b[:, :], sq[:, :], True, True, ['onesb', 'sq'], [pxk])
                                          A_(lambda e, px=px: e.activation(out=on1[:, :], in_=px[:, :], func=AF.Sqrt, scale=1.0 / 128, bias=eps_c[:, :]), [pxk, 'epsc'], ['on1'])
                                          V_(lambda e: e.reciprocal(out=on1[:, :], in_=on1[:, :]), ['on1'], ['on1'])
                                          V_(lambda e: e.scalar_tensor_tensor(out=OT[:, h, :], in0=on0[:, :], scalar=gsub_c[:, :], in1=on1[:, :], op0=ALU.mult, op1=ALU.mult), ['on0', 'on1', 'gsubc'], ['OT'])
                                      attend(qb, KT, kp, h, lambda kt, h=h: (VV[:, kt, h * 128:(h + 1) * 128], 'VV%d' % (kt // 4)), nkt,
                                             None, True, QT, h, outf, False)
                              for hm in range(4):
                                  kp = slice((hm % 2) * 64, (hm % 2) * 64 + 64)

                                  def outm(po, pok, hm=hm):
                                      V_(lambda e: e.tensor_tensor(out=OTm[0:64, hm, :], in0=po[0:64, :], in1=rbc[0:64, :], op=ALU.mult), [pok, 'rbc'], ['OTm'])
                                  attend(qb, MKT, kp, hm // 2, lambda kt, hm=hm: (MV[:, kt, hm, :], 'MV'), 2, None, False, QT, 6 + hm // 2, outm, True)
                          tiles = [(None, NS)] if smp else [(4 * qb + i, 128) for i in range(4)]
                          for ti, (t, r) in enumerate(tiles):
                              if smp:
                                  os_, om_ = OTs_self, OTs_mem; osk, omk = 'OTs_self', 'OTs_mem'; cs_ = slice(0, NS)
                                  src = hs_src; dst = hmid_d[S:S + NS, :]
                              else:
                                  os_, om_ = OT, OTm; osk, omk = 'OT', 'OTm'; cs_ = slice(ti * 128, (ti + 1) * 128)
                                  src = h_src[t * 128:(t + 1) * 128, :]; dst = hmid_d[t * 128:(t + 1) * 128, :]
                              dma(ldq(), hres[0:r, :], src, ['h1_d'], ['hres'])
                              for hf, (py, pyk) in enumerate(((pA, 'pA'), (pB, 'pB'))):
                                  for c in range(nsc):
                                      mm(py[0:r, :], os_[0:pk_, c, cs_], Wo[0:pk_, c, hf * 512:(hf + 1) * 512], c == 0, False, [osk, 'Wo'], [pyk])
                                  for c in range(4):
                                      mm(py[0:r, :], om_[0:64, c, cs_], Wom[0:64, c, hf * 512:(hf + 1) * 512], False, c == 3, [omk, 'Wom'], [pyk])
                              if True:
                                  st = None
                                  post_norm_residual(st, (pA, pB), ('pA', 'pB'), r, gp, 'gp', hres, 'hres', hout, 'hout')
                              dma('sync', dst, hout[0:r, :], ['hout0', 'hout1'], ['hmid_d'])

              stop_if('P2_%d' % L)
              sch.barrier()
              p3 = ExitStack()
              with p3:
                  Wg = sb(p3, [128, 8, 2 * DFF], BF16, "Wg"); Wd = sb(p3, [128, 22, D], BF16, "Wd")
                  load_w_bf16(Wg, 'Wg', w_gu[L], D, 2 * DFF, 8)
                  load_w_bf16(Wd, 'Wd', w_dn[L], DFF, D, 22)
                  gp = sb(p3, [128, D], F32, "gp3"); gc3 = sb(p3, [128, 8], F32, "gc3")
                  dma('sync', gp[:, :], gpost[L, 1:2, :].to_broadcast([128, D]), [], ['gp3'])
                  dma('sync', gc3[:, :], gcols[L, 1], [], ['gc3'])
                  xT = sb(p3, [128, 8, 512], BF16, "xT3"); fT = sb(p3, [128, 22, 512], BF16, "fT")
                  hm = [sb(p3, [128, D], F32, "hm%d" % i) for i in range(4)]
                  sg = sb(p3, [128, 512], F32, "sg"); hout = sb(p3, [128, D], F32, "hout3")
                  for qb in range(NB + 1):
                      smp = (qb == NB)
                      tiles = [(None, NS)] if smp else [(4 * qb + i, 128) for i in range(4)]
                      T = NS if smp else 512
                      for ti, (t, r) in enumerate(tiles):
                          src = hmid_d[S:S + NS, :] if smp else hmid_d[t * 128:(t + 1) * 128, :]
                          sch.op(ldq(), lambda e, ti=ti, r=r, src=src: e.dma_start(out=hm[ti][0:r, :], in_=src), ['hmid_d'], ['hm%d' % ti], dma=True)
                          norm_from(hm[ti], r, gc3, 'gc3', xT, 'xT3', ti * 128, 'hm%d' % ti)
                      for j in range(22):
                          for c in range(8):
                              mm(pA[:, 0:T], Wg[:, c, j * 128:(j + 1) * 128], xT[:, c, 0:T], c == 0, c == 7, ['Wg'] + xkeys('xT3'), ['pA'])
                          for c in range(8):
                              mm(pB[:, 0:T], Wg[:, c, DFF + j * 128:DFF + (j + 1) * 128], xT[:, c, 0:T], c == 0, c == 7, ['Wg'] + xkeys('xT3'), ['pB'])
                          A_(lambda e, T=T: e.activation(out=sg[:, 0:T], in_=pA[:, 0:T], func=AF.Silu), ['pA'], ['sg'])
                          V_(lambda e, j=j, T=T: e.tensor_tensor(out=fT[:, j, 0:T], in0=sg[:, 0:T], in1=pB[:, 0:T], op=ALU.mult), ['sg', 'pB'], ['fT'])
                      for ti, (t, r) in enumerate(tiles):
                          cs_ = slice(ti * 128, ti * 128 + r)
                          for hf, (py, pyk) in enumerate(((pO[0], 'pO0'), (pO[1], 'pO1'))):
                              for j in range(22):
                                  mm(py[0:r, :], fT[:, j, cs_], Wd[:, j, hf * 512:(hf + 1) * 512], j == 0, j == 21, ['fT', 'Wd'], [pyk])
                          if True:
                              st = None
                              post_norm_residual(st, (pO[0], pO[1]), ('pO0', 'pO1'), r, gp, 'gp3', hm[ti], 'hm%d' % ti, hout, 'hout3')
                          dst = hs_dst if smp else h_dst[t * 128:(t + 1) * 128, :]
                          dma('sync', dst, hout[0:r, :], ['hout30', 'hout31'], ['h1_d'])
              sch.barrier()

          except _Stop:
            break
        sch.emit()
    except AssertionError:
        if not KSTOP:
            raise
    return nc


_CACHE = {}


def _consts(S, NPG):
    c = {}
    c["c_ident"] = np.eye(128, dtype=np.float32)
    pp = np.arange(128)
    c["c_tri"] = (pp[:, None] <= pp[None, :]).astype(np.float32)
    c["c_suf"] = (pp[:, None] > pp[None, :]).astype(np.float32)
    c["c_maskT"] = (pp[None, :] >= pp[:, None]).astype(np.float32)
    c["c_iota"] = pp.astype(np.float32)[:, None]
    inv = (np.float32(10000.0) ** (-np.arange(0, 64, 2, dtype=np.float32) / np.float32(64))).astype(np.float32)
    pos = np.arange(S, dtype=np.float32)
    ang = (pos[:, None] * inv[None, :]).astype(np.float32)
    c["c_cos"] = np.ascontiguousarray(np.tile(np.cos(ang).astype(np.float32), (1, 12)))
    c["c_sin"] = np.ascontiguousarray(np.tile(np.sin(ang).astype(np.float32), (1, 12)))
    angs = (np.full((NS, 1), NPG * 128, dtype=np.float32) * inv[None, :]).astype(np.float32)
    c["c_cos_s"] = np.ascontiguousarray(np.tile(np.cos(angs).astype(np.float32), (1, 12)))
    c["c_sin_s"] = np.ascontiguousarray(np.tile(np.sin(angs).astype(np.float32), (1, 12)))
    sbias = np.full((128, 12), NEG, dtype=np.float32); sbias[0, :] = 0.0
    c["c_selfb"] = sbias
    bmf = np.zeros((12, 12, 64), np.float32)
    for h in range(12):
        bmf[h, h, :] = 1.0
    c["c_bm_fox"] = bmf.reshape(12, SW)
    bmd = np.zeros((12, 6, 128), np.float32)
    for h in range(12):
        bmd[h, h // 2, :] = 1.0
    c["c_bm_diff"] = bmd.reshape(12, SW)
    return c


def kernel(x_prompt, x_sample, cache_fox_k, cache_fox_v, cache_fox_logf, cache_diff_k, cache_diff_v,
           cache_mem_k, cache_mem_v, page_table, mem_prompt, w_in_fox, b_f_fox, w_in_diff,
           lam_q1, lam_k1, lam_q2, lam_k2, g_subln, g_pre_mix, g_post_mix, g_pre_ffn, g_post_ffn,
           g_mem, w_mem_kv, w_out, w_gate_up, w_down):
    f = lambda a: np.ascontiguousarray(np.asarray(a, dtype=np.float32))
    x_prompt = f(x_prompt); x_sample = f(x_sample)
    B, S, _ = x_prompt.shape
    DB = x_sample.shape[0]
    NPOOL = cache_fox_k.shape[1]
    page_table = np.ascontiguousarray(np.asarray(page_table, dtype=np.int32))
    NPG = page_table.shape[1]
    key = (S, NPG, NPOOL)
    if key not in _CACHE:
        _CACHE[key] = build(S, NPG, NPOOL)
    nc = _CACHE[key]
    cst = _consts(S, NPG)
    cfk = f(cache_fox_k).reshape(NPOOL * 128, SW); cfv = f(cache_fox_v).reshape(NPOOL * 128, SW)
    cfl = f(cache_fox_logf).reshape(NPOOL * 128, 12)
    cdk = f(cache_diff_k).reshape(NPOOL * 128, SW); cdv = f(cache_diff_v).reshape(NPOOL * 128, SW)
    cmk = f(cache_mem_k).reshape(2, DB, 256, 256); cmv = f(cache_mem_v).reshape(2, DB, 256, 256)
    gl = [f(g_pre_mix), f(g_pre_ffn), f(g_mem)]
    gcols = np.zeros((2, 3, 128, 8), np.float32)
    for L in range(2):
        for j in range(3):
            gcols[L, j] = gl[j][L].reshape(8, 128).T
    gpost = np.ascontiguousarray(np.stack([f(g_post_mix), f(g_post_ffn)], axis=1))
    lamv = np.ascontiguousarray(np.stack([f(lam_q1)[0], f(lam_k1)[0], f(lam_q2)[0], f(lam_k2)[0]], axis=0))
    shared = dict(cfk=cfk, cfv=cfv, cfl=cfl, cdk=cdk, cdv=cdv,
                  w_in_fox=f(w_in_fox)[0], w_in_diff=f(w_in_diff)[0], b_f=f(b_f_fox).reshape(1, 12), lamv=lamv,
                  gsub=f(g_subln)[0].reshape(128, 1), gcols=gcols, gpost=gpost,
                  w_mem=f(w_mem_kv), w_out=f(w_out), w_gu=f(w_gate_up), w_dn=f(w_down))
    shared.update(cst)
    in_maps = []
    for c in range(NCORES):
        b = c % B
        m = dict(shared)
        m["xp"] = x_prompt[b]
        m["xs"] = np.ascontiguousarray(x_sample[NS * c:NS * (c + 1), 0, :])
        m["cmk"] = np.ascontiguousarray(cmk[:, NS * c:NS * (c + 1)])
        m["cmv"] = np.ascontiguousarray(cmv[:, NS * c:NS * (c + 1)])
        m["pt"] = np.ascontiguousarray(page_table[NS * c:NS * (c + 1)])
        m["memp"] = f(mem_prompt)[b]
        in_maps.append(m)
    res = run_bass_kernel_spmd(nc, in_maps, core_ids=list(range(NCORES)))
    R = res.results
    g = lambda name, cores: np.stack([np.asarray(R[c][name], dtype=np.float32) for c in cores], axis=0)
    pc = list(range(B)); ac = list(range(NCORES))
    y_p = g("o_yp", pc)
    y_s = g("o_ys", ac).reshape(DB, 1, D)
    fk_p = g("o_fkp", pc).reshape(1, B, S, 12, 64); fv_p = g("o_fvp", pc).reshape(1, B, S, 12, 64)
    fl_p = g("o_flp", pc).reshape(1, B, S, 12)
    fk_s = g("o_fks", ac).reshape(1, DB, 1, 12, 64); fv_s = g("o_fvs", ac).reshape(1, DB, 1, 12, 64)
    fl_s = g("o_fls", ac).reshape(1, DB, 1, 12)
    dk_p = g("o_dkp", pc).reshape(1, B, S, 6, 128); dv_p = g("o_dvp", pc).reshape(1, B, S, 6, 128)
    dk_s = g("o_dks", ac).reshape(1, DB, 1, 6, 128); dv_s = g("o_dvs", ac).reshape(1, DB, 1, 6, 128)
    mk = np.ascontiguousarray(g("o_mk", pc).transpose(1, 0, 2, 3)).reshape(2, B, 256, 4, 64)
    mv = np.ascontiguousarray(g("o_mv", pc).transpose(1, 0, 2, 3)).reshape(2, B, 256, 4, 64)
    return (y_p, y_s, fk_p, fv_p, fl_p, fk_s, fv_s, fl_s, dk_p, dv_p, dk_s, dv_s, mk, mv)
```
